# Optimizing a Trainium2 kernel written in Bass

```python
import jax, jax.numpy as jnp
from jax import lax
import numpy as np

D_MODEL = 1024
BATCH = 32
SEQ = 2048
DEPTH = 1
DEC_BATCH = 128
DEC_SEQ = 4
PAST_LEN = 8192
PAGE_SIZE = 128

MIX_WIDTH = D_MODEL
HEAD_DIM = 64
N_ATT_HEADS = (MIX_WIDTH // 2) // HEAD_DIM
ATT_WIDTH = N_ATT_HEADS * HEAD_DIM
N_ML_HEADS = 4
ML_WIDTH = MIX_WIDTH - ATT_WIDTH
ML_DK = ML_WIDTH // N_ML_HEADS
ML_DV = ML_DK
DILATED_PATTERNS = ((128, 1), (512, 4), (2048, 16))
MAX_WINDOW = 2048
ROPE_THETA = 10000.0
D_FF = 256 * ((8 * D_MODEL // 3 + 255) // 256)
ML_CHUNK = 128
NORM_EPS = 1e-6
IN_WIDTH = 3 * ATT_WIDTH + 4 * ML_WIDTH + 2 * N_ML_HEADS

kernel_name = "hymba_dilated_swa_mlstm_macaron_step"


def _rmsnorm(x, g):
    xf = x.astype(jnp.float32)
    y = xf * lax.rsqrt(jnp.mean(xf * xf, axis=-1, keepdims=True) + NORM_EPS)
    return (y * g.astype(jnp.float32)).astype(x.dtype)


def _rope(x, pos):
    half = HEAD_DIM // 2
    inv = ROPE_THETA ** (-jnp.arange(half, dtype=jnp.float32) / half)
    ang = pos.astype(jnp.float32)[:, None] * inv[None, :]
    cos = jnp.cos(ang)[None, :, None, :]
    sin = jnp.sin(ang)[None, :, None, :]
    xf = x.astype(jnp.float32)
    x1, x2 = xf[..., :half], xf[..., half:]
    return jnp.concatenate([x1 * cos - x2 * sin, x1 * sin + x2 * cos], axis=-1).astype(x.dtype)


def _swiglu(x, w_gate, w_up, w_down):
    return (jax.nn.silu(x @ w_gate) * (x @ w_up)) @ w_down


def _project(h, pos, w_in, q_gain, k_gain):
    B, T, _ = h.shape
    z = h @ w_in
    widths = (ATT_WIDTH,) * 3 + (ML_WIDTH,) * 4 + (N_ML_HEADS,) * 2
    idx = [sum(widths[:i + 1]) for i in range(len(widths) - 1)]
    aq, ak, av, mq, mk, mv, mo, mi, mf = jnp.split(z, idx, axis=-1)
    aq = _rope(_rmsnorm(aq.reshape(B, T, N_ATT_HEADS, HEAD_DIM), q_gain), pos)
    ak = _rope(_rmsnorm(ak.reshape(B, T, N_ATT_HEADS, HEAD_DIM), k_gain), pos)
    av = av.reshape(B, T, N_ATT_HEADS, HEAD_DIM)
    mq = mq.reshape(B, T, N_ML_HEADS, ML_DK)
    mk = mk.reshape(B, T, N_ML_HEADS, ML_DK) * (ML_DK ** -0.5)
    mv = mv.reshape(B, T, N_ML_HEADS, ML_DV)
    return aq, ak, av, mq, mk, mv, mo, mi, mf


def _merge_by_denominator(outs, lses):
    wts = jax.nn.softmax(jnp.stack(lses, 0), axis=0)
    return jnp.einsum('pbth,pbthd->bthd', wts, jnp.stack(outs, 0))


def _dilated_band_prompt(q, k, v, dil, n_back):
    B, S, H, Dh = q.shape
    L = S // dil
    nb = -(-L // n_back)
    Lp = nb * n_back

    def split(t):
        t = t.reshape(B, L, dil, H, Dh).transpose(0, 2, 1, 3, 4)
        t = jnp.pad(t, ((0, 0), (0, 0), (0, Lp - L), (0, 0), (0, 0)))
        return t.reshape(B, dil, nb, n_back, H, Dh)

    def with_prev(t):
        prev = jnp.pad(t[:, :, :-1], ((0, 0), (0, 0), (1, 0), (0, 0), (0, 0), (0, 0)))
        return jnp.concatenate([prev, t], axis=3)

    qb = split(q)
    kb = with_prev(split(k))
    vb = with_prev(split(v))
    s = jnp.einsum('brnqhd,brnkhd->brnhqk', qb, kb).astype(jnp.float32) * (HEAD_DIM ** -0.5)
    i = jnp.arange(n_back)[:, None]
    j = jnp.arange(2 * n_back)[None, :]
    band = (j >= i) & (j <= i + n_back)
    blk = jnp.arange(nb)[:, None, None]
    valid = band[None] & ((blk > 0) | (j[None] >= n_back))
    s = jnp.where(valid[None, None, :, None], s, -jnp.inf)
    m = jnp.max(s, axis=-1, keepdims=True)
    p = jnp.exp(s - m)
    den = jnp.sum(p, axis=-1, keepdims=True)
    o = jnp.einsum('brnhqk,brnkhd->brnqhd', p / den, vb.astype(jnp.float32))
    lse = (m + jnp.log(den))[..., 0]
    o = o.reshape(B, dil, Lp, H, Dh)[:, :, :L].transpose(0, 2, 1, 3, 4).reshape(B, S, H, Dh)
    lse = lse.transpose(0, 1, 2, 4, 3).reshape(B, dil, Lp, H)[:, :, :L]
    lse = lse.transpose(0, 2, 1, 3).reshape(B, S, H)
    return o, lse


def _dilated_attn_prompt(q, k, v):
    outs, lses = [], []
    for window, dil in DILATED_PATTERNS:
        o, l = _dilated_band_prompt(q, k, v, dil, window // dil)
        outs.append(o)
        lses.append(l)
    return _merge_by_denominator(outs, lses)


def _dilated_attn_sample(q, k_new, v_new, cache_k, cache_v):
    T = q.shape[1]
    Lb = cache_k.shape[1]
    kc = jnp.concatenate([cache_k, k_new.astype(cache_k.dtype)], axis=1)
    vc = jnp.concatenate([cache_v, v_new.astype(cache_v.dtype)], axis=1)
    t = jnp.arange(T)
    outs, lses = [], []
    for window, dil in DILATED_PATTERNS:
        n_back = window // dil
        j = jnp.arange(n_back + 1)
        rows = Lb + t[:, None] - dil * j[None, :]
        valid = rows >= 0
        rows = jnp.maximum(rows, 0)
        kg = kc[:, rows]
        vg = vc[:, rows]
        s = jnp.einsum('bthd,btjhd->bthj', q, kg).astype(jnp.float32) * (HEAD_DIM ** -0.5)
        s = jnp.where(valid[None, :, None, :], s, -jnp.inf)
        m = jnp.max(s, axis=-1, keepdims=True)
        p = jnp.exp(s - m)
        den = jnp.sum(p, axis=-1, keepdims=True)
        outs.append(jnp.einsum('bthj,btjhd->bthd', p / den, vg.astype(jnp.float32)))
        lses.append((m + jnp.log(den))[..., 0])
    return _merge_by_denominator(outs, lses)


def _mlstm_chunk(q, k, v, ig, lf, C, n, m):
    L = q.shape[1]
    b = jnp.cumsum(lf, axis=1).transpose(0, 2, 1)
    igt = ig.transpose(0, 2, 1)
    causal = jnp.tril(jnp.ones((L, L), dtype=bool))
    d = jnp.where(causal, b[..., :, None] - b[..., None, :] + igt[..., None, :], -jnp.inf)
    inter = b + m[..., None]
    m_t = jnp.maximum(jnp.max(d, axis=-1), inter)
    a = jnp.exp(d - m_t[..., None]) * jnp.einsum('bthk,bshk->bhts', q, k)
    w_inter = jnp.exp(inter - m_t)
    num = jnp.einsum('bhts,bshv->bthv', a, v) + jnp.einsum('bht,bthk,bhkv->bthv', w_inter, q, C)
    den = jnp.sum(a, axis=-1) + w_inter * jnp.einsum('bthk,bhk->bht', q, n)
    den = jnp.maximum(jnp.abs(den), jnp.exp(-m_t))
    h = num / den.transpose(0, 2, 1)[..., None]
    b_last = b[..., -1]
    g = b_last[..., None] - b + igt
    m_new = jnp.maximum(b_last + m, jnp.max(g, axis=-1))
    w_s = jnp.exp(g - m_new[..., None])
    w_c = jnp.exp(b_last + m - m_new)
    C_new = w_c[..., None, None] * C + jnp.einsum('bhs,bshk,bshv->bhkv', w_s, k, v)
    n_new = w_c[..., None] * n + jnp.einsum('bhs,bshk->bhk', w_s, k)
    return h, C_new, n_new, m_new


def _mlstm_prompt(q, k, v, ig, lf):
    B, S, H, Dk = q.shape
    ch = min(ML_CHUNK, S)
    nc = S // ch

    def to_chunks(a):
        return a.reshape((B, nc, ch) + a.shape[2:]).swapaxes(0, 1)

    C0 = jnp.zeros((B, H, Dk, ML_DV), jnp.float32)
    n0 = jnp.zeros((B, H, Dk), jnp.float32)
    m0 = jnp.zeros((B, H), jnp.float32)

    def step(carry, xs):
        C, n, m = carry
        h, C, n, m = _mlstm_chunk(*xs, C, n, m)
        return (C, n, m), h

    (C, n, m), hs = lax.scan(step, (C0, n0, m0), tuple(to_chunks(a) for a in (q, k, v, ig, lf)))
    return hs.swapaxes(0, 1).reshape(B, S, H, ML_DV), C, n, m


def _layer(x, pos, p, attn_cache, ml_state):
    B, T, _ = x.shape
    f32 = jnp.float32
    x = x + 0.5 * _swiglu(_rmsnorm(x, p['ffn1_norm']), p['ffn1_w_gate'], p['ffn1_w_up'], p['ffn1_w_down'])
    h = _rmsnorm(x, p['mix_norm'])
    aq, ak, av, mq, mk, mv, mo, mi, mf = _project(h, pos, p['w_in'], p['q_norm'], p['k_norm'])
    ig = mi.astype(f32) + p['b_igate'].astype(f32)
    lf = jax.nn.log_sigmoid(mf.astype(f32) + p['b_fgate'].astype(f32))
    mq, mk, mv = mq.astype(f32), mk.astype(f32), mv.astype(f32)
    if attn_cache is None:
        att = _dilated_attn_prompt(aq, ak, av)
        n_keep = min(MAX_WINDOW, T)
        att_state = (ak[:, T - n_keep:], av[:, T - n_keep:])
        hm, C, n, m = _mlstm_prompt(mq, mk, mv, ig, lf)
    else:
        att = _dilated_attn_sample(aq, ak, av, attn_cache[0], attn_cache[1])
        att_state = (ak, av)
        C0, n0, m0 = ml_state
        hm, C, n, m = _mlstm_chunk(mq, mk, mv, ig, lf, C0.astype(f32), n0.astype(f32), m0.astype(f32))
    hm = _rmsnorm(hm, p['ml_out_norm']) * jax.nn.sigmoid(mo.astype(f32)).reshape(B, T, N_ML_HEADS, ML_DV)
    mix = jnp.concatenate([att.reshape(B, T, ATT_WIDTH).astype(x.dtype),
                           hm.reshape(B, T, ML_WIDTH).astype(x.dtype)], axis=-1)
    x = x + mix @ p['w_out']
    x = x + 0.5 * _swiglu(_rmsnorm(x, p['ffn2_norm']), p['ffn2_w_gate'], p['ffn2_w_up'], p['ffn2_w_down'])
    return x, att_state, (C, n, m)


def setup_inputs(seed: int = 0) -> dict:
    key = jax.random.key(seed)
    ks = jax.random.split(key, 24)
    f32 = jnp.float32

    def nrm(k, shape, scale):
        return jax.random.normal(k, shape, f32) * scale

    def gain(k, shape):
        return 1.0 + 0.02 * jax.random.normal(k, shape, f32)

    win_buf = min(MAX_WINDOW, PAST_LEN)
    return {
        'x_prompt': nrm(ks[0], (BATCH, SEQ, D_MODEL), 1.0),
        'x_sample': nrm(ks[1], (DEC_BATCH, DEC_SEQ, D_MODEL), 1.0),
        'cache_k_win': nrm(ks[2], (DEPTH, DEC_BATCH, win_buf, N_ATT_HEADS, HEAD_DIM), 1.0),
        'cache_v_win': nrm(ks[3], (DEPTH, DEC_BATCH, win_buf, N_ATT_HEADS, HEAD_DIM), 1.0),
        'state_C': nrm(ks[4], (DEPTH, DEC_BATCH, N_ML_HEADS, ML_DK, ML_DV), 0.3),
        'state_n': nrm(ks[5], (DEPTH, DEC_BATCH, N_ML_HEADS, ML_DK), 0.3),
        'state_m': jax.random.uniform(ks[6], (DEPTH, DEC_BATCH, N_ML_HEADS), f32, 0.0, 4.0),
        'ffn1_norm': gain(ks[7], (DEPTH, D_MODEL)),
        'ffn1_w_gate': nrm(ks[8], (DEPTH, D_MODEL, D_FF), D_MODEL ** -0.5),
        'ffn1_w_up': nrm(ks[9], (DEPTH, D_MODEL, D_FF), D_MODEL ** -0.5),
        'ffn1_w_down': nrm(ks[10], (DEPTH, D_FF, D_MODEL), D_FF ** -0.5),
        'mix_norm': gain(ks[11], (DEPTH, D_MODEL)),
        'w_in': nrm(ks[12], (DEPTH, D_MODEL, IN_WIDTH), D_MODEL ** -0.5),
        'q_norm': gain(ks[13], (DEPTH, HEAD_DIM)),
        'k_norm': gain(ks[14], (DEPTH, HEAD_DIM)),
        'b_igate': nrm(ks[15], (DEPTH, N_ML_HEADS), 0.1),
        'b_fgate': jnp.linspace(3.0, 6.0, N_ML_HEADS, dtype=f32)[None, :] + nrm(ks[16], (DEPTH, N_ML_HEADS), 0.1),
        'ml_out_norm': gain(ks[17], (DEPTH, N_ML_HEADS, ML_DV)),
        'w_out': nrm(ks[18], (DEPTH, MIX_WIDTH, D_MODEL), MIX_WIDTH ** -0.5),
        'ffn2_norm': gain(ks[19], (DEPTH, D_MODEL)),
        'ffn2_w_gate': nrm(ks[20], (DEPTH, D_MODEL, D_FF), D_MODEL ** -0.5),
        'ffn2_w_up': nrm(ks[21], (DEPTH, D_MODEL, D_FF), D_MODEL ** -0.5),
        'ffn2_w_down': nrm(ks[22], (DEPTH, D_FF, D_MODEL), D_FF ** -0.5),
    }


def reference(x_prompt, x_sample, cache_k_win, cache_v_win, state_C, state_n, state_m,
              ffn1_norm, ffn1_w_gate, ffn1_w_up, ffn1_w_down, mix_norm, w_in, q_norm, k_norm,
              b_igate, b_fgate, ml_out_norm, w_out, ffn2_norm, ffn2_w_gate, ffn2_w_up, ffn2_w_down):
    pos_p = jnp.arange(x_prompt.shape[1], dtype=jnp.int32)
    pos_s = PAST_LEN + jnp.arange(x_sample.shape[1], dtype=jnp.int32)
    yp, ys = x_prompt, x_sample
    kp_l, vp_l, ks_l, vs_l = [], [], [], []
    Cp_l, np_l, mp_l, Cs_l, ns_l, ms_l = [], [], [], [], [], []
    for l in range(DEPTH):
        p = {
            'ffn1_norm': ffn1_norm[l], 'ffn1_w_gate': ffn1_w_gate[l], 'ffn1_w_up': ffn1_w_up[l],
            'ffn1_w_down': ffn1_w_down[l], 'mix_norm': mix_norm[l], 'w_in': w_in[l],
            'q_norm': q_norm[l], 'k_norm': k_norm[l], 'b_igate': b_igate[l], 'b_fgate': b_fgate[l],
            'ml_out_norm': ml_out_norm[l], 'w_out': w_out[l], 'ffn2_norm': ffn2_norm[l],
            'ffn2_w_gate': ffn2_w_gate[l], 'ffn2_w_up': ffn2_w_up[l], 'ffn2_w_down': ffn2_w_down[l],
        }
        yp, (kp, vp), (Cp, np_, mp) = _layer(yp, pos_p, p, None, None)
        ys, (ks, vs), (Cs, ns, ms) = _layer(ys, pos_s, p, (cache_k_win[l], cache_v_win[l]),
                                            (state_C[l], state_n[l], state_m[l]))
        kp_l.append(kp); vp_l.append(vp); ks_l.append(ks); vs_l.append(vs)
        Cp_l.append(Cp); np_l.append(np_); mp_l.append(mp)
        Cs_l.append(Cs); ns_l.append(ns); ms_l.append(ms)
    return (yp, ys, jnp.stack(kp_l), jnp.stack(vp_l), jnp.stack(ks_l), jnp.stack(vs_l),
            jnp.stack(Cp_l), jnp.stack(np_l), jnp.stack(mp_l),
            jnp.stack(Cs_l), jnp.stack(ns_l), jnp.stack(ms_l))
```

```python
import contextlib
import numpy as np
import ml_dtypes
import concourse.bass as bass
import concourse.mybir as mybir
from concourse.bass_utils import run_bass_kernel_spmd

F32 = mybir.dt.float32
BF16 = mybir.dt.bfloat16
ALU = mybir.AluOpType
AF = mybir.ActivationFunctionType
AX = mybir.AxisListType

ENGS = ['pe', 'act', 'dve', 'pool', 'sp']
NCORES = 8
D = 1024
DFF = 2816
NJ = 22
S = 2048
NSEQ = 4
NSB = 16
NR = 4
NSLOT = 88
EPS = 1e-6
WIN_BUF = 2048
PAST = 8192
BIG = 30000.0


class Sched:
    def __init__(self, nc):
        self.nc = nc
        self.streams = {e: [] for e in ENGS}
        self.cnt = {}
        self.seen = {e: {} for e in ENGS}
        self.lastw = {}
        self.readers = {}
        self.semkeys = []
        for e in ENGS:
            self._semkey(e)

    def _semkey(self, k):
        if k not in self.cnt:
            self.cnt[k] = 0
            self.semkeys.append(k)

    def _need(self, eng, tok, waits, self_ok):
        if tok is None:
            return
        k, v = tok
        if k == eng and not self_ok:
            return
        if self.seen[eng].get(k, 0) >= v:
            return
        self.seen[eng][k] = v
        waits.append(tok)

    def _deps(self, eng, reads, writes):
        waits = []
        raw_self = eng in ('act', 'dve', 'pool')
        waw_self = eng == 'pool'
        for b in reads:
            self._need(eng, self.lastw.get(b), waits, raw_self)
        for b in writes:
            self._need(eng, self.lastw.get(b), waits, waw_self)
            for r in self.readers.get(b, ()):
                self._need(eng, r, waits, waw_self)
        return waits

    def _commit(self, tok, reads, writes):
        for b in reads:
            self.readers.setdefault(b, []).append(tok)
        for b in writes:
            self.lastw[b] = tok
            self.readers[b] = []

    def op(self, eng, fn, reads=(), writes=()):
        waits = self._deps(eng, reads, writes)
        self.cnt[eng] += 1
        tok = (eng, self.cnt[eng])
        self.streams[eng].append((waits, fn, eng, 1))
        self._commit(tok, reads, writes)
        return tok

    def dma(self, q, semkey, fn, reads=(), writes=()):
        self._semkey(semkey)
        waits = self._deps(q, reads, writes)
        self.cnt[semkey] += 16
        tok = (semkey, self.cnt[semkey])
        self.streams[q].append((waits, fn, semkey, 16))
        self._commit(tok, reads, writes)
        return tok

    def wait_all(self, eng, toks):
        waits = []
        for t in toks:
            self._need(eng, t, waits, True)
        self.streams[eng].append((waits, None, None, 0))

    def all_tokens(self):
        return [(k, self.cnt[k]) for k in self.semkeys if self.cnt[k] > 0]

    def barrier(self):
        toks = self.all_tokens()
        for e in ENGS:
            self.wait_all(e, toks)

    def build(self):
        nc = self.nc
        with contextlib.ExitStack() as st:
            sems = {}
            for i, k in enumerate(self.semkeys):
                if self.cnt[k] > 0:
                    sems[k] = st.enter_context(nc.semaphore("s%d" % i))
            block = st.enter_context(nc.Block())
            engobj = {'pe': 'tensor', 'act': 'scalar', 'dve': 'vector', 'pool': 'gpsimd', 'sp': 'sync'}

            def replay(name):
                def body(e):
                    for waits, fn, sk, inc in self.streams[name]:
                        for (k, v) in waits:
                            e.wait_ge(sems[k], v)
                        if fn is not None:
                            fn(e).then_inc(sems[sk], inc)
                return body
            for name in ENGS:
                if self.streams[name]:
                    getattr(block, engobj[name])(replay(name))


def bcast(ap, shape):
    return ap.to_broadcast(list(shape))


CF = {}
_off = 0
for _n, _w in [('ident', 128), ('posmask', 128), ('onesrow', 128), ('sel', 512), ('onescol', 64), ('posmask_s', 64), ('negones', 128)]:
    CF[_n] = (_off, _w)
    _off += _w
NCF = _off
CB = {}
_off = 0
for _n, _w in [('mprev', 128), ('mcur', 128), ('m16', 128), ('mA', 4), ('mnew64', 64), ('identb', 128), ('onesb', 1), ('rmask', 16), ('bmask', 1024), ('posmb', 128)]:
    CB[_n] = (_off, _w)
    _off += _w
NCB = _off


def make_consts():
    cf = np.zeros((128, NCF), np.float32)
    o, w = CF['ident']
    cf[:, o:o + w] = np.eye(128, dtype=np.float32)
    o, w = CF['posmask']
    s = np.arange(128)[:, None]
    t = np.arange(128)[None, :]
    cf[:, o:o + w] = np.where(s > t, BIG, 0.0)
    o, w = CF['onesrow']
    cf[:, o:o + w] = 1.0
    o, w = CF['sel']
    sel = np.zeros((128, 4, 128), np.float32)
    for h in range(4):
        sel[h, h, :] = 1.0
    cf[:, o:o + w] = sel.reshape(128, 512)
    o, w = CF['onescol']
    cf[:, o:o + w] = 1.0
    cb = np.zeros((128, NCB), np.float32)
    j = np.arange(128)[:, None]
    i = np.arange(128)[None, :]
    o, w = CB['mprev']
    cb[:, o:o + w] = (j >= i)
    o, w = CB['mcur']
    cb[:, o:o + w] = (j <= i)
    o, w = CB['m16']
    m16 = np.zeros((128, 4, 32), np.float32)
    for n in range(4):
        ip = np.arange(128)[:, None]
        il = np.arange(32)[None, :]
        m16[:, n, :] = (ip <= 32 * n + il) & (ip < 32 * (n + 1))
    cb[:, o:o + w] = m16.reshape(128, 128)
    o, w = CB['mA']
    cb[:, o:o + w] = (np.arange(128)[:, None] >= np.arange(4)[None, :])
    o, w = CB['mnew64']
    mn = np.zeros((128, 64), np.float32)
    for r in range(64):
        for c in range(64):
            if r // 4 == c // 4:
                tp, tq = r % 4, c % 4
                mn[r, c] = 3.0 if tp == tq else (1.0 if tp < tq else 0.0)
    cb[:, o:o + w] = mn
    o, w = CB['identb']
    cb[:, o:o + w] = np.eye(128)
    o, w = CB['onesb']
    cb[:, o:o + w] = 1.0
    o, w = CB['rmask']
    cb[:, o:o + w] = (np.arange(128)[:, None] // 4 == np.arange(16)[None, :])
    o, w = CB['bmask']
    bm = (np.arange(64)[None, :] // 4 == np.arange(16)[:, None]).astype(np.float32)
    cb[:, o:o + w] = np.broadcast_to(bm.reshape(1, 1024), (128, 1024))
    o, w = CF['negones']
    cf[:, o:o + w] = -1.0
    o, w = CB['posmb']
    cb[:, o:o + w] = np.where(np.arange(128)[:, None] > np.arange(128)[None, :], BIG, 0.0)
    o, w = CF['posmask_s']
    ss_ = np.arange(64)[:, None]
    tt_ = np.arange(64)[None, :]
    pm = np.where((ss_ // 4 == tt_ // 4) & (ss_ <= tt_), 0.0, BIG)
    cf[0:64, o:o + w] = pm
    return cf, cb.astype(ml_dtypes.bfloat16)


def rope_tables(pos):
    half = 32
    inv = (np.float32(10000.0) ** (-np.arange(half, dtype=np.float32) / np.float32(half))).astype(np.float32)
    ang = pos.astype(np.float32)[:, None] * inv[None, :]
    return np.cos(ang).astype(np.float32), np.sin(ang).astype(np.float32)


def pack_weights(inp):
    wall = np.zeros((NSLOT, 128, 2048), np.float32)

    def ffn(base, wg, wu, wd):
        g = wg.reshape(8, 128, NJ, 128).transpose(2, 1, 0, 3).reshape(NJ, 128, 1024)
        u = wu.reshape(8, 128, NJ, 128).transpose(2, 1, 0, 3).reshape(NJ, 128, 1024)
        wall[base:base + NJ, :, 0:1024] = g
        wall[base:base + NJ, :, 1024:2048] = u
        d = wd.reshape(11, 2, 128, 1024).transpose(0, 2, 1, 3).reshape(11, 128, 2048)
        wall[base + NJ:base + NJ + 11] = d

    ffn(0, inp['ffn1_w_gate'][0], inp['ffn1_w_up'][0], inp['ffn1_w_down'][0])
    ffn(55, inp['ffn2_w_gate'][0], inp['ffn2_w_up'][0], inp['ffn2_w_down'][0])
    w_in = inp['w_in'][0]
    for gi, c0 in enumerate([0, 512, 1024, 2048, 2560, 3072]):
        w = w_in[:, c0:c0 + 512].reshape(2, 4, 128, 512).transpose(0, 2, 1, 3).reshape(2, 128, 2048)
        wall[33 + 2 * gi:35 + 2 * gi] = w
    for i in range(4):
        for u in range(2):
            head = (i % 2) * 2 + u
            c0 = (1536 if i < 2 else 2048) + head * 128
            w = w_in[:, c0:c0 + 128].reshape(8, 128, 128).transpose(1, 0, 2).reshape(128, 1024)
            wall[45 + i, :, u * 1024:(u + 1) * 1024] = w
    w_out = inp['w_out'][0]
    for i in range(4):
        for a in range(2):
            h = 2 * i + a
            wall[49 + i, 0:64, a * 1024:(a + 1) * 1024] = w_out[h * 64:(h + 1) * 64, :]
    for i in range(2):
        for a in range(2):
            h = 2 * i + a
            wall[53 + i, :, a * 1024:(a + 1) * 1024] = w_out[512 + h * 128:512 + (h + 1) * 128, :]
    return wall


PB = {}
_off = 0
for _n, _w in [('g1', 8), ('gm', 8), ('g2', 8), ('qg', 64), ('kg', 64), ('mlg', 512), ('bi', 1), ('nbf', 1), ('wgate', 64)]:
    PB[_n] = (_off, _w)
    _off += _w
NPB = _off


def pack_params(inp):
    pb = np.zeros((128, NPB), np.float32)

    def put(name, arr):
        o, w = PB[name]
        pb[:, o:o + w] = arr
    put('g1', inp['ffn1_norm'][0].reshape(8, 128).T)
    put('gm', inp['mix_norm'][0].reshape(8, 128).T)
    put('g2', inp['ffn2_norm'][0].reshape(8, 128).T)
    put('qg', np.broadcast_to(inp['q_norm'][0][None, :], (128, 64)))
    put('kg', np.broadcast_to(inp['k_norm'][0][None, :], (128, 64)))
    put('mlg', np.broadcast_to(inp['ml_out_norm'][0].reshape(1, 512), (128, 512)))
    bi = np.zeros((128, 1), np.float32)
    bi[0:4, 0] = inp['b_igate'][0]
    put('bi', bi)
    nbf = np.zeros((128, 1), np.float32)
    nbf[0:4, 0] = inp['b_fgate'][0]
    put('nbf', nbf)
    wg = inp['w_in'][0][:, 3584:3592].reshape(8, 128, 8).transpose(1, 0, 2).reshape(128, 64)
    put('wgate', wg)
    return pb


class MK:
    def __init__(self, n_seq=NSEQ, do_sample=True, dbg=None, n_spans=4, sample_parts=('pre', 'attn', 'ml')):
        self.sample_parts = sample_parts
        self.n_seq = n_seq
        self.n_spans = n_spans
        self.do_sample = do_sample
        self.dbg = dbg or {}
        self.nc = bass.Bass("TRN2", target_bir_lowering=False)
        self.s = Sched(self.nc)
        self.wuse = 0
        self.wloaded = 0
        self.total_loads = NSLOT * (n_seq * n_spans + (1 if do_sample else 0))

    def mark(self, name):
        if not hasattr(self, 'marks'):
            self.marks = []
        self.marks.append((name, dict(self.s.cnt)))

    def dram(self, name, shape, dt, kind):
        return self.nc.dram_tensor(name, list(shape), dt, kind=kind).ap()

    def declare(self):
        I, O = "ExternalInput", "ExternalOutput"
        d = self.dram
        ns = self.n_seq
        self.xp = d("xp", [ns * S, D], F32, I)
        self.wall = d("wall", [NSLOT * 128, 2048], F32, I)
        self.pb_d = d("pb", [128, NPB], F32, I)
        self.cf_d = d("cf", [128, NCF], F32, I)
        self.cb_d = d("cb", [128, NCB], BF16, I)
        self.cosp = d("cosp", [S, 32], F32, I)
        self.sinp = d("sinp", [S, 32], F32, I)
        self.yp = d("yp", [ns * S, D], F32, O)
        self.kp = d("kp", [ns * S, 512], F32, O)
        self.vp = d("vp", [ns * S, 512], F32, O)
        self.Cp = d("Cp", [ns * 4 * 128, 128], F32, O)
        self.np_ = d("np", [ns * 4, 128], F32, O)
        self.mp = d("mp", [ns, 4], F32, O)
        self.ws = d("ws", [NSLOT * 128, 2048], BF16, "Internal")
        self.vscr = d("vscr", [512, 520], BF16, "Internal")
        if self.do_sample:
            self.xs = d("xs", [64, D], F32, I)
            self.ck = d("ck", [NSB * WIN_BUF, 512], F32, I)
            self.cv = d("cv", [NSB * WIN_BUF, 512], F32, I)
            self.sC = d("sC", [NSB * 4 * 128, 128], F32, I)
            self.sn = d("sn", [NSB * 4, 128], F32, I)
            self.sm = d("sm", [NSB, 4], F32, I)
            self.coss = d("coss", [64, 32], F32, I)
            self.sins = d("sins", [64, 32], F32, I)
            self.ys = d("ys", [64, D], F32, O)
            self.ksn = d("ksn", [64, 512], F32, O)
            self.vsn = d("vsn", [64, 512], F32, O)
            self.Cs = d("Cs", [NSB * 4 * 128, 128], F32, O)
            self.nsn = d("nsn", [NSB * 4, 128], F32, O)
            self.msn = d("msn", [NSB, 4], F32, O)
        self.dbg_out = {}
        for name, shape in self.dbg.items():
            self.dbg_out[name] = d("dbg_" + name, shape, F32, O)

    def emit_load(self, g):
        k = g % NSLOT
        r = g % NR
        src = self.ws[k * 128:(k + 1) * 128, :]
        dst = self.ring[r]
        self.s.dma('sp', 'ring%d' % r, lambda e: e.dma_start(out=dst[:], in_=src),
                   reads=[('ws', k // 11)], writes=[('ring', r)])

    def wslot(self):
        g = self.wuse
        self.wuse += 1
        while self.wloaded < min(g + NR, self.total_loads):
            self.emit_load(self.wloaded)
            self.wloaded += 1
        r = g % NR
        return self.ring[r], ('ring', r)

    def cf(self, name):
        o, w = CF[name]
        return self.cft[:, o:o + w]

    def cbm(self, name):
        o, w = CB[name]
        return self.cbt[:, o:o + w]

    def pbv(self, name):
        o, w = PB[name]
        return self.pbt[:, o:o + w]

    def build(self):
        nc = self.nc
        self.declare()
        with contextlib.ExitStack() as st:
            T = lambda name, shape, dt: st.enter_context(nc.sbuf_tensor(name, list(shape), dt))
            self.banks = [st.enter_context(nc.psum_tensor("bank%d" % i, [128, 512], F32)) for i in range(8)]
            self.bk = [('bank', i) for i in range(8)]
            self.cft = T("cft", [128, NCF], F32)
            self.cbt = T("cbt", [128, NCB], BF16)
            self.pbt = T("pbt", [128, NPB], F32)
            self.wgb = T("wgb", [128, 64], BF16)
            self.epsc = T("epsc", [128, 1], F32)
            self.ring = [T("ring%d" % i, [128, 2048], BF16) for i in range(NR)]
            self.x1 = T("x1", [128, 4, D], F32)
            self.xs_ = [T("xs%d" % i, [128, D], F32) for i in range(2)]
            self.hT = T("hT", [128, 8, 512], BF16)
            self.arena = T("arena", [128, NJ * 512], BF16)
            self.sg = [T("sg%d" % i, [128, 512], BF16) for i in range(2)]
            self.sm_ = T("small", [128, 64], F32)
            self.ssh_ = [T("ssh%d" % i, [128, 16], F32) for i in range(4)]
            self.cs = T("cs", [128, 2, 4, 32], F32)
            self.csg = T("csg", [128, 2, 4, 4, 32], F32)
            self.attT = T("attT", [65, 8, 512], BF16)
            self.hmT = T("hmT", [128, 4, 512], BF16)
            self.PT = T("PT", [128, 512], BF16)
            self.accw = [T("accw%d" % i, [65, 512], F32) for i in range(2)]
            self.rows = T("rows", [4, 2, 512], F32)
            self.mrow = T("mrow", [4, 4], F32)
            self.Caug = T("Caug", [128, 4, 129], F32)
            self.Cbf = T("Cbf", [128, 4, 129], BF16)
            A = self.arena
            self.actT = A[:, :].rearrange("p (j n) -> p j n", n=512)
            self.QT = A[:, 0:2048].rearrange("p (a n) -> p a n", n=512)
            self.mqT = A[:, 2048:4096].rearrange("p (a n) -> p a n", n=512)
            self.mkT = A[:, 4096:6144].rearrange("p (a n) -> p a n", n=512)
            self.ktm = A[:, 6144:8192].rearrange("p (t n) -> p t n", n=512)
            self.mva = A[:, 8192:8192 + 2064].rearrange("p (t h c) -> p t h c", t=4, h=4)

            s = self.s
            s.dma('sp', 'cst0', lambda e: e.dma_start(out=self.cft[:], in_=self.cf_d), writes=['cft'])
            s.dma('sp', 'cst1', lambda e: e.dma_start(out=self.cbt[:], in_=self.cb_d), writes=['cbt'])
            s.dma('sp', 'cst2', lambda e: e.dma_start(out=self.pbt[:], in_=self.pb_d), writes=['pbt'])
            for gidx in range(8):
                src = self.wall[gidx * 11 * 128:(gidx + 1) * 11 * 128, :]
                dst = self.ws[gidx * 11 * 128:(gidx + 1) * 11 * 128, :]
                s.dma('pool', 'cv%d' % gidx, lambda e, src=src, dst=dst: e.dma_start(out=dst, in_=src), writes=[('ws', gidx)])
            s.op('pool', lambda e: e.memset(self.epsc[:], EPS), writes=['epsc'])
            s.op('dve', lambda e: e.tensor_copy(self.wgb[:], self.pbv('wgate')), reads=['pbt'], writes=['wgb'])
            self.total_loads = NSLOT * self.n_seq * self.n_spans
            with contextlib.ExitStack() as st3:
                T3 = lambda name, shape, dt: st3.enter_context(nc.sbuf_tensor(name, list(shape), dt))
                self.KT = T3("KT", [128, 4, S], BF16)
                self.Vnat = T3("Vnat", [128, 8, 520], BF16)
                self.V4 = T3("V4", [128, 8, 520], BF16)
                self.V16 = T3("V16", [128, 16, 520], BF16)
                self.PTs = [self.PT] + [T3("PTx%d" % i, [128, 512], BF16) for i in range(4)]
                self.wkt = T3("wkbig", [128, 12, 512], F32)
                self.wk = [self.wkt[:, i, :] for i in range(12)]
                self.xpre = None
                self.crow2 = T3("crow2", [4, 2, 6, 128], F32)
                self.U4 = T3("U4", [4, 2, 4, 128], F32)
                self.cols2 = T3("cols2", [128, 2, 32], F32)
                self.aT4 = T3("aT4", [128, 4, 128], BF16)
                self.qs4 = T3("qs4", [128, 4, 128], BF16)
                self.ks4 = T3("ks4", [128, 4, 128], BF16)
                self.sc4 = T3("sc4", [128, 16], F32)
                self.Ew4, self.nw4, self.t14, self.t24, self.sq4 = self.wk[0], self.wk[1], self.wk[2], self.wk[3], self.wk[4]
                if self.n_seq * self.n_spans > 0:
                    s.op('pool', lambda e: e.memset(self.Vnat[:], 1.0), writes=['Vnat'])
                for q in range(self.n_seq):
                    for n in range(self.n_spans):
                        self.prompt_span(q, n)
                s.barrier()
                s.build()

            self.s = s = Sched(nc)
            self.wuse = 0
            self.wloaded = 0
            self.total_loads = NSLOT
            self.x_in_wk = False
            if self.do_sample:
                with contextlib.ExitStack() as st2:
                    T2 = lambda name, shape, dt: st2.enter_context(nc.sbuf_tensor(name, list(shape), dt))
                    self.wk = [T2("wks%d" % i, [128, 512], F32) for i in range(6)]
                    self.sample_phase(T2)
                    s.wait_all('sp', s.all_tokens())
                    s.build()
        return nc

    def norm_to_hT(self, tiles, gname, srckeys):
        s = self.s
        ident = self.cf('ident')
        gain = self.pbv(gname)
        for ti, (t, tsz) in enumerate(tiles):
            xs = self.xs_[ti % 2]
            xk = ('xs', ti % 2)
            ss = self.sm_[:, 0 + ti:1 + ti]
            rs = self.sm_[:, 4 + ti:5 + ti]
            xsrc, xkeys = self.get_xsrc(t)
            src = xsrc[0:tsz, :]
            s.op('act', lambda e, xs=xs, src=src, ss=ss, tsz=tsz: e.activation(out=xs[0:tsz, :], in_=src, func=AF.Square, accum_out=ss[0:tsz, :]),
                 reads=xkeys, writes=[xk, ('ss', ti), 'so'])
            self.rstd(ss[0:tsz, :], rs[0:tsz, :], 1.0 / D, [('ss', ti)], [('rs', ti)])
            s.op('act', lambda e, xs=xs, src=src, rs=rs, tsz=tsz: e.activation(out=xs[0:tsz, :], in_=src, func=AF.Copy, scale=rs[0:tsz, :]),
                 reads=xkeys + [('rs', ti)], writes=[xk])
            for half in range(2):
                b = (2 * ti + half) % 8
                bank = self.banks[b]
                for c in range(4):
                    kc = half * 4 + c
                    s.op('pe', lambda e, bank=bank, c=c, kc=kc, xs=xs, tsz=tsz: e.transpose(bank[:, c * 128:c * 128 + tsz], xs[0:tsz, kc * 128:(kc + 1) * 128], ident[0:tsz, 0:tsz]),
                         reads=[xk, 'cft'], writes=[self.bk[b]])
                dst = self.hT[:, half * 4:half * 4 + 4, t * 128:t * 128 + tsz]
                srcp = bank[:, :].rearrange("p (c n) -> p c n", n=128)[:, :, 0:tsz]
                g = bcast(gain[:, half * 4:half * 4 + 4].unsqueeze(2), [128, 4, tsz])
                s.op('dve', lambda e, dst=dst, srcp=srcp, g=g: e.tensor_tensor(out=dst, in0=srcp, in1=g, op=ALU.mult),
                     reads=[self.bk[b], 'pbt'], writes=[('hT', t)])

    def build_csg(self, ntile, tsz):
        s = self.s
        for which, nm in enumerate(['qg', 'kg']):
            g = self.pbv(nm)
            for kind, (tab, half) in enumerate([(0, 0), (1, 1), (1, 0), (0, 1)]):
                gb = bcast(g[0:tsz, half * 32:(half + 1) * 32].unsqueeze(1), [tsz, ntile, 32])
                s.op('dve', lambda e, which=which, kind=kind, tab=tab, gb=gb: e.tensor_tensor(out=self.csg[0:tsz, which, kind, 0:ntile, :], in0=self.cs[0:tsz, tab, 0:ntile, :], in1=gb, op=ALU.mult),
                     reads=['cs', 'pbt'], writes=['csg'])

    def get_xsrc(self, t):
        if getattr(self, 'x_in_wk', False):
            return self.wkt[:, 2 * t:2 * t + 2, :].rearrange("p a n -> p (a n)"), [('wk', 2 * t), ('wk', 2 * t + 1)]
        return self.x1[:, t, :], [('x1', t)]

    def rstd(self, ss, rs, inv_n, rk, wk_):
        s = self.s
        s.op('act', lambda e: e.activation(out=rs, in_=ss, func=AF.Ln, scale=inv_n, bias=self.epsc[0:ss.shape[0], :]), reads=list(rk) + ['epsc'], writes=list(wk_))
        s.op('act', lambda e: e.activation(out=rs, in_=rs, func=AF.Exp, scale=-0.5), reads=list(wk_), writes=list(wk_))

    def ffn(self, tiles, ntok, final_out=None):
        s = self.s
        nt = len(tiles)
        hk = [('hT', t) for t, _ in tiles]
        for j in range(NJ):
            slot, sk = self.wslot()
            bg, bu = j % 2, 2 + j % 2
            for kc in range(8):
                s.op('pe', lambda e, slot=slot, kc=kc, bg=bg: e.matmul(self.banks[bg][:, 0:ntok], slot[:, kc * 128:(kc + 1) * 128], self.hT[:, kc, 0:ntok], start=(kc == 0), stop=(kc == 7)),
                     reads=[sk] + hk, writes=[self.bk[bg]])
            for kc in range(8):
                s.op('pe', lambda e, slot=slot, kc=kc, bu=bu: e.matmul(self.banks[bu][:, 0:ntok], slot[:, 1024 + kc * 128:1024 + (kc + 1) * 128], self.hT[:, kc, 0:ntok], start=(kc == 0), stop=(kc == 7)),
                     reads=[sk] + hk, writes=[self.bk[bu]])
            sg = self.sg[j % 2]
            s.op('act', lambda e, sg=sg, bg=bg: e.activation(out=sg[:, 0:ntok], in_=self.banks[bg][:, 0:ntok], func=AF.Silu),
                 reads=[self.bk[bg]], writes=[('sg', j % 2)])
            s.op('dve', lambda e, sg=sg, bu=bu, j=j: e.tensor_tensor(out=self.actT[:, j, 0:ntok], in0=self.banks[bu][:, 0:ntok], in1=sg[:, 0:ntok], op=ALU.mult),
                 reads=[self.bk[bu], ('sg', j % 2)], writes=[('actT', j)] + self.alias_keys(j))
        for jj in range(11):
            slot, sk = self.wslot()
            for a in range(2):
                j = 2 * jj + a
                for ti, (t, tsz) in enumerate(tiles):
                    for hf in range(2):
                        b = ti * 2 + hf
                        s.op('pe', lambda e, slot=slot, a=a, j=j, t=t, tsz=tsz, hf=hf, b=b: e.matmul(
                            self.banks[b][0:tsz, :], self.actT[:, j, t * 128:t * 128 + tsz], slot[:, a * 1024 + hf * 512:a * 1024 + (hf + 1) * 512],
                            start=(j == 0), stop=(j == NJ - 1)),
                            reads=[sk, ('actT', j)], writes=[self.bk[b]])
        for ti, (t, tsz) in enumerate(tiles):
            for hf in range(2):
                b = ti * 2 + hf
                dst = self.x1[0:tsz, t, hf * 512:(hf + 1) * 512]
                xsrc, xkeys = self.get_xsrc(t)
                res = xsrc[0:tsz, hf * 512:(hf + 1) * 512]
                s.op('dve', lambda e, dst=dst, res=res, b=b, tsz=tsz: e.scalar_tensor_tensor(out=dst, in0=self.banks[b][0:tsz, :], scalar=0.5, in1=res, op0=ALU.mult, op1=ALU.add),
                     reads=[self.bk[b]] + xkeys, writes=[('x1', t)])

    def alias_keys(self, j):
        if j < 4:
            return ['QT']
        if j < 8:
            return ['mqT']
        if j < 12:
            return ['mkT']
        if j < 16:
            return ['ktm']
        if j < 21:
            return ['mva']
        return []

    def tm_matmuls(self, tiles, bank0):
        s = self.s
        for half in range(2):
            slot, sk = self.wslot()
            for ti, (t, tsz) in enumerate(tiles):
                b = bank0 + ti
                for c in range(4):
                    kc = half * 4 + c
                    s.op('pe', lambda e, slot=slot, c=c, kc=kc, t=t, tsz=tsz, b=b, half=half: e.matmul(
                        self.banks[b][0:tsz, :], self.hT[:, kc, t * 128:t * 128 + tsz], slot[:, c * 512:(c + 1) * 512],
                        start=(half == 0 and c == 0), stop=(half == 1 and c == 3)),
                        reads=[sk, ('hT', t)], writes=[self.bk[b]])

    def tm_group(self, tiles, bank0, evac):
        self.tm_matmuls(tiles, bank0)
        for ti, (t, tsz) in enumerate(tiles):
            evac(ti, t, tsz, bank0 + ti)

    def qk_evac_a(self, which, ti, t, tsz, b, kout):
        s = self.s
        bank = self.banks[b]
        nw_ = len(self.wk)
        i_sq, i_zn, i_kr = (ti, 4 + ti, 8 + ti) if nw_ >= 12 else (0, 1, 2)
        sq, zn, kr = self.wk[i_sq], self.wk[i_zn], self.wk[i_kr]
        ksq, kzn, krk = ('wk', i_sq), ('wk', i_zn), ('wk', i_kr)
        ssh = self.sm_[:, 16 + 8 * (ti % 2):24 + 8 * (ti % 2)] if False else self.ssh_[ti][:, 0:8]
        rsh = self.ssh_[ti][:, 8:16]
        sk_, rk_ = ('ssh', ti), ('rsh', ti)
        s.op('act', lambda e: e.activation(out=sq[0:tsz, :], in_=bank[0:tsz, :], func=AF.Square), reads=[self.bk[b]], writes=[ksq])
        yield
        s.op('dve', lambda e: e.tensor_reduce(out=ssh[0:tsz, :], in_=sq[0:tsz, :].rearrange("p (h d) -> p h d", d=64), axis=AX.X, op=ALU.add),
             reads=[ksq], writes=[sk_])
        yield
        s.op('act', lambda e: e.activation(out=rsh[0:tsz, :], in_=ssh[0:tsz, :], func=AF.Ln, scale=1.0 / 64, bias=self.epsc[0:tsz, :]), reads=[sk_, 'epsc'], writes=[rk_])
        yield
        s.op('act', lambda e: e.activation(out=rsh[0:tsz, :], in_=rsh[0:tsz, :], func=AF.Exp, scale=-0.5), reads=[rk_], writes=[rk_])
        yield
        b3 = bank[0:tsz, :].rearrange("p (h d) -> p h d", d=64)
        z3 = zn[0:tsz, :].rearrange("p (h d) -> p h d", d=64)
        s.op('dve', lambda e: e.tensor_tensor(out=z3, in0=b3, in1=bcast(rsh[0:tsz, :].unsqueeze(2), [tsz, 8, 64]), op=ALU.mult),
             reads=[self.bk[b], rk_], writes=[kzn])
        yield
        x1v, x2v = z3[:, :, 0:32], z3[:, :, 32:64]
        k3 = kr[0:tsz, :].rearrange("p (h d) -> p h d", d=64)
        o1, o2 = k3[:, :, 0:32], k3[:, :, 32:64]
        tabs = [bcast(self.csg[0:tsz, which, kind, ti, :].unsqueeze(1), [tsz, 8, 32]) for kind in range(4)]
        t3 = sq[0:tsz, :].rearrange("p (h d) -> p h d", d=64)
        ta, tb = t3[:, :, 0:32], t3[:, :, 32:64]
        s.op('dve', lambda e: e.tensor_tensor(out=o1, in0=x1v, in1=tabs[0], op=ALU.mult), reads=[kzn, 'csg'], writes=[krk])
        s.op('pool', lambda e: e.tensor_tensor(out=ta, in0=x2v, in1=tabs[1], op=ALU.mult), reads=[kzn, 'csg'], writes=[ksq])
        yield
        s.op('dve', lambda e: e.tensor_tensor(out=o2, in0=x2v, in1=tabs[3], op=ALU.mult), reads=[kzn, 'csg'], writes=[krk])
        s.op('pool', lambda e: e.tensor_tensor(out=tb, in0=x1v, in1=tabs[2], op=ALU.mult), reads=[kzn, 'csg'], writes=[ksq])
        yield
        s.op('dve', lambda e: e.tensor_tensor(out=o1, in0=o1, in1=ta, op=ALU.subtract), reads=[ksq, krk], writes=[krk])
        yield
        s.op('dve', lambda e: e.tensor_tensor(out=o2, in0=o2, in1=tb, op=ALU.add), reads=[ksq, krk], writes=[krk])
        if kout is not None:
            self.deferred.append(('kout%d' % ti, kout, kr[0:tsz, :], [krk]))
        yield

    def qk_evac_b(self, ti, t, tsz, b, dst_T, dst_key, col0):
        s = self.s
        nw_ = len(self.wk)
        i_kr = 8 + ti if nw_ >= 12 else 2
        kr, krk = self.wk[i_kr], ('wk', i_kr)
        ident = self.cf('ident')
        tbank = self.banks[b]
        for p in range(4):
            s.op('pe', lambda e, p=p: e.transpose(tbank[:, p * 128:p * 128 + tsz], kr[0:tsz, p * 128:(p + 1) * 128], ident[0:tsz, 0:tsz]),
                 reads=[krk, 'cft'], writes=[self.bk[b]])
        dst = dst_T[:, :, col0:col0 + tsz]
        s.op('act', lambda e: e.activation(out=dst, in_=tbank[:, :].rearrange("p (a n) -> p a n", n=128)[:, :, 0:tsz], func=AF.Copy),
             reads=[self.bk[b]], writes=[dst_key] + ([('actT', j) for j in range(4)] if dst_key == 'QT' else []))

    def run_gens(self, gens):
        gens = list(gens)
        while gens:
            nxt = []
            for g in gens:
                try:
                    next(g)
                    nxt.append(g)
                except StopIteration:
                    pass
            gens = nxt

    def w_in_phase(self, tiles, ntok, q, n, kout_fn, vout_fn, kt_col0):
        s = self.s
        self.deferred = []
        s.op('pool', lambda e: e.memset(self.mva[:, 0:len(tiles), :, 128:129], 1.0), writes=['mva'] + [('actT', j) for j in range(16, 21)])
        self.norm_to_hT(tiles, 'gm', None)
        KTd = self.KTs if n is None else self.KT
        self.tm_matmuls(tiles, 0)
        self.run_gens([self.qk_evac_a(0, ti, t, tsz, ti, None) for ti, (t, tsz) in enumerate(tiles)])
        self.tm_matmuls(tiles, 4)
        for ti, (t, tsz) in enumerate(tiles):
            self.qk_evac_b(ti, t, tsz, ti, self.QT, 'QT', t * 128)
        self.run_gens([self.qk_evac_a(1, ti, t, tsz, 4 + ti, kout_fn(t, tsz)) for ti, (t, tsz) in enumerate(tiles)])

        def v_evac(ti, t, tsz, b):
            vi = 4 + ti if len(self.wk) >= 12 else 4 + (ti % 2)
            vf = self.wk[vi]
            vk = ('wk', vi)
            s.op('act', lambda e: e.activation(out=vf[0:tsz, :], in_=self.banks[b][0:tsz, :], func=AF.Copy), reads=[self.bk[b]], writes=[vk])
            self.deferred.append(('vout%d' % ti, vout_fn(t, tsz), vf[0:tsz, :], [vk]))
            self.v_store(ti, t, tsz, self.banks[b], self.bk[b], n)
        self.tm_matmuls(tiles, 0)
        for ti, (t, tsz) in enumerate(tiles):
            self.qk_evac_b(ti, t, tsz, 4 + ti, KTd, 'KT', kt_col0 + t * 128)
        for ti, (t, tsz) in enumerate(tiles):
            v_evac(ti, t, tsz, ti)

        def mk_evac(ti, t, tsz, b):
            s.op('dve', lambda e: e.tensor_scalar(out=self.ktm[0:tsz, ti, :], in0=self.banks[b][0:tsz, :], scalar1=128.0 ** -0.5, scalar2=None, op0=ALU.mult),
                 reads=[self.bk[b]], writes=['ktm', ('actT', 12), ('actT', 13), ('actT', 14), ('actT', 15)])
        self.tm_group(tiles, 4, mk_evac)

        def mv_evac(ti, t, tsz, b):
            s.op('dve', lambda e: e.tensor_copy(self.mva[0:tsz, ti, :, 0:128], self.banks[b][0:tsz, :].rearrange("p (h c) -> p h c", c=128)),
                 reads=[self.bk[b]], writes=['mva'] + [('actT', j) for j in range(16, 21)])
        self.tm_group(tiles, 0, mv_evac)

        def mo_evac(ti, t, tsz, b):
            dst = self.xs_[ti // 2][0:tsz, (ti % 2) * 512:(ti % 2 + 1) * 512]
            s.op('act', lambda e: e.activation(out=dst, in_=self.banks[b][0:tsz, :], func=AF.Sigmoid),
                 reads=[self.bk[b]], writes=['so', ('xs', ti // 2)])
        self.tm_group(tiles, 4, mo_evac)

        for i in range(4):
            slot, sk = self.wslot()
            for u in range(2):
                head = (i % 2) * 2 + u
                b = (2 * i + u) % 4
                for kc in range(8):
                    s.op('pe', lambda e, slot=slot, u=u, kc=kc, b=b: e.matmul(self.banks[b][:, 0:ntok], slot[:, u * 1024 + kc * 128:u * 1024 + (kc + 1) * 128],
                                                                           self.hT[:, kc, 0:ntok], start=(kc == 0), stop=(kc == 7)),
                         reads=[sk] + [('hT', t) for t, _ in tiles], writes=[self.bk[b]])
                if i < 2:
                    s.op('act', lambda e, head=head, b=b: e.activation(out=self.mqT[:, head, 0:ntok], in_=self.banks[b][:, 0:ntok], func=AF.Copy),
                         reads=[self.bk[b]], writes=['mqT'] + [('actT', j) for j in range(4, 8)])
                else:
                    s.op('dve', lambda e, head=head, b=b: e.tensor_scalar(out=self.mkT[:, head, 0:ntok], in0=self.banks[b][:, 0:ntok], scalar1=128.0 ** -0.5, scalar2=None, op0=ALU.mult),
                         reads=[self.bk[b]], writes=['mkT'] + [('actT', j) for j in range(8, 12)])
        for gi in range(2):
            b = 4 + gi
            for kc in range(8):
                s.op('pe', lambda e, gi=gi, kc=kc, b=b: e.matmul(self.banks[b][0:4, 0:ntok], self.wgb[:, kc * 8 + gi * 4:kc * 8 + gi * 4 + 4], self.hT[:, kc, 0:ntok],
                                                               start=(kc == 0), stop=(kc == 7)),
                     reads=['wgb'] + [('hT', t) for t, _ in tiles], writes=[self.bk[b]])
        bi = self.pbv('bi')
        nbf = self.pbv('nbf')
        R = self.rows
        s.op('act', lambda e: e.activation(out=R[0:4, 0, 0:ntok], in_=self.banks[4][0:4, 0:ntok], func=AF.Identity, bias=bi[0:4, :]),
             reads=[self.bk[4], 'pbt'], writes=['ig'])
        s.op('dve', lambda e: e.tensor_scalar(out=R[0:4, 1, 0:ntok], in0=self.banks[5][0:4, 0:ntok], scalar1=nbf[0:4, :], scalar2=-1.0, op0=ALU.add, op1=ALU.mult),
             reads=[self.bk[5], 'pbt'], writes=['nlf'])
        s.op('act', lambda e: e.activation(out=R[0:4, 1, 0:ntok], in_=R[0:4, 1, 0:ntok], func=AF.Exp), reads=['nlf'], writes=['nlf'])
        s.op('dve', lambda e: e.tensor_scalar(out=R[0:4, 1, 0:ntok], in0=R[0:4, 1, 0:ntok], scalar1=1.0, scalar2=None, op0=ALU.add), reads=['nlf'], writes=['nlf'])
        s.op('act', lambda e: e.activation(out=R[0:4, 1, 0:ntok], in_=R[0:4, 1, 0:ntok], func=AF.Ln), reads=['nlf'], writes=['nlf'])

        for (sem, dst, src, rk_) in self.deferred:
            s.dma('sp', sem, lambda e, dst=dst, src=src: e.dma_start(out=dst, in_=src), reads=rk_)
        self.deferred = []

    def v_store(self, ti, t, tsz, vf, vk, n):
        s = self.s
        if n is None:
            s.op('act', lambda e: e.activation(out=self.Vnew[0:tsz, :].rearrange("p (h c) -> p h c", c=65)[:, :, 0:64],
                                               in_=vf[0:tsz, :].rearrange("p (h c) -> p h c", c=64), func=AF.Copy), reads=[vk], writes=['Vnew'])
            return
        tile_i = 4 * (n % 2) + ti
        s.op('act', lambda e: e.activation(out=self.Vnat[0:tsz, tile_i, :].rearrange("p (h c) -> p h c", c=65)[:, :, 0:64],
                                           in_=vf[0:tsz, :].rearrange("p (h c) -> p h c", c=64), func=AF.Copy),
             reads=[vk], writes=['Vnat'])
        if ti == 3:
            blk = 4 * (n % 2)
            s.dma('sp', 'vw', lambda e: e.dma_start(out=self.vscr.rearrange("(t p) c -> p t c", p=128), in_=self.Vnat[:, blk:blk + 4, :]),
                  reads=['Vnat'], writes=['vscr'])
            s.dma('sp', 'vr4', lambda e: e.dma_start(out=self.V4[:, blk:blk + 4, :], in_=self.vscr.rearrange("(i g) c -> i g c", g=4)),
                  reads=['vscr'], writes=['V4'])
            s.dma('sp', 'vr16', lambda e: e.dma_start(out=self.V16[32 * n:32 * n + 32, :, :], in_=self.vscr.rearrange("(i r) c -> i r c", r=16)),
                  reads=['vscr'], writes=['V16'])

    def attention(self, n):
        s = self.s
        mprev = self.cbm('mprev')
        mcur = self.cbm('mcur')
        m16 = self.cbm('m16').rearrange("p (n c) -> p n c", c=32)
        jobs = []
        for h in range(8):
            p, off = h // 2, (h % 2) * 64
            QTh = self.QT[off:off + 64, p, :]
            KTh = self.KT[off:off + 64, p, :]
            acc = self.banks[6 + (h % 2)]
            Q4 = QTh.rearrange("d (i g) -> d g i", g=4)
            K4 = KTh.rearrange("d (i g) -> d g i", g=4)
            A4 = acc[0:65, :].rearrange("p (i g) -> p g i", g=4)
            Q16 = QTh.rearrange("d (i r) -> d r i", r=16)
            K16 = KTh.rearrange("d (i r) -> d r i", r=16)
            A16 = acc[0:65, :].rearrange("p (i r) -> p r i", r=16)
            hj = []
            tl = []
            for b in range(4):
                B = 4 * n + b
                if B == 0:
                    continue
                tl.append((KTh[:, (B - 1) * 128:B * 128], QTh[:, b * 128:(b + 1) * 128], b * 128, 128,
                           self.Vnat[:, (B - 1) % 8, h * 65:(h + 1) * 65], acc[0:65, b * 128:(b + 1) * 128]))
            c0 = 128 if n == 0 else 0
            hj.append((tl, 128, c0, bcast(mprev.unsqueeze(1), [128, (512 - c0) // 128, 128]), 128))
            tl = []
            for b in range(4):
                B = 4 * n + b
                tl.append((KTh[:, B * 128:(B + 1) * 128], QTh[:, b * 128:(b + 1) * 128], b * 128, 128,
                           self.Vnat[:, B % 8, h * 65:(h + 1) * 65], acc[0:65, b * 128:(b + 1) * 128]))
            hj.append((tl, 128, 0, bcast(mcur.unsqueeze(1), [128, 4, 128]), 128))
            if n > 0:
                tl = []
                for g in range(4):
                    tl.append((K4[:, g, (n - 1) * 128:n * 128], Q4[:, g, :], g * 128, 128,
                               self.V4[:, (4 * (n - 1) + g) % 8, h * 65:(h + 1) * 65], A4[:, g, :]))
                hj.append((tl, 128, 0, bcast(mprev.unsqueeze(1), [128, 4, 128]), 128))
            tl = []
            for g in range(4):
                tl.append((K4[:, g, n * 128:(n + 1) * 128], Q4[:, g, :], g * 128, 128,
                           self.V4[:, (4 * n + g) % 8, h * 65:(h + 1) * 65], A4[:, g, :]))
            hj.append((tl, 128, 0, bcast(mcur.unsqueeze(1), [128, 4, 128]), 128))
            nk = 32 * (n + 1)
            tl = []
            for r in range(16):
                tl.append((K16[:, r, 0:nk], Q16[:, r, :], r * 32, 32, self.V16[0:nk, r, h * 65:(h + 1) * 65], A16[:, r, :]))
            hj.append((tl, nk, 0, bcast(m16[0:nk, n, :].unsqueeze(1), [nk, 16, 32]), 32))
            for bi_, it in enumerate(hj):
                jobs.append((h, it, bi_ == 0, bi_ == len(hj) - 1))
        N = len(jobs)
        NPT = len(self.PTs)

        def qk(i):
            h, (tl, nk, c0, mk_, tw), first, last = jobs[i]
            sb = i % 5
            bank = self.banks[sb]
            for (kap, qap, col, nq, vap, oap) in tl:
                s.op('pe', lambda e, kap=kap, qap=qap, col=col, nq=nq, bank=bank, nk=nk: e.matmul(bank[0:nk, col:col + nq], kap, qap, start=True, stop=True),
                     reads=['KT', 'QT'], writes=[self.bk[sb]])

        def em(i):
            h, (tl, nk, c0, mk_, tw), first, last = jobs[i]
            sb = i % 5
            bank = self.banks[sb]
            PT = self.PTs[i % NPT]
            pk = ('PT', i % NPT)
            s.op('act', lambda e: e.activation(out=PT[0:nk, c0:512], in_=bank[0:nk, c0:512], func=AF.Exp, scale=0.125), reads=[self.bk[sb]], writes=[pk])
            pv = PT[0:nk, c0:512].rearrange("p (a n) -> p a n", n=tw)
            s.op('dve', lambda e: e.tensor_tensor(out=pv, in0=pv, in1=mk_, op=ALU.mult), reads=[pk, 'cbt'], writes=[pk])

        def pvm(i):
            h, (tl, nk, c0, mk_, tw), first, last = jobs[i]
            PT = self.PTs[i % NPT]
            pk = ('PT', i % NPT)
            accb = 6 + (h % 2)
            nt_ = len(tl)
            for j, (kap, qap, col, nq, vap, oap) in enumerate(tl):
                s.op('pe', lambda e, j=j, col=col, nq=nq, vap=vap, oap=oap: e.matmul(oap, vap, PT[0:nk, col:col + nq], start=(first and j == 0), stop=(last and j == nt_ - 1), skip_group_check=True),
                     reads=[pk, 'Vnat', 'V4', 'V16'], writes=[self.bk[accb]])

        pending = []
        LA = 4
        for i in range(min(LA, N)):
            qk(i)
        for i in range(N):
            em(i)
            if i + LA < N:
                qk(i + LA)
            pvm(i)
            h, _, first, last = jobs[i]
            if last:
                self.attn_norm_a(h)
                pending.append((h, i + 2))
            for (hh, when) in list(pending):
                if when <= i:
                    self.attn_norm_b(hh)
                    pending.remove((hh, when))
        for (hh, when) in pending:
            self.attn_norm_b(hh)

    def attn_norm_a(self, h):
        s = self.s
        accb = 6 + (h % 2)
        aw = self.accw[h % 2]
        ak = ('accw', h % 2)
        s.op('act', lambda e: e.activation(out=aw[0:65, :], in_=self.banks[accb][0:65, :], func=AF.Copy), reads=[self.bk[accb]], writes=[ak])
        s.op('act', lambda e: e.activation(out=aw[64:65, :], in_=aw[64:65, :], func=AF.Ln), reads=[ak], writes=[ak])
        s.op('act', lambda e: e.activation(out=aw[64:65, :], in_=aw[64:65, :], func=AF.Exp, scale=-1.0), reads=[ak], writes=[ak])

    def attn_norm_b(self, h):
        s = self.s
        aw = self.accw[h % 2]
        ak = ('accw', h % 2)
        ones = self.cf('onescol')
        s.op('pe', lambda e: e.matmul(self.banks[5][0:64, :], ones[64:65, 0:64], aw[64:65, :], start=True, stop=True), reads=[ak, 'cft'], writes=[self.bk[5]])
        s.op('dve', lambda e: e.tensor_tensor(out=self.attT[0:64, h, :], in0=aw[0:64, :], in1=self.banks[5][0:64, :], op=ALU.mult),
             reads=[ak, self.bk[5]], writes=['attT'])

    def attn_norm(self, h, accb, bcb, ncol, dst):
        s = self.s
        aw = self.accw[h % 2]
        ak = ('accw', h % 2)
        s.op('act', lambda e: e.activation(out=aw[0:65, 0:ncol], in_=self.banks[accb][0:65, 0:ncol], func=AF.Copy), reads=[self.bk[accb]], writes=[ak])
        s.op('dve', lambda e: e.reciprocal(aw[64:65, 0:ncol], aw[64:65, 0:ncol]), reads=[ak], writes=[ak])
        ones = self.cf('onescol')
        s.op('pe', lambda e: e.matmul(self.banks[bcb][0:64, 0:ncol], ones[64:65, 0:64], aw[64:65, 0:ncol], start=True, stop=True),
             reads=[ak, 'cft'], writes=[self.bk[bcb]])
        s.op('dve', lambda e: e.tensor_tensor(out=dst, in0=aw[0:64, 0:ncol], in1=self.banks[bcb][0:64, 0:ncol], op=ALU.mult),
             reads=[ak, self.bk[bcb]], writes=['attT'])

    def ml_cols_cbc(self, L, rk, posm):
        s = self.s
        CR = self.crow
        nb, u, c, wint, emt, wsr = [CR[0:4, i, 0:L] for i in range(6)]
        ident = self.cf('ident')
        sel = self.cf('sel').rearrange("p (h m) -> p h m", m=128)
        for i, src in enumerate([u, wint, emt, wsr]):
            s.op('pe', lambda e, i=i, src=src: e.transpose(self.banks[1][0:L, 4 * i:4 * i + 4], src, ident[0:4, 0:4]), reads=[rk, 'cft'], writes=[self.bk[1]])
        s.op('act', lambda e: e.activation(out=self.cols[0:L, :], in_=self.banks[1][0:L, 0:16], func=AF.Copy), reads=[self.bk[1]], writes=['cols'])
        for h in range(4):
            s.op('pe', lambda e, h=h: e.matmul(self.banks[0][0:L, h * 128:h * 128 + L], sel[0:4, h, 0:L], c, start=True, stop=False), reads=[rk, 'cft'], writes=[self.bk[0]])
            s.op('pe', lambda e, h=h: e.matmul(self.banks[0][0:L, h * 128:h * 128 + L], ident[0:L, 0:L], posm[0:L, 0:L], start=False, stop=True), reads=['cft'], writes=[self.bk[0]])

    def ml_head_av(self, h, L, ci, cs_):
        s = self.s
        stb = 2 + (h % 2)
        s.op('pe', lambda e: e.matmul(self.banks[stb][0:L, 0:L], self.mkT[:, h, cs_], self.mqT[:, h, cs_], start=True, stop=True),
             reads=['mkT', 'mqT'], writes=[self.bk[stb]])
        s.op('act', lambda e: e.activation(out=self.Ew[0:L, 0:L], in_=self.banks[0][0:L, h * 128:h * 128 + L], func=AF.Exp, scale=-1.0, bias=self.cols[0:L, h:h + 1]),
             reads=[self.bk[0], 'cols'], writes=['Ew'])
        s.op('dve', lambda e: e.tensor_tensor(out=self.aT[0:L, 0:L], in0=self.banks[stb][0:L, 0:L], in1=self.Ew[0:L, 0:L], op=ALU.mult),
             reads=[self.bk[stb], 'Ew'], writes=['aT'])
        s.op('pe', lambda e: e.matmul(self.banks[4][0:L, 0:129], self.aT[0:L, 0:L], self.mva[0:L, ci, h, :], start=True, stop=True),
             reads=['aT', 'mva'], writes=[self.bk[4]])

    def ml_head_out(self, h, L, ci, cs_):
        s = self.s
        ident = self.cf('ident')
        mlg = self.pbv('mlg')
        nw = self.numw
        s.op('act', lambda e: e.activation(out=nw[0:L, 0:129], in_=self.banks[5][0:L, 0:129], func=AF.Copy, scale=self.cols[0:L, 4 + h:5 + h]),
             reads=[self.bk[5], 'cols'], writes=['numw'])
        s.op('dve', lambda e: e.tensor_tensor(out=nw[0:L, 0:129], in0=nw[0:L, 0:129], in1=self.banks[4][0:L, 0:129], op=ALU.add),
             reads=[self.bk[4], 'numw'], writes=['numw'])
        den, ssq, rr = nw[0:L, 129:130], nw[0:L, 130:131], nw[0:L, 131:132]
        s.op('dve', lambda e: e.scalar_tensor_tensor(out=den, in0=nw[0:L, 128:129], scalar=-1.0, in1=nw[0:L, 128:129], op0=ALU.mult, op1=ALU.max),
             reads=['numw'], writes=['numw'])
        s.op('dve', lambda e: e.tensor_tensor(out=den, in0=den, in1=self.cols[0:L, 8 + h:9 + h], op=ALU.max),
             reads=['numw', 'cols'], writes=['numw'])
        s.op('dve', lambda e: e.reciprocal(den, den), reads=['numw'], writes=['numw'])
        hgj = self.hg[0:L, 0, :]
        s.op('act', lambda e: e.activation(out=hgj, in_=nw[0:L, 0:128], func=AF.Square, scale=den, accum_out=ssq), reads=['numw'], writes=['hg0', 'ssq'])
        self.rstd(ssq, ssq, 1.0 / 128, ['ssq'], ['ssq'])
        s.op('dve', lambda e: e.tensor_tensor(out=rr, in0=den, in1=ssq, op=ALU.mult), reads=['numw', 'ssq'], writes=['rr'])
        s.op('dve', lambda e: e.scalar_tensor_tensor(out=hgj, in0=nw[0:L, 0:128], scalar=rr, in1=mlg[0:L, h * 128:(h + 1) * 128], op0=ALU.mult, op1=ALU.mult),
             reads=['numw', 'rr', 'pbt'], writes=['hg0'])
        so_ap = self.xs_[ci // 2][0:L, (ci % 2) * 512 + h * 128:(ci % 2) * 512 + (h + 1) * 128]
        hg2 = self.hg[0:L, 1, :]
        s.op('pool', lambda e: e.tensor_tensor(out=hg2, in0=hgj, in1=so_ap, op=ALU.mult), reads=['hg0', 'so'], writes=['hg1'])
        s.op('pe', lambda e: e.transpose(self.banks[7][:, 0:L], hg2, ident[0:L, 0:L]), reads=['hg1', 'cft'], writes=[self.bk[7]])
        s.op('act', lambda e: e.activation(out=self.hmT[:, h, cs_], in_=self.banks[7][:, 0:L], func=AF.Copy), reads=[self.bk[7]], writes=['hmT'])

    def mlstm_chunk(self, ci, tsz, col0, first_of_seq):
        s = self.s
        R = self.rows
        L = tsz
        cs_ = slice(col0, col0 + L)
        CR = self.crow
        ig, nlf = R[0:4, 0, cs_], R[0:4, 1, cs_]
        nb, u, c, wint, emt, wsr = [CR[0:4, i, 0:L] for i in range(6)]
        mprev, ncl = self.mrow[0:4, 0:1], self.mrow[0:4, 1:2]
        ones4 = self.cf('onesrow')[0:4, 0:L]
        rk = ('rows', ci)
        s.op('dve', lambda e: e.tensor_tensor_scan(nb, ones4, nlf, 0.0, ALU.mult, ALU.add), reads=['nlf', 'cft'], writes=[rk])
        s.op('dve', lambda e: e.tensor_tensor(out=u, in0=ig, in1=nb, op=ALU.add), reads=['ig', rk], writes=[rk])
        s.op('dve', lambda e: e.tensor_tensor_scan(c, u, u, mprev, ALU.max, ALU.max), reads=[rk, 'mprev'], writes=[rk])
        s.op('act', lambda e: e.activation(out=wint, in_=c, func=AF.Exp, scale=-1.0, bias=mprev), reads=[rk, 'mprev'], writes=[rk])
        s.op('dve', lambda e: e.tensor_tensor(out=emt, in0=nb, in1=c, op=ALU.subtract), reads=[rk], writes=[rk])
        s.op('act', lambda e: e.activation(out=emt, in_=emt, func=AF.Exp), reads=[rk], writes=[rk])
        s.op('dve', lambda e: e.tensor_scalar(out=ncl, in0=CR[0:4, 2, L - 1:L], scalar1=-1.0, scalar2=None, op0=ALU.mult), reads=[rk], writes=['ncl'])
        s.op('act', lambda e: e.activation(out=wsr, in_=u, func=AF.Exp, bias=ncl), reads=[rk, 'ncl'], writes=[rk])
        s.op('dve', lambda e: e.tensor_tensor(out=mprev, in0=CR[0:4, 2, L - 1:L], in1=CR[0:4, 0, L - 1:L], op=ALU.subtract),
             reads=[rk], writes=['mprev'])
        self.ml_cols_cbc(L, rk, self.cf('posmask'))
        sel = self.cf('sel').rearrange("p (h m) -> p h m", m=128)
        for h in range(4):
            s.op('pe', lambda e, h=h: e.matmul(self.banks[1][:, 32 + h:33 + h], sel[0:4, h, :], CR[0:4, 3, L - 1:L], start=True, stop=True),
                 reads=[rk, 'cft'], writes=[self.bk[1]])
        s.op('act', lambda e: e.activation(out=self.wcs[:, :], in_=self.banks[1][:, 32:36], func=AF.Copy), reads=[self.bk[1]], writes=['wcs'])
        for h in range(4):
            self.ml_head_av(h, L, ci, cs_)
            s.op('pe', lambda e, h=h: e.matmul(self.banks[5][0:L, 0:129], self.mqT[:, h, cs_], self.Cbf[:, h, :], start=True, stop=True),
                 reads=['mqT', 'Cbf'], writes=[self.bk[5]])
            self.ml_head_out(h, L, ci, cs_)
            s.op('dve', lambda e, h=h: e.tensor_scalar(out=self.ks[0:L, :], in0=self.ktm[0:L, ci, h * 128:(h + 1) * 128], scalar1=self.cols[0:L, 12 + h:13 + h], scalar2=None, op0=ALU.mult),
                 reads=['ktm', 'cols'], writes=['ks'])
            s.op('pe', lambda e, h=h: e.matmul(self.banks[6][:, 0:129], self.ks[0:L, :], self.mva[0:L, ci, h, :], start=True, stop=True),
                 reads=['ks', 'mva'], writes=[self.bk[6]])
            s.op('dve', lambda e, h=h: e.scalar_tensor_tensor(out=self.Caug[:, h, :], in0=self.Caug[:, h, :], scalar=self.wcs[:, h:h + 1], in1=self.banks[6][:, 0:129], op0=ALU.mult, op1=ALU.add),
                 reads=[self.bk[6], 'wcs', 'Caug'], writes=['Caug'])
            s.op('act', lambda e, h=h: e.activation(out=self.Cbf[:, h, :], in_=self.Caug[:, h, :], func=AF.Copy), reads=['Caug'], writes=['Cbf'])

    def sample_phase(self, T):
        s = self.s
        tiles = [(0, 64)]
        ntok = 64
        self.crow = T("crow", [4, 6, 128], F32)
        self.cols = T("cols", [128, 16], F32)
        self.wcs = T("wcs", [128, 4], F32)
        self.Ew = T("Ew", [128, 128], F32)
        self.aT = T("aT", [128, 128], BF16)
        self.ks = T("ks", [128, 128], BF16)
        self.numw = T("numw", [128, 132], F32)
        self.hg = T("hg", [128, 2, 128], F32)
        self.KTs = T("KTs", [128, 4, 64], BF16)
        self.Vnew = T("Vnew", [64, 520], BF16)
        x1keys = [('x1', 0)]
        s.dma('sp', 'xin', lambda e: e.dma_start(out=self.x1[0:64, 0, :], in_=self.xs), writes=x1keys)
        s.dma('sp', 'csin', lambda e: e.dma_start(out=self.cs[0:64, 0, 0, :], in_=self.coss), writes=['cs'])
        s.dma('sp', 'csin', lambda e: e.dma_start(out=self.cs[0:64, 1, 0, :], in_=self.sins), writes=['cs'])
        self.build_csg(1, 64)
        self.mark('s_start')
        if 'pre' in self.sample_parts:
            self.sample_prefetch(T)
        self.norm_to_hT(tiles, 'g1', None)
        self.ffn(tiles, ntok)
        self.mark('s_ffn1')
        self.w_in_phase(tiles, ntok, None, None, lambda t, tsz: self.ksn[0:64, :], lambda t, tsz: self.vsn[0:64, :], 0)
        self.mark('s_w_in')
        if 'dummy' in self.sample_parts:
            self._dummy = T('dummy_pad', [128, 6144], BF16)
        if 'attn' in self.sample_parts:
            self.sample_attention(T)
        self.mark('s_attn')
        if 'ml' in self.sample_parts:
            self.sample_mlstm(T)
        self.mark('s_ml')
        self.w_out_phase(tiles)
        self.norm_to_hT(tiles, 'g2', None)
        self.ffn(tiles, ntok)
        self.mark('s_end')
        s.dma('sp', 'yout', lambda e: e.dma_start(out=self.ys, in_=self.x1[0:64, 0, :]), reads=x1keys)

    def cache_load(self, b):
        s = self.s
        base = b * WIN_BUF
        for nm, src, bufs in (('kc', self.ck, self.kcb), ('vc', self.cv, self.vcb)):
            buf = bufs[b % 2]
            key = (nm, b % 2)
            sem = '%s%d' % (nm, b % 2)
            s.dma('pool', sem, lambda e, buf=buf, src=src: e.dma_start(out=buf[:, 0, :], in_=src[base + 1920:base + 2048, :], max_dma_last_dim=4096), writes=[key])
            s.dma('pool', sem, lambda e, buf=buf, src=src: e.dma_start(out=buf[:, 1:5, :], in_=src[base + 1536:base + 2048, :].rearrange("(i g) c -> i g c", g=4), max_dma_last_dim=4096), writes=[key])
            s.dma('pool', sem, lambda e, buf=buf, src=src: e.dma_start(out=buf[:, 5:9, :], in_=src[base:base + 2048, :].rearrange("(i r) c -> i r c", r=16)[:, 0:4, :], max_dma_last_dim=4096), writes=[key])

    def sample_prefetch(self, T):
        self.kcb = [T("kcb%d" % i, [128, 9, 512], BF16) for i in range(2)]
        self.vcb = [T("vcb%d" % i, [128, 9, 512], BF16) for i in range(2)]
        self.cache_load(0)
        self.cache_load(1)

    def sample_attention(self, T):
        s = self.s
        KcT = T("KcT", [128, 36, 128], BF16)
        PTs = [T("PTs%d" % i, [128, 288], BF16) for i in range(2)]
        accb = 6
        acc = self.banks[accb]
        onesb = self.cbm('onesb')
        identb = self.cbm('identb')
        mA = self.cbm('mA')
        mnew = self.cbm('mnew64')
        Qpad = T("Qpad", [128, 4, 2, 64], BF16)
        self.Qpad = Qpad
        s.op('pool', lambda e: e.memset(Qpad[:], 0.0), writes=['Qpad'])
        s.op('pool', lambda e: e.tensor_copy(Qpad[0:64, :, 0, :], self.QT[0:64, :, 0:64]), reads=['QT'], writes=['Qpad'])
        s.op('pool', lambda e: e.tensor_copy(Qpad[64:128, :, 1, :], self.QT[64:128, :, 0:64]), reads=['QT'], writes=['Qpad'])
        for p in range(4):
            s.op('pe', lambda e, p=p: e.matmul(self.banks[0][0:64, p * 128:(p + 1) * 128], self.KTs[:, p, 0:64], Qpad[:, p, :, :], start=True, stop=True),
                 reads=['KT', 'Qpad'], writes=[self.bk[0]])
        if getattr(self, 'sa_level', 9) == -3:
            return
        s.op('act', lambda e: e.activation(out=self.PT[0:64, :], in_=self.banks[0][0:64, :], func=AF.Exp, scale=0.125), reads=[self.bk[0]], writes=['PT'])
        if getattr(self, 'sa_level', 9) == -4:
            return
        pv = self.PT[0:64, :].rearrange("p (h n) -> p h n", n=64)
        s.op('dve', lambda e: e.tensor_tensor(out=pv, in0=pv, in1=bcast(mnew[0:64, :].unsqueeze(1), [64, 8, 64]), op=ALU.mult), reads=['PT', 'cbt'], writes=['PT'])
        if getattr(self, 'sa_level', 9) == -1:
            return
        for h in range(8):
            s.op('pe', lambda e, h=h: e.matmul(acc[0:64, h * 64:(h + 1) * 64], self.Vnew[0:64, h * 65:h * 65 + 64], self.PT[0:64, h * 64:(h + 1) * 64],
                                               start=(h == 0), stop=False, skip_group_check=True), reads=['PT', 'Vnew'], writes=[self.bk[accb]])
        s.op('pe', lambda e: e.matmul(acc[64:65, :], onesb[0:64, 0:1], self.PT[0:64, :], start=True, stop=False, skip_group_check=True),
             reads=['PT', 'cbt'], writes=[self.bk[accb]])
        if getattr(self, 'sa_level', 9) == -2:
            return
        acc3 = acc[0:64, :].rearrange("p (h n) -> p h n", n=64)
        den3 = acc[64:65, :].rearrange("p (h n) -> p h n", n=64)
        args = (KcT, PTs, acc, accb, onesb, identb, mA)
        self._sample_attn_seq(0, *args, 'T')
        self._sample_attn_seq(0, *args, 'E')
        for b in range(NSB):
            self._sample_attn_seq(b, *args, 'QK')
            if b + 1 < NSB:
                self._sample_attn_seq(b + 1, *args, 'T')
            self._sample_attn_seq(b, *args, 'EM')
            if b + 1 < NSB:
                self._sample_attn_seq(b + 1, *args, 'E')
            self._sample_attn_seq(b, *args, 'PV')
        aw = self.accw[0]
        s.op('act', lambda e: e.activation(out=aw[0:65, :], in_=acc[0:65, :], func=AF.Copy), reads=[self.bk[accb]], writes=[('accw', 0)])
        s.op('act', lambda e: e.activation(out=aw[64:65, :], in_=aw[64:65, :], func=AF.Ln), reads=[('accw', 0)], writes=[('accw', 0)])
        s.op('act', lambda e: e.activation(out=aw[64:65, :], in_=aw[64:65, :], func=AF.Exp, scale=-1.0), reads=[('accw', 0)], writes=[('accw', 0)])
        ones = self.cf('onescol')
        s.op('pe', lambda e: e.matmul(self.banks[7][0:64, :], ones[64:65, 0:64], aw[64:65, :], start=True, stop=True), reads=[('accw', 0), 'cft'], writes=[self.bk[7]])
        s.op('dve', lambda e: e.tensor_tensor(out=self.attT[0:64, :, 0:64], in0=aw[0:64, :].rearrange("p (h n) -> p h n", n=64),
                                              in1=self.banks[7][0:64, :].rearrange("p (h n) -> p h n", n=64), op=ALU.mult),
             reads=[('accw', 0), self.bk[7]], writes=['attT'])

    def _sample_attn_seq(self, b, KcT, PTs, acc, accb, onesb, identb, mA, part):
        s = self.s
        den3 = acc[64:65, :].rearrange("p (h n) -> p h n", n=64)
        kb, vb = self.kcb[b % 2], self.vcb[b % 2]
        kk, vk = ('kc', b % 2), ('vc', b % 2)
        PTb = PTs[b % 2]
        pk = ('PTs', b % 2)
        if part in ('T', 'E'):
            for bb in range(5):
                bank = self.banks[1 + bb]
                bbf = bank[:, :].bitcast(BF16)
                n_in = 8 if bb < 4 else 4
                if part == 'T':
                    for j in range(n_in):
                        idx = bb * 8 + j
                        tile_, p = idx // 4, idx % 4
                        s.op('pe', lambda e, bbf=bbf, j=j, tile_=tile_, p=p, kb=kb: e.transpose(bbf[:, j * 128:(j + 1) * 128], kb[:, tile_, p * 128:(p + 1) * 128], identb),
                             reads=[kk, 'cbt'], writes=[self.bk[1 + bb]])
                else:
                    dst = KcT[:, bb * 8:bb * 8 + n_in, :]
                    srcv = bbf[:, 0:n_in * 128].rearrange("p (a n) -> p a n", n=128)
                    if bb % 2 == 0:
                        s.op('act', lambda e, dst=dst, srcv=srcv: e.activation(out=dst, in_=srcv, func=AF.Copy), reads=[self.bk[1 + bb]], writes=[('KcT', bb)])
                    else:
                        s.op('dve', lambda e, dst=dst, srcv=srcv: e.tensor_copy(dst, srcv), reads=[self.bk[1 + bb]], writes=[('KcT', bb)])
            return
        if part == 'QK':
            for tile_ in range(9):
                for p in range(4):
                    idx = tile_ * 4 + p
                    col = (tile_ * 8 + 2 * p) * 4
                    s.op('pe', lambda e, idx=idx, col=col, p=p: e.matmul(self.banks[0][:, col:col + 8].rearrange("r (a t) -> r a t", t=4), KcT[:, idx, :], self.Qpad[:, p, :, 4 * b:4 * b + 4], start=True, stop=True),
                         reads=[('KcT', idx // 8), 'Qpad'], writes=[self.bk[0]])
            return
        if part == 'EM':
            s.op('act', lambda e: e.activation(out=PTb[:, :], in_=self.banks[0][:, 0:288], func=AF.Exp, scale=0.125), reads=[self.bk[0]], writes=[pk])
            pa = PTb[:, 0:32].rearrange("p (h t) -> p h t", t=4)
            s.op('dve', lambda e: e.tensor_tensor(out=pa, in0=pa, in1=bcast(mA.unsqueeze(1), [128, 8, 4]), op=ALU.mult), reads=[pk, 'cbt'], writes=[pk])
            return
        last_b = (b == NSB - 1)
        for h in range(8):
            s.op('pe', lambda e, h=h: e.matmul(acc[0:64, h * 64 + 4 * b:h * 64 + 4 * b + 4], vb[:, 0, h * 64:(h + 1) * 64], PTb[:, h * 4:h * 4 + 4],
                                               start=False, stop=False, skip_group_check=True), reads=[pk, vk], writes=[self.bk[accb]])
            for t in range(4):
                for grp in (1, 5):
                    tile_ = grp + t
                    col = (tile_ * 8 + h) * 4 + t
                    s.op('pe', lambda e, h=h, t=t, tile_=tile_, col=col: e.matmul(acc[0:64, h * 64 + 4 * b + t:h * 64 + 4 * b + t + 1], vb[:, tile_, h * 64:(h + 1) * 64],
                                                                             PTb[:, col:col + 1], start=False, stop=False, skip_group_check=True),
                         reads=[pk, vk], writes=[self.bk[accb]])
        for h in range(8):
            s.op('pe', lambda e, h=h: e.matmul(acc[64:65, h * 64 + 4 * b:h * 64 + 4 * b + 4], onesb[:, 0:1], PTb[:, h * 4:h * 4 + 4], start=False, stop=False, skip_group_check=True),
                 reads=[pk, 'cbt'], writes=[self.bk[accb]])
        for t in range(4):
            for grp in (1, 5):
                tile_ = grp + t
                rhs = PTb[:, tile_ * 32:(tile_ + 1) * 32].rearrange("p (h t) -> p h t", t=4)[:, :, t:t + 1]
                lastmm = last_b and t == 3 and grp == 5
                s.op('pe', lambda e, rhs=rhs, t=t, lastmm=lastmm: e.matmul(den3[:, :, 4 * b + t:4 * b + t + 1], onesb[:, 0:1], rhs, start=False, stop=lastmm, skip_group_check=True),
                     reads=[pk, 'cbt'], writes=[self.bk[accb]])
        if b + 2 < NSB:
            self.cache_load(b + 2)

    def sample_mlstm(self, T):
        s = self.s
        L = 64
        ci = 0
        cs_ = slice(0, 64)
        R = self.rows
        CR = self.crow
        ig, nlf = R[0:4, 0, cs_], R[0:4, 1, cs_]
        nb, u, c, wint, emt, wsr = [CR[0:4, i, 0:L] for i in range(6)]
        v3 = lambda ap: ap.rearrange("p (b t) -> p b t", t=4)
        nb3, nlf3, u3, c3, wint3, ws3 = v3(nb), v3(nlf), v3(u), v3(c), v3(wint), v3(wsr)
        m0 = T("m0r", [4, 16], F32)
        mnw = T("mnw", [4, 16], F32)
        s.dma('sp', 'm0in', lambda e: e.dma_start(out=m0[:, :], in_=self.sm.rearrange("b h -> h b"), allow_slow_non_contiguous=True), writes=['m0'])
        rk = ('rows', 0)
        s.op('dve', lambda e: e.tensor_copy(nb3[:, :, 0:1], nlf3[:, :, 0:1]), reads=['nlf'], writes=[rk])
        for i in range(1, 4):
            s.op('dve', lambda e, i=i: e.tensor_tensor(out=nb3[:, :, i:i + 1], in0=nb3[:, :, i - 1:i], in1=nlf3[:, :, i:i + 1], op=ALU.add), reads=['nlf', rk], writes=[rk])
        s.op('dve', lambda e: e.tensor_tensor(out=u, in0=ig, in1=nb, op=ALU.add), reads=['ig', rk], writes=[rk])
        s.op('dve', lambda e: e.tensor_tensor(out=c3[:, :, 0:1], in0=u3[:, :, 0:1], in1=m0[:, :].unsqueeze(2), op=ALU.max), reads=[rk, 'm0'], writes=[rk])
        for i in range(1, 4):
            s.op('dve', lambda e, i=i: e.tensor_tensor(out=c3[:, :, i:i + 1], in0=c3[:, :, i - 1:i], in1=u3[:, :, i:i + 1], op=ALU.max), reads=[rk], writes=[rk])
        s.op('dve', lambda e: e.tensor_tensor(out=wint3, in0=bcast(m0[:, :].unsqueeze(2), [4, 16, 4]), in1=c3, op=ALU.subtract), reads=[rk, 'm0'], writes=[rk])
        s.op('act', lambda e: e.activation(out=wint, in_=wint, func=AF.Exp), reads=[rk], writes=[rk])
        s.op('dve', lambda e: e.tensor_tensor(out=emt, in0=nb, in1=c, op=ALU.subtract), reads=[rk], writes=[rk])
        s.op('act', lambda e: e.activation(out=emt, in_=emt, func=AF.Exp), reads=[rk], writes=[rk])
        s.op('dve', lambda e: e.tensor_tensor(out=ws3, in0=u3, in1=bcast(c3[:, :, 3:4], [4, 16, 4]), op=ALU.subtract), reads=[rk], writes=[rk])
        s.op('act', lambda e: e.activation(out=wsr, in_=wsr, func=AF.Exp), reads=[rk], writes=[rk])
        s.op('dve', lambda e: e.tensor_tensor(out=mnw[:, :].unsqueeze(2), in0=c3[:, :, 3:4], in1=nb3[:, :, 3:4], op=ALU.subtract), reads=[rk], writes=['mnw'])
        s.dma('sp', 'mout', lambda e: e.dma_start(out=self.msn.rearrange("b h -> h b"), in_=mnw[:, :], allow_slow_non_contiguous=True), reads=['mnw'])
        self.ml_cols_cbc(L, rk, self.cf('posmask_s'))
        sel = self.cf('sel').rearrange("p (h m) -> p h m", m=128)
        wcs_s = T("wcs_s", [128, 4, 16], F32)
        for h in range(4):
            s.op('pe', lambda e, h=h: e.matmul(self.banks[1][:, 32 + 16 * h:48 + 16 * h], sel[0:4, h, :], wint3[:, :, 3], start=True, stop=True), reads=[rk, 'cft'], writes=[self.bk[1]])
        s.op('act', lambda e: e.activation(out=wcs_s[:, :, :], in_=self.banks[1][:, 32:96].rearrange("p (h b) -> p h b", b=16), func=AF.Copy), reads=[self.bk[1]], writes=['wcs_s'])
        ident = self.cf('ident')
        n0 = T("n0", [64, 128], F32)
        n0T = T("n0T", [128, 64], F32)
        nnT = T("nnT", [128, 64], F32)
        nno = T("nno", [64, 128], F32)
        s.dma('sp', 'n0in', lambda e: e.dma_start(out=n0[:, :], in_=self.sn), writes=['n0'])
        s.op('pe', lambda e: e.transpose(self.banks[7][:, 0:64], n0[:, :], ident[0:64, 0:64]), reads=['n0', 'cft'], writes=[self.bk[7]])
        s.op('act', lambda e: e.activation(out=n0T[:, :], in_=self.banks[7][:, 0:64], func=AF.Copy), reads=[self.bk[7]], writes=['n0T'])
        C0 = [T("C0_%d" % i, [128, 16, 129], F32) for i in range(2)]
        C0b = T("C0b", [128, 16, 129], BF16)
        qpad = T("qpad", [128, 16, 64], BF16)
        kspad = T("kspad", [64, 16, 128], BF16)
        bmask = self.cbm('bmask').rearrange("p (b n) -> p b n", n=64)
        rmask = self.cbm('rmask')
        sCv = self.sC.rearrange("(b h k) v -> h k b v", h=4, k=128)
        Csv = self.Cs.rearrange("(b h k) v -> h k b v", h=4, k=128)
        for h in range(4):
            Ch = C0[h % 2]
            ck_ = ('C0', h % 2)
            s.dma('sp', 'c0in%d' % (h % 2), lambda e, h=h, Ch=Ch: e.dma_start(out=Ch[:, :, 0:128], in_=sCv[h]), writes=[ck_])
            s.op('pool', lambda e, h=h, Ch=Ch: e.tensor_copy(Ch[:, :, 128], n0T[:, :].rearrange("k (b h) -> k h b", h=4)[:, h, :]), reads=['n0T'], writes=[ck_])
            s.op('act', lambda e, Ch=Ch: e.activation(out=C0b[:, :, :], in_=Ch[:, :, :], func=AF.Copy), reads=[ck_], writes=['C0b'])
            self.ml_head_av(h, L, ci, cs_)
            s.op('dve', lambda e, h=h: e.tensor_tensor(out=qpad[:, :, :], in0=bcast(self.mqT[:, h, 0:64].unsqueeze(1), [128, 16, 64]), in1=bmask, op=ALU.mult),
                 reads=['mqT', 'cbt'], writes=['qpad'])
            for b in range(NSB):
                s.op('pe', lambda e, b=b: e.matmul(self.banks[5][0:L, 0:129], qpad[:, b, :], C0b[:, b, :], start=(b == 0), stop=(b == NSB - 1)),
                     reads=['qpad', 'C0b'], writes=[self.bk[5]])
            self.ml_head_out(h, L, ci, cs_)
            s.op('dve', lambda e, h=h: e.tensor_scalar(out=self.ks[0:L, :], in0=self.ktm[0:L, ci, h * 128:(h + 1) * 128], scalar1=self.cols[0:L, 12 + h:13 + h], scalar2=None, op0=ALU.mult),
                 reads=['ktm', 'cols'], writes=['ks'])
            s.op('dve', lambda e: e.tensor_tensor(out=kspad[:, :, :], in0=bcast(self.ks[0:64, :].unsqueeze(1), [64, 16, 128]), in1=bcast(rmask[0:64, :].unsqueeze(2), [64, 16, 128]), op=ALU.mult),
                 reads=['ks', 'cbt'], writes=['kspad'])
            for b in range(NSB):
                ub = 6 + (b % 2)
                s.op('pe', lambda e, b=b, ub=ub, h=h: e.matmul(self.banks[ub][:, 0:129], kspad[0:64, b, :], self.mva[0:64, ci, h, :], start=True, stop=True),
                     reads=['kspad', 'mva'], writes=[self.bk[ub]])
                s.op('dve', lambda e, b=b, ub=ub, h=h, Ch=Ch: e.scalar_tensor_tensor(out=Ch[:, b, :], in0=Ch[:, b, :], scalar=wcs_s[:, h, b:b + 1], in1=self.banks[ub][:, 0:129], op0=ALU.mult, op1=ALU.add),
                     reads=[self.bk[ub], 'wcs_s', 'C0b'], writes=[ck_])
            s.dma('sp', 'c0out%d' % (h % 2), lambda e, h=h, Ch=Ch: e.dma_start(out=Csv[h], in_=Ch[:, :, 0:128]), reads=[ck_])
            s.op('pool', lambda e, h=h, Ch=Ch: e.tensor_copy(nnT[:, :].rearrange("k (b h) -> k h b", h=4)[:, h, :], Ch[:, :, 128]), reads=[ck_], writes=['nnT'])
        s.op('pe', lambda e: e.transpose(self.banks[7][0:64, 0:128], nnT[:, :], ident[:, :]), reads=['nnT', 'cft'], writes=[self.bk[7]])
        s.op('act', lambda e: e.activation(out=nno[:, :], in_=self.banks[7][0:64, 0:128], func=AF.Copy), reads=[self.bk[7]], writes=['nno'])
        s.dma('sp', 'nout', lambda e: e.dma_start(out=self.nsn, in_=nno[:, :]), reads=['nno'])

    def ml_rows(self, ci):
        s = self.s
        L = 128
        R = self.rows
        cs_ = slice(ci * 128, ci * 128 + L)
        CR = self.crow2
        pb_ = ci % 2
        ig, nlf = R[0:4, 0, cs_], R[0:4, 1, cs_]
        nb, u, c, wint, emt, wsr = [CR[0:4, pb_, i, :] for i in range(6)]
        mprev, ncl = self.mrow[0:4, 0:1], self.mrow[0:4, 1:2]
        ones4 = self.cf('onesrow')[0:4, 0:L]
        rk = ('rows2', pb_)
        s.op('dve', lambda e: e.tensor_tensor_scan(nb, ones4, nlf, 0.0, ALU.mult, ALU.add), reads=['nlf', 'cft'], writes=[rk])
        s.op('dve', lambda e: e.tensor_tensor(out=u, in0=ig, in1=nb, op=ALU.add), reads=['ig', rk], writes=[rk])
        s.op('dve', lambda e: e.tensor_tensor_scan(c, u, u, mprev, ALU.max, ALU.max), reads=[rk, 'mprev'], writes=[rk])
        s.op('act', lambda e: e.activation(out=wint, in_=c, func=AF.Exp, scale=-1.0, bias=mprev), reads=[rk, 'mprev'], writes=[rk])
        s.op('dve', lambda e: e.tensor_tensor(out=emt, in0=nb, in1=c, op=ALU.subtract), reads=[rk], writes=[rk])
        s.op('act', lambda e: e.activation(out=emt, in_=emt, func=AF.Exp), reads=[rk], writes=[rk])
        s.op('dve', lambda e: e.tensor_scalar(out=ncl, in0=CR[0:4, pb_, 2, L - 1:L], scalar1=-1.0, scalar2=None, op0=ALU.mult), reads=[rk], writes=['ncl'])
        s.op('act', lambda e: e.activation(out=wsr, in_=u, func=AF.Exp, bias=ncl), reads=[rk, 'ncl'], writes=[rk])
        s.op('dve', lambda e: e.tensor_tensor(out=mprev, in0=CR[0:4, pb_, 2, L - 1:L], in1=CR[0:4, pb_, 0, L - 1:L], op=ALU.subtract), reads=[rk], writes=['mprev'])
        sel = self.cf('sel').rearrange("p (h m) -> p h m", m=128)
        s.op('dve', lambda e: e.tensor_tensor(out=self.U4[:, pb_, :, :], in0=bcast(u.unsqueeze(1), [4, 4, 128]), in1=sel[0:4, :, :], op=ALU.mult), reads=[rk, 'cft'], writes=[('U4', pb_)])

    def ml_front(self, ci):
        s = self.s
        L = 128
        pb_ = ci % 2
        CR = self.crow2
        nb, u, c, wint, emt, wsr = [CR[0:4, pb_, i, :] for i in range(6)]
        rk = ('rows2', pb_)
        cs_ = slice(ci * 128, ci * 128 + L)
        ident = self.cf('ident')
        identb = self.cbm('identb')
        posmb = self.cbm('posmb')
        negones = self.cf('negones')
        sel = self.cf('sel').rearrange("p (h m) -> p h m", m=128)
        A, B, C_, W = self.banks[7], self.banks[0], self.banks[1], self.banks[6]
        for i, src in enumerate([wint, emt, wsr]):
            s.op('pe', lambda e, i=i, src=src: e.transpose(A[:, 4 * i:4 * i + 4], src, ident[0:4, 0:4]), reads=[rk, 'cft'], writes=[self.bk[7]])
        for h in range(4):
            s.op('pe', lambda e, h=h: e.matmul(A[:, 12 + h:13 + h], sel[0:4, h, :], CR[0:4, pb_, 3, L - 1:L], start=True, stop=True), reads=[rk, 'cft'], writes=[self.bk[7]])
        s.op('act', lambda e: e.activation(out=self.cols2[:, pb_, 0:16], in_=A[:, 0:16], func=AF.Copy), reads=[self.bk[7]], writes=[('cols2', pb_)])
        for h in range(4):
            o = B[:, h * 128:(h + 1) * 128]
            s.op('pe', lambda e, h=h, o=o: e.matmul(o, sel[0:4, h, :], c, start=True, stop=False), reads=[rk, 'cft'], writes=[self.bk[0]])
            s.op('pe', lambda e, h=h, o=o: e.matmul(o, self.U4[:, pb_, h, :], negones[0:4, :], start=False, stop=False), reads=[('U4', pb_), 'cft'], writes=[self.bk[0]])
            s.op('pe', lambda e, h=h, o=o: e.matmul(o, identb, posmb, start=False, stop=True), reads=['cbt'], writes=[self.bk[0]])
        s.op('act', lambda e: e.activation(out=self.Ew4[:, :], in_=B[:, :], func=AF.Exp, scale=-1.0), reads=[self.bk[0]], writes=[('wk', 0)])
        for h in range(4):
            s.op('pe', lambda e, h=h: e.matmul(C_[:, h * 128:(h + 1) * 128], self.mkT[:, h, cs_], self.mqT[:, h, cs_], start=True, stop=True), reads=['mkT', 'mqT'], writes=[self.bk[1]])
        s.op('dve', lambda e: e.tensor_tensor(out=self.aT4[:, :, :], in0=C_[:, :].rearrange("p (h n) -> p h n", n=128), in1=self.Ew4[:, :].rearrange("p (h n) -> p h n", n=128), op=ALU.mult),
             reads=[self.bk[1], ('wk', 0)], writes=['aT4'])
        for h in range(4):
            s.op('pe', lambda e, h=h: e.matmul(W[:, h * 128:(h + 1) * 128], sel[0:4, h, :], wint, start=True, stop=True), reads=[rk, 'cft'], writes=[self.bk[6]])
        s.op('dve', lambda e: e.tensor_tensor(out=self.qs4[:, :, :], in0=self.mqT[:, :, cs_], in1=W[:, :].rearrange("p (h n) -> p h n", n=128), op=ALU.mult),
             reads=[self.bk[6], 'mqT'], writes=['qs4'])

    def ml_back_a(self, ci):
        s = self.s
        L = 128
        cs_ = slice(ci * 128, ci * 128 + L)
        ident = self.cf('ident')
        A, D_, E_, F_ = self.banks[7], self.banks[2], self.banks[3], self.banks[4]
        mlg = self.pbv('mlg')
        c2 = self.cols2[:, ci % 2, :]
        ck2 = ('cols2', ci % 2)
        ck2b = ('cols2b', ci % 2)
        for h in range(4):
            s.op('pe', lambda e, h=h: e.matmul(D_[:, h * 128:(h + 1) * 128], self.aT4[:, h, :], self.mva[:, ci, h, 0:128], start=True, stop=False), reads=['aT4', 'mva'], writes=[self.bk[2]])
            s.op('pe', lambda e, h=h: e.matmul(D_[:, h * 128:(h + 1) * 128], self.qs4[:, h, :], self.Cbf[:, h, 0:128], start=False, stop=True), reads=['qs4', 'Cbf'], writes=[self.bk[2]])
            s.op('pe', lambda e, h=h: e.matmul(A[:, 16 + h:17 + h], self.aT4[:, h, :], self.mva[:, ci, h, 128:129], start=True, stop=False), reads=['aT4', 'mva'], writes=[self.bk[7]])
            s.op('pe', lambda e, h=h: e.matmul(A[:, 16 + h:17 + h], self.qs4[:, h, :], self.Cbf[:, h, 128:129], start=False, stop=True), reads=['qs4', 'Cbf'], writes=[self.bk[7]])
        s.op('dve', lambda e: e.tensor_tensor(out=self.ks4[:, :, :], in0=self.ktm[:, ci, :].rearrange("p (h n) -> p h n", n=128), in1=bcast(c2[:, 8:12].unsqueeze(2), [128, 4, 128]), op=ALU.mult),
             reads=['ktm', ck2], writes=['ks4'])
        for h in range(4):
            s.op('pe', lambda e, h=h: e.matmul(E_[:, h * 128:(h + 1) * 128], self.ks4[:, h, :], self.mva[:, ci, h, 0:128], start=True, stop=True), reads=['ks4', 'mva'], writes=[self.bk[3]])
            s.op('pe', lambda e, h=h: e.matmul(A[:, 24 + h:25 + h], self.ks4[:, h, :], self.mva[:, ci, h, 128:129], start=True, stop=True), reads=['ks4', 'mva'], writes=[self.bk[7]])
        s.op('act', lambda e: e.activation(out=c2[:, 16:28], in_=A[:, 16:28], func=AF.Copy), reads=[self.bk[7]], writes=[ck2b])

    def ml_back_a2(self, ci):
        s = self.s
        L = 128
        cs_ = slice(ci * 128, ci * 128 + L)
        A, D_, E_, F_ = self.banks[7], self.banks[2], self.banks[3], self.banks[4]
        mlg = self.pbv('mlg')
        c2 = self.cols2[:, ci % 2, :]
        ck2 = ('cols2', ci % 2)
        ck2b = ('cols2b', ci % 2)
        nw = self.nw4
        sc = self.sc4
        s.op('act', lambda e: e.activation(out=nw[:, :], in_=D_[:, :], func=AF.Copy), reads=[self.bk[2]], writes=[('wk', 1)])
        s.op('dve', lambda e: e.scalar_tensor_tensor(out=sc[:, 0:4], in0=c2[:, 16:20], scalar=-1.0, in1=c2[:, 16:20], op0=ALU.mult, op1=ALU.max), reads=[ck2b], writes=['sc4'])
        s.op('dve', lambda e: e.tensor_tensor(out=sc[:, 0:4], in0=sc[:, 0:4], in1=c2[:, 4:8], op=ALU.max), reads=['sc4', ck2], writes=['sc4'])
        s.op('dve', lambda e: e.reciprocal(sc[:, 4:8], sc[:, 0:4]), reads=['sc4'], writes=['sc4'])
        v3 = lambda ap: ap.rearrange("p (h n) -> p h n", n=128)
        s.op('dve', lambda e: e.tensor_tensor(out=v3(self.t14[:, :]), in0=v3(nw[:, :]), in1=bcast(sc[:, 4:8].unsqueeze(2), [128, 4, 128]), op=ALU.mult), reads=[('wk', 1), 'sc4'], writes=[('wk', 2)])
        s.op('act', lambda e: e.activation(out=self.sq4[:, :], in_=self.t14[:, :], func=AF.Square), reads=[('wk', 2)], writes=[('wk', 4)])
        s.op('pool', lambda e: e.tensor_tensor(out=self.t24[:, :], in0=self.t14[:, :], in1=mlg, op=ALU.mult), reads=[('wk', 2), 'pbt'], writes=[('wk', 3)])
        so_ap = self.xs_[ci // 2][:, (ci % 2) * 512:(ci % 2 + 1) * 512]
        s.op('pool', lambda e: e.tensor_tensor(out=self.t24[:, :], in0=self.t24[:, :], in1=so_ap, op=ALU.mult), reads=[('wk', 3), 'so'], writes=[('wk', 3)])
        s.op('dve', lambda e: e.tensor_reduce(out=sc[:, 8:12], in_=v3(self.sq4[:, :]), axis=AX.X, op=ALU.add), reads=[('wk', 4)], writes=['sc4'])
        self.rstd(sc[:, 8:12], sc[:, 12:16], 1.0 / 128, ['sc4'], ['sc4b'])
        s.op('dve', lambda e: e.tensor_tensor(out=v3(self.t14[:, :]), in0=v3(self.t24[:, :]), in1=bcast(sc[:, 12:16].unsqueeze(2), [128, 4, 128]), op=ALU.mult), reads=[('wk', 3), 'sc4b'], writes=[('wk', 2)])

    def ml_back_b(self, ci):
        s = self.s
        L = 128
        cs_ = slice(ci * 128, ci * 128 + L)
        ident = self.cf('ident')
        A, D_, E_, F_ = self.banks[7], self.banks[2], self.banks[3], self.banks[4]
        c2 = self.cols2[:, ci % 2, :]
        ck2 = ('cols2', ci % 2)
        ck2b = ('cols2b', ci % 2)
        v3 = lambda ap: ap.rearrange("p (h n) -> p h n", n=128)
        for h in range(4):
            s.op('pe', lambda e, h=h: e.transpose(F_[:, h * 128:(h + 1) * 128], self.t14[:, h * 128:(h + 1) * 128], ident), reads=[('wk', 2), 'cft'], writes=[self.bk[4]])
        s.op('act', lambda e: e.activation(out=self.hmT[:, :, cs_], in_=v3(F_[:, :]), func=AF.Copy), reads=[self.bk[4]], writes=['hmT'])

    def ml_back_c(self, ci):
        s = self.s
        L = 128
        cs_ = slice(ci * 128, ci * 128 + L)
        ident = self.cf('ident')
        A, D_, E_, F_ = self.banks[7], self.banks[2], self.banks[3], self.banks[4]
        c2 = self.cols2[:, ci % 2, :]
        ck2 = ('cols2', ci % 2)
        ck2b = ('cols2b', ci % 2)
        v3 = lambda ap: ap.rearrange("p (h n) -> p h n", n=128)
        C3 = self.Caug[:, :, 0:128]
        wcb = bcast(c2[:, 12:16].unsqueeze(2), [128, 4, 128])
        s.op('dve', lambda e: e.tensor_tensor(out=C3, in0=C3, in1=wcb, op=ALU.mult), reads=['Caug', ck2], writes=['Caug'])
        s.op('dve', lambda e: e.tensor_tensor(out=C3, in0=C3, in1=v3(E_[:, :]), op=ALU.add), reads=['Caug', self.bk[3]], writes=['Caug'])
        n3 = self.Caug[:, :, 128]
        s.op('dve', lambda e: e.tensor_tensor(out=n3, in0=n3, in1=c2[:, 12:16], op=ALU.mult), reads=['Caug', ck2], writes=['Caug'])
        s.op('dve', lambda e: e.tensor_tensor(out=n3, in0=n3, in1=c2[:, 24:28], op=ALU.add), reads=['Caug', ck2b], writes=['Caug'])
        s.op('act', lambda e: e.activation(out=self.Cbf[:, :, :], in_=self.Caug[:, :, :], func=AF.Copy), reads=['Caug'], writes=['Cbf'])

    def mlstm_span_fast(self):
        self.ml_rows(0)
        self.ml_rows(1)
        self.ml_front(0)
        for ci in range(4):
            self.ml_back_a(ci)
            self.ml_back_c(ci)
            self.ml_back_a2(ci)
            if ci + 1 < 4:
                self.ml_front(ci + 1)
            self.ml_back_b(ci)
            if ci + 2 < 4:
                self.ml_rows(ci + 2)

    def w_out_phase(self, tiles):
        s = self.s
        for i in range(4):
            slot, sk = self.wslot()
            for a in range(2):
                h = 2 * i + a
                for ti, (t, tsz) in enumerate(tiles):
                    for hf in range(2):
                        b = ti * 2 + hf
                        s.op('pe', lambda e, slot=slot, a=a, h=h, t=t, tsz=tsz, hf=hf, b=b: e.matmul(
                            self.banks[b][0:tsz, :], self.attT[0:64, h, t * 128:t * 128 + tsz], slot[0:64, a * 1024 + hf * 512:a * 1024 + (hf + 1) * 512],
                            start=(h == 0), stop=False), reads=[sk, 'attT'], writes=[self.bk[b]])
        for i in range(2):
            slot, sk = self.wslot()
            for a in range(2):
                h = 2 * i + a
                for ti, (t, tsz) in enumerate(tiles):
                    for hf in range(2):
                        b = ti * 2 + hf
                        s.op('pe', lambda e, slot=slot, a=a, h=h, t=t, tsz=tsz, hf=hf, b=b: e.matmul(
                            self.banks[b][0:tsz, :], self.hmT[:, h, t * 128:t * 128 + tsz], slot[:, a * 1024 + hf * 512:a * 1024 + (hf + 1) * 512],
                            start=False, stop=(h == 3)), reads=[sk, 'hmT'], writes=[self.bk[b]])
        for ti, (t, tsz) in enumerate(tiles):
            for hf in range(2):
                b = ti * 2 + hf
                dst = self.x1[0:tsz, t, hf * 512:(hf + 1) * 512]
                s.op('dve', lambda e, dst=dst, b=b, tsz=tsz: e.tensor_tensor(out=dst, in0=self.banks[b][0:tsz, :], in1=dst, op=ALU.add),
                     reads=[self.bk[b]], writes=[('x1', t)])

    def prompt_span(self, q, n):
        s = self.s
        tiles = [(t, 128) for t in range(4)]
        ntok = 512
        row0 = q * S + n * 512
        x1keys = [('x1', t) for t in range(4)]
        if self.xpre == (q, n):
            self.x_in_wk = True
        else:
            self.x_in_wk = False
            s.dma('sp', 'xin', lambda e: e.dma_start(out=self.x1[:, :, :], in_=self.xp[row0:row0 + 512, :].rearrange("(t p) d -> p t d", p=128)), writes=x1keys)
        s.dma('sp', 'csin', lambda e: e.dma_start(out=self.cs[:, 0, :, :], in_=self.cosp[n * 512:(n + 1) * 512, :].rearrange("(t p) d -> p t d", p=128)), writes=['cs'])
        s.dma('sp', 'csin', lambda e: e.dma_start(out=self.cs[:, 1, :, :], in_=self.sinp[n * 512:(n + 1) * 512, :].rearrange("(t p) d -> p t d", p=128)), writes=['cs'])
        self.build_csg(4, 128)
        self.mark('start')
        self.norm_to_hT(tiles, 'g1', None)
        self.ffn(tiles, ntok)
        self.x_in_wk = False
        self.mark('ffn1')
        if 'x1a' in self.dbg and q == 0 and n == 0:
            s.dma('sp', 'dbg', lambda e: e.dma_start(out=self.dbg_out['x1a'].rearrange("(t p) d -> p t d", p=128), in_=self.x1[:, :, :]), reads=x1keys)
        if n == 0:
            s.op('pool', lambda e: e.memset(self.Caug[:], 0.0), writes=['Caug'])
            s.op('pool', lambda e: e.memset(self.Cbf[:], 0.0), writes=['Cbf'])
            s.op('pool', lambda e: e.memset(self.mrow[:], 0.0), writes=['mprev', 'ncl'])
        kout_fn = lambda t, tsz: self.kp[row0 + t * 128:row0 + t * 128 + tsz, :]
        vout_fn = lambda t, tsz: self.vp[row0 + t * 128:row0 + t * 128 + tsz, :]
        self.w_in_phase(tiles, ntok, q, n, kout_fn, vout_fn, n * 512)
        self.mark('w_in')
        self.attention(n)
        self.mark('attn')
        self.mlstm_span_fast()
        self.mark('mlstm')
        if n == 3:
            for h in range(4):
                r0 = (q * 4 + h) * 128
                s.dma('sp', 'cout', lambda e, h=h, r0=r0: e.dma_start(out=self.Cp[r0:r0 + 128, :], in_=self.Caug[:, h, 0:128]), reads=['Caug'])
            for h in range(4):
                s.dma('sp', 'cout', lambda e, h=h: e.dma_start(out=self.np_[q * 4 + h:q * 4 + h + 1, :].rearrange("a k -> k a"), in_=self.Caug[:, h, 128:129]), reads=['Caug'])
            s.dma('sp', 'cout', lambda e: e.dma_start(out=self.mp[q:q + 1, :].rearrange("a h -> h a"), in_=self.mrow[0:4, 0:1]), reads=['mprev'])
        self.w_out_phase(tiles)
        self.mark('w_out')
        nxt = (q, n + 1) if n + 1 < self.n_spans else ((q + 1, 0) if q + 1 < self.n_seq else None)
        if nxt is not None:
            nrow = nxt[0] * S + nxt[1] * 512
            s.dma('sp', 'xpre', lambda e: e.dma_start(out=self.wkt[:, 0:8, :].rearrange("p (t a) n -> p t (a n)", a=2), in_=self.xp[nrow:nrow + 512, :].rearrange("(t p) d -> p t d", p=128)),
                  writes=[('wk', i) for i in range(8)])
            self.xpre = nxt
        self.norm_to_hT(tiles, 'g2', None)
        self.ffn(tiles, ntok)
        self.mark('ffn2')
        s.dma('sp', 'yout', lambda e: e.dma_start(out=self.yp[row0:row0 + 512, :].rearrange("(t p) d -> p t d", p=128), in_=self.x1[:, :, :]), reads=x1keys)


def mask_view(PT, nk, c0):
    return PT[0:nk, c0:512]


_NC_CACHE = {}


def make_in_maps(inp, n_seq=NSEQ, do_sample=True, cores=NCORES):
    cf, cb = make_consts()
    cosp, sinp = rope_tables(np.arange(S))
    coss, sins = rope_tables(PAST + (np.arange(64) % 4))
    wall = pack_weights(inp).reshape(NSLOT * 128, 2048)
    pb = pack_params(inp)
    maps = []
    for c in range(cores):
        m = {"xp": np.ascontiguousarray(inp['x_prompt'][NSEQ * c:NSEQ * c + n_seq].reshape(n_seq * S, D)),
             "wall": wall, "pb": pb, "cf": cf, "cb": cb, "cosp": cosp, "sinp": sinp}
        if do_sample:
            b0, b1 = NSB * c, NSB * (c + 1)
            m.update({
                "xs": np.ascontiguousarray(inp['x_sample'][b0:b1].reshape(64, D)),
                "ck": np.ascontiguousarray(inp['cache_k_win'][0, b0:b1].reshape(NSB * WIN_BUF, 512)),
                "cv": np.ascontiguousarray(inp['cache_v_win'][0, b0:b1].reshape(NSB * WIN_BUF, 512)),
                "sC": np.ascontiguousarray(inp['state_C'][0, b0:b1].reshape(NSB * 4 * 128, 128)),
                "sn": np.ascontiguousarray(inp['state_n'][0, b0:b1].reshape(NSB * 4, 128)),
                "sm": np.ascontiguousarray(inp['state_m'][0, b0:b1].reshape(NSB, 4)),
                "coss": coss, "sins": sins,
            })
        maps.append(m)
    return maps


def kernel(**inputs):
    inp = {k: np.asarray(v) for k, v in inputs.items()}
    if 'nc' not in _NC_CACHE:
        _NC_CACHE['nc'] = MK().build()
    nc = _NC_CACHE['nc']
    maps = make_in_maps(inp)
    res = run_bass_kernel_spmd(nc, maps, core_ids=list(range(NCORES)))
    R = res.results
    f = lambda name: [np.asarray(r[name], dtype=np.float32) for r in R]
    yp = np.concatenate([a.reshape(NSEQ, S, D) for a in f('yp')], 0)
    ys = np.concatenate([a.reshape(NSB, 4, D) for a in f('ys')], 0)
    kp = np.concatenate([a.reshape(NSEQ, S, 8, 64) for a in f('kp')], 0)[None]
    vp = np.concatenate([a.reshape(NSEQ, S, 8, 64) for a in f('vp')], 0)[None]
    ks = np.concatenate([a.reshape(NSB, 4, 8, 64) for a in f('ksn')], 0)[None]
    vs = np.concatenate([a.reshape(NSB, 4, 8, 64) for a in f('vsn')], 0)[None]
    Cp = np.concatenate([a.reshape(NSEQ, 4, 128, 128) for a in f('Cp')], 0)[None]
    npp = np.concatenate([a.reshape(NSEQ, 4, 128) for a in f('np')], 0)[None]
    mp = np.concatenate([a.reshape(NSEQ, 4) for a in f('mp')], 0)[None]
    Cs = np.concatenate([a.reshape(NSB, 4, 128, 128) for a in f('Cs')], 0)[None]
    nss = np.concatenate([a.reshape(NSB, 4, 128) for a in f('nsn')], 0)[None]
    ms = np.concatenate([a.reshape(NSB, 4) for a in f('msn')], 0)[None]
    return (yp, ys, kp, vp, ks, vs, Cp, npp, mp, Cs, nss, ms)
```

```python
import contextlib
import numpy as np
import ml_dtypes
import concourse.bass as bass
import concourse.mybir as mybir
from concourse.bass_utils import run_bass_kernel_spmd

F32 = mybir.dt.float32
BF16 = mybir.dt.bfloat16
ALU = mybir.AluOpType
AF = mybir.ActivationFunctionType
AX = mybir.AxisListType

ENGS = ['pe', 'act', 'dve', 'pool', 'sp']
NCORES = 8
D = 1024
DFF = 2816
NJ = 22
S = 2048
NSEQ = 4
NSB = 16
NR = 4
NSLOT = 88
EPS = 1e-6
WIN_BUF = 2048
PAST = 8192
BIG = 30000.0


class Sched:
    def __init__(self, nc):
        self.nc = nc
        self.streams = {e: [] for e in ENGS}
        self.cnt = {}
        self.seen = {e: {} for e in ENGS}
        self.lastw = {}
        self.readers = {}
        self.semkeys = []
        for e in ENGS:
            self._semkey(e)

    def _semkey(self, k):
        if k not in self.cnt:
            self.cnt[k] = 0
            self.semkeys.append(k)

    def _need(self, eng, tok, waits, self_ok):
        if tok is None:
            return
        k, v = tok
        if k == eng and not self_ok:
            return
        if self.seen[eng].get(k, 0) >= v:
            return
        self.seen[eng][k] = v
        waits.append(tok)

    def _deps(self, eng, reads, writes):
        waits = []
        raw_self = eng in ('act', 'dve', 'pool')
        waw_self = eng == 'pool'
        for b in reads:
            self._need(eng, self.lastw.get(b), waits, raw_self)
        for b in writes:
            self._need(eng, self.lastw.get(b), waits, waw_self)
            for r in self.readers.get(b, ()):
                self._need(eng, r, waits, waw_self)
        return waits

    def _commit(self, tok, reads, writes):
        for b in reads:
            self.readers.setdefault(b, []).append(tok)
        for b in writes:
            self.lastw[b] = tok
            self.readers[b] = []

    def op(self, eng, fn, reads=(), writes=()):
        waits = self._deps(eng, reads, writes)
        self.cnt[eng] += 1
        tok = (eng, self.cnt[eng])
        self.streams[eng].append((waits, fn, eng, 1))
        self._commit(tok, reads, writes)
        return tok

    def dma(self, q, semkey, fn, reads=(), writes=()):
        self._semkey(semkey)
        waits = self._deps(q, reads, writes)
        self.cnt[semkey] += 16
        tok = (semkey, self.cnt[semkey])
        self.streams[q].append((waits, fn, semkey, 16))
        self._commit(tok, reads, writes)
        return tok

    def wait_all(self, eng, toks):
        waits = []
        for t in toks:
            self._need(eng, t, waits, True)
        self.streams[eng].append((waits, None, None, 0))

    def all_tokens(self):
        return [(k, self.cnt[k]) for k in self.semkeys if self.cnt[k] > 0]

    def barrier(self):
        toks = self.all_tokens()
        for e in ENGS:
            self.wait_all(e, toks)

    def build(self):
        nc = self.nc
        with contextlib.ExitStack() as st:
            sems = {}
            for i, k in enumerate(self.semkeys):
                if self.cnt[k] > 0:
                    sems[k] = st.enter_context(nc.semaphore("s%d" % i))
            block = st.enter_context(nc.Block())
            engobj = {'pe': 'tensor', 'act': 'scalar', 'dve': 'vector', 'pool': 'gpsimd', 'sp': 'sync'}

            def replay(name):
                def body(e):
                    for waits, fn, sk, inc in self.streams[name]:
                        for (k, v) in waits:
                            e.wait_ge(sems[k], v)
                        if fn is not None:
                            fn(e).then_inc(sems[sk], inc)
                return body
            for name in ENGS:
                if self.streams[name]:
                    getattr(block, engobj[name])(replay(name))


def bcast(ap, shape):
    return ap.to_broadcast(list(shape))


CF = {}
_off = 0
for _n, _w in [('ident', 128), ('posmask', 128), ('onesrow', 128), ('sel', 512), ('onescol', 64), ('posmask_s', 64), ('negones', 128)]:
    CF[_n] = (_off, _w)
    _off += _w
NCF = _off
CB = {}
_off = 0
for _n, _w in [('mprev', 128), ('mcur', 128), ('m16', 128), ('mA', 4), ('mnew64', 64), ('identb', 128), ('onesb', 1), ('rmask', 16), ('bmask', 1024), ('posmb', 128)]:
    CB[_n] = (_off, _w)
    _off += _w
NCB = _off


def make_consts():
    cf = np.zeros((128, NCF), np.float32)
    o, w = CF['ident']
    cf[:, o:o + w] = np.eye(128, dtype=np.float32)
    o, w = CF['posmask']
    s = np.arange(128)[:, None]
    t = np.arange(128)[None, :]
    cf[:, o:o + w] = np.where(s > t, BIG, 0.0)
    o, w = CF['onesrow']
    cf[:, o:o + w] = 1.0
    o, w = CF['sel']
    sel = np.zeros((128, 4, 128), np.float32)
    for h in range(4):
        sel[h, h, :] = 1.0
    cf[:, o:o + w] = sel.reshape(128, 512)
    o, w = CF['onescol']
    cf[:, o:o + w] = 1.0
    cb = np.zeros((128, NCB), np.float32)
    j = np.arange(128)[:, None]
    i = np.arange(128)[None, :]
    o, w = CB['mprev']
    cb[:, o:o + w] = (j >= i)
    o, w = CB['mcur']
    cb[:, o:o + w] = (j <= i)
    o, w = CB['m16']
    m16 = np.zeros((128, 4, 32), np.float32)
    for n in range(4):
        ip = np.arange(128)[:, None]
        il = np.arange(32)[None, :]
        m16[:, n, :] = (ip <= 32 * n + il) & (ip < 32 * (n + 1))
    cb[:, o:o + w] = m16.reshape(128, 128)
    o, w = CB['mA']
    cb[:, o:o + w] = (np.arange(128)[:, None] >= np.arange(4)[None, :])
    o, w = CB['mnew64']
    mn = np.zeros((128, 64), np.float32)
    for r in range(64):
        for c in range(64):
            if r // 4 == c // 4:
                tp, tq = r % 4, c % 4
                mn[r, c] = 3.0 if tp == tq else (1.0 if tp < tq else 0.0)
    cb[:, o:o + w] = mn
    o, w = CB['identb']
    cb[:, o:o + w] = np.eye(128)
    o, w = CB['onesb']
    cb[:, o:o + w] = 1.0
    o, w = CB['rmask']
    cb[:, o:o + w] = (np.arange(128)[:, None] // 4 == np.arange(16)[None, :])
    o, w = CB['bmask']
    bm = (np.arange(64)[None, :] // 4 == np.arange(16)[:, None]).astype(np.float32)
    cb[:, o:o + w] = np.broadcast_to(bm.reshape(1, 1024), (128, 1024))
    o, w = CF['negones']
    cf[:, o:o + w] = -1.0
    o, w = CB['posmb']
    cb[:, o:o + w] = np.where(np.arange(128)[:, None] > np.arange(128)[None, :], BIG, 0.0)
    o, w = CF['posmask_s']
    ss_ = np.arange(64)[:, None]
    tt_ = np.arange(64)[None, :]
    pm = np.where((ss_ // 4 == tt_ // 4) & (ss_ <= tt_), 0.0, BIG)
    cf[0:64, o:o + w] = pm
    return cf, cb.astype(ml_dtypes.bfloat16)


def rope_tables(pos):
    half = 32
    inv = (np.float32(10000.0) ** (-np.arange(half, dtype=np.float32) / np.float32(half))).astype(np.float32)
    ang = pos.astype(np.float32)[:, None] * inv[None, :]
    return np.cos(ang).astype(np.float32), np.sin(ang).astype(np.float32)


def pack_weights(inp):
    wall = np.zeros((NSLOT, 128, 2048), np.float32)

    def ffn(base, wg, wu, wd):
        g = wg.reshape(8, 128, NJ, 128).transpose(2, 1, 0, 3).reshape(NJ, 128, 1024)
        u = wu.reshape(8, 128, NJ, 128).transpose(2, 1, 0, 3).reshape(NJ, 128, 1024)
        wall[base:base + NJ, :, 0:1024] = g
        wall[base:base + NJ, :, 1024:2048] = u
        d = wd.reshape(11, 2, 128, 1024).transpose(0, 2, 1, 3).reshape(11, 128, 2048)
        wall[base + NJ:base + NJ + 11] = d

    ffn(0, inp['ffn1_w_gate'][0], inp['ffn1_w_up'][0], inp['ffn1_w_down'][0])
    ffn(55, inp['ffn2_w_gate'][0], inp['ffn2_w_up'][0], inp['ffn2_w_down'][0])
    w_in = inp['w_in'][0]
    for gi, c0 in enumerate([0, 512, 1024, 2048, 2560, 3072]):
        w = w_in[:, c0:c0 + 512].reshape(2, 4, 128, 512).transpose(0, 2, 1, 3).reshape(2, 128, 2048)
        wall[33 + 2 * gi:35 + 2 * gi] = w
    for i in range(4):
        for u in range(2):
            head = (i % 2) * 2 + u
            c0 = (1536 if i < 2 else 2048) + head * 128
            w = w_in[:, c0:c0 + 128].reshape(8, 128, 128).transpose(1, 0, 2).reshape(128, 1024)
            wall[45 + i, :, u * 1024:(u + 1) * 1024] = w
    w_out = inp['w_out'][0]
    for i in range(4):
        for a in range(2):
            h = 2 * i + a
            wall[49 + i, 0:64, a * 1024:(a + 1) * 1024] = w_out[h * 64:(h + 1) * 64, :]
    for i in range(2):
        for a in range(2):
            h = 2 * i + a
            wall[53 + i, :, a * 1024:(a + 1) * 1024] = w_out[512 + h * 128:512 + (h + 1) * 128, :]
    return wall


PB = {}
_off = 0
for _n, _w in [('g1', 8), ('gm', 8), ('g2', 8), ('qg', 64), ('kg', 64), ('mlg', 512), ('bi', 1), ('nbf', 1), ('wgate', 64)]:
    PB[_n] = (_off, _w)
    _off += _w
NPB = _off


def pack_params(inp):
    pb = np.zeros((128, NPB), np.float32)

    def put(name, arr):
        o, w = PB[name]
        pb[:, o:o + w] = arr
    put('g1', inp['ffn1_norm'][0].reshape(8, 128).T)
    put('gm', inp['mix_norm'][0].reshape(8, 128).T)
    put('g2', inp['ffn2_norm'][0].reshape(8, 128).T)
    put('qg', np.broadcast_to(inp['q_norm'][0][None, :], (128, 64)))
    put('kg', np.broadcast_to(inp['k_norm'][0][None, :], (128, 64)))
    put('mlg', np.broadcast_to(inp['ml_out_norm'][0].reshape(1, 512), (128, 512)))
    bi = np.zeros((128, 1), np.float32)
    bi[0:4, 0] = inp['b_igate'][0]
    put('bi', bi)
    nbf = np.zeros((128, 1), np.float32)
    nbf[0:4, 0] = inp['b_fgate'][0]
    put('nbf', nbf)
    wg = inp['w_in'][0][:, 3584:3592].reshape(8, 128, 8).transpose(1, 0, 2).reshape(128, 64)
    put('wgate', wg)
    return pb


class MK:
    def __init__(self, n_seq=NSEQ, do_sample=True, dbg=None, n_spans=4, sample_parts=('pre', 'attn', 'ml')):
        self.sample_parts = sample_parts
        self.n_seq = n_seq
        self.n_spans = n_spans
        self.do_sample = do_sample
        self.dbg = dbg or {}
        self.nc = bass.Bass("TRN2", target_bir_lowering=False)
        self.s = Sched(self.nc)
        self.wuse = 0
        self.wloaded = 0
        self.nr = NR
        self.total_loads = NSLOT * (n_seq * n_spans + (1 if do_sample else 0))

    def mark(self, name):
        if not hasattr(self, 'marks'):
            self.marks = []
        self.marks.append((name, dict(self.s.cnt)))

    def dram(self, name, shape, dt, kind):
        return self.nc.dram_tensor(name, list(shape), dt, kind=kind).ap()

    def declare(self):
        I, O = "ExternalInput", "ExternalOutput"
        d = self.dram
        ns = self.n_seq
        self.xp = d("xp", [ns * S, D], F32, I)
        self.wall = d("wall", [NSLOT * 128, 2048], F32, I)
        self.pb_d = d("pb", [128, NPB], F32, I)
        self.cf_d = d("cf", [128, NCF], F32, I)
        self.cb_d = d("cb", [128, NCB], BF16, I)
        self.cosp = d("cosp", [S, 32], F32, I)
        self.sinp = d("sinp", [S, 32], F32, I)
        self.yp = d("yp", [ns * S, D], F32, O)
        self.kp = d("kp", [ns * S, 512], F32, O)
        self.vp = d("vp", [ns * S, 512], F32, O)
        self.Cp = d("Cp", [ns * 4 * 128, 128], F32, O)
        self.np_ = d("np", [ns * 4, 128], F32, O)
        self.mp = d("mp", [ns, 4], F32, O)
        self.ws = d("ws", [NSLOT * 128, 2048], BF16, "Internal")
        self.vscr = d("vscr", [512, 520], BF16, "Internal")
        if self.do_sample:
            self.xs = d("xs", [64, D], F32, I)
            self.ck = d("ck", [NSB * WIN_BUF, 512], F32, I)
            self.cv = d("cv", [NSB * WIN_BUF, 512], F32, I)
            self.sC = d("sC", [NSB * 4 * 128, 128], F32, I)
            self.sn = d("sn", [NSB * 4, 128], F32, I)
            self.sm = d("sm", [NSB, 4], F32, I)
            self.coss = d("coss", [64, 32], F32, I)
            self.sins = d("sins", [64, 32], F32, I)
            self.ys = d("ys", [64, D], F32, O)
            self.ksn = d("ksn", [64, 512], F32, O)
            self.vsn = d("vsn", [64, 512], F32, O)
            self.Cs = d("Cs", [NSB * 4 * 128, 128], F32, O)
            self.nsn = d("nsn", [NSB * 4, 128], F32, O)
            self.msn = d("msn", [NSB, 4], F32, O)
        self.dbg_out = {}
        for name, shape in self.dbg.items():
            self.dbg_out[name] = d("dbg_" + name, shape, F32, O)

    def emit_load(self, g):
        k = g % NSLOT
        r = g % self.nr
        src = self.ws[k * 128:(k + 1) * 128, :]
        dst = self.ring[r]
        self.s.dma('sp', 'ring%d' % r, lambda e: e.dma_start(out=dst[:], in_=src),
                   reads=[('ws', k // 11)], writes=[('ring', r)])

    def wslot(self):
        g = self.wuse
        self.wuse += 1
        while self.wloaded < min(g + self.nr, self.total_loads):
            self.emit_load(self.wloaded)
            self.wloaded += 1
        r = g % self.nr
        return self.ring[r], ('ring', r)

    def cf(self, name):
        o, w = CF[name]
        return self.cft[:, o:o + w]

    def cbm(self, name):
        o, w = CB[name]
        return self.cbt[:, o:o + w]

    def pbv(self, name):
        o, w = PB[name]
        return self.pbt[:, o:o + w]

    def build(self):
        nc = self.nc
        self.declare()
        with contextlib.ExitStack() as st:
            T = lambda name, shape, dt: st.enter_context(nc.sbuf_tensor(name, list(shape), dt))
            self.banks = [st.enter_context(nc.psum_tensor("bank%d" % i, [128, 512], F32)) for i in range(8)]
            self.bk = [('bank', i) for i in range(8)]
            self.cft = T("cft", [128, NCF], F32)
            self.cbt = T("cbt", [128, NCB], BF16)
            self.pbt = T("pbt", [128, NPB], F32)
            self.wgb = T("wgb", [128, 64], BF16)
            self.epsc = T("epsc", [128, 1], F32)
            self.ring = [T("ring%d" % i, [128, 2048], BF16) for i in range(NR)]
            self.x1 = T("x1", [128, 4, D], F32)
            self.xs_ = [T("xs%d" % i, [128, D], F32) for i in range(2)]
            self.hT = T("hT", [128, 8, 512], BF16)
            self.arena = T("arena", [128, NJ * 512], BF16)
            self.sg = [T("sg%d" % i, [128, 512], BF16) for i in range(2)]
            self.sm_ = T("small", [128, 64], F32)
            self.ssh_ = [T("ssh%d" % i, [128, 16], F32) for i in range(4)]
            self.cs = T("cs", [128, 2, 4, 32], F32)
            self.csg = T("csg", [128, 2, 4, 4, 32], F32)
            self.attT = T("attT", [65, 8, 512], BF16)
            self.hmT = T("hmT", [128, 4, 512], BF16)
            self.PT = T("PT", [128, 512], BF16)
            self.accw = [T("accw%d" % i, [65, 512], F32) for i in range(2)]
            self.rows = T("rows", [4, 2, 512], F32)
            self.mrow = T("mrow", [4, 4], F32)
            self.Caug = T("Caug", [128, 4, 129], F32)
            self.Cbf = T("Cbf", [128, 4, 129], BF16)
            A = self.arena
            self.actT = A[:, :].rearrange("p (j n) -> p j n", n=512)
            self.QT = A[:, 0:2048].rearrange("p (a n) -> p a n", n=512)
            self.mqT = A[:, 2048:4096].rearrange("p (a n) -> p a n", n=512)
            self.mkT = A[:, 4096:6144].rearrange("p (a n) -> p a n", n=512)
            self.ktm = A[:, 6144:8192].rearrange("p (t n) -> p t n", n=512)
            self.mva = A[:, 8192:8192 + 2064].rearrange("p (t h c) -> p t h c", t=4, h=4)

            s = self.s
            s.dma('sp', 'cst0', lambda e: e.dma_start(out=self.cft[:], in_=self.cf_d), writes=['cft'])
            s.dma('sp', 'cst1', lambda e: e.dma_start(out=self.cbt[:], in_=self.cb_d), writes=['cbt'])
            s.dma('sp', 'cst2', lambda e: e.dma_start(out=self.pbt[:], in_=self.pb_d), writes=['pbt'])
            for gidx in range(8):
                src = self.wall[gidx * 11 * 128:(gidx + 1) * 11 * 128, :]
                dst = self.ws[gidx * 11 * 128:(gidx + 1) * 11 * 128, :]
                s.dma('pool', 'cv%d' % gidx, lambda e, src=src, dst=dst: e.dma_start(out=dst, in_=src), writes=[('ws', gidx)])
            s.op('pool', lambda e: e.memset(self.epsc[:], EPS), writes=['epsc'])
            s.op('dve', lambda e: e.tensor_copy(self.wgb[:], self.pbv('wgate')), reads=['pbt'], writes=['wgb'])
            self.total_loads = NSLOT * self.n_seq * self.n_spans
            with contextlib.ExitStack() as st3:
                T3 = lambda name, shape, dt: st3.enter_context(nc.sbuf_tensor(name, list(shape), dt))
                self.KT = T3("KT", [128, 4, S], BF16)
                self.ring = self.ring[:NR] + [T3("ring%d" % NR, [128, 2048], BF16)]
                self.nr = NR + 1
                self.Vnat = T3("Vnat", [128, 8, 520], BF16)
                self.V4 = T3("V4", [128, 8, 520], BF16)
                self.V16 = T3("V16", [128, 16, 520], BF16)
                self.PTs = [self.PT] + [T3("PTx%d" % i, [128, 512], BF16) for i in range(4)]
                self.wkt = T3("wkbig", [128, 12, 512], F32)
                self.wk = [self.wkt[:, i, :] for i in range(12)]
                self.xpre = None
                self.crow2 = T3("crow2", [4, 2, 6, 128], F32)
                self.U4 = T3("U4", [4, 2, 4, 128], F32)
                self.cols2 = T3("cols2", [128, 2, 32], F32)
                self.aT4 = T3("aT4", [128, 4, 128], BF16)
                self.qs4 = T3("qs4", [128, 4, 128], BF16)
                self.ks4 = T3("ks4", [128, 4, 128], BF16)
                self.sc4 = T3("sc4", [128, 16], F32)
                self.Ew4, self.nw4, self.t14, self.t24, self.sq4 = self.wk[0], self.wk[1], self.wk[2], self.wk[3], self.wk[4]
                if self.n_seq * self.n_spans > 0:
                    s.op('pool', lambda e: e.memset(self.Vnat[:], 1.0), writes=['Vnat'])
                for q in range(self.n_seq):
                    for n in range(self.n_spans):
                        self.prompt_span(q, n)
                s.barrier()
                s.build()

            self.s = s = Sched(nc)
            self.wuse = 0
            self.wloaded = 0
            self.total_loads = NSLOT
            self.x_in_wk = False
            self.ring = self.ring[:NR]
            self.nr = NR
            if self.do_sample:
                with contextlib.ExitStack() as st2:
                    T2 = lambda name, shape, dt: st2.enter_context(nc.sbuf_tensor(name, list(shape), dt))
                    self.wk = [T2("wks%d" % i, [128, 512], F32) for i in range(6)]
                    self.sample_phase(T2)
                    s.wait_all('sp', s.all_tokens())
                    s.build()
        return nc

    def norm_to_hT(self, tiles, gname, srckeys):
        s = self.s
        ident = self.cf('ident')
        gain = self.pbv(gname)
        for ti, (t, tsz) in enumerate(tiles):
            xs = self.xs_[ti % 2]
            xk = ('xs', ti % 2)
            ss = self.sm_[:, 0 + ti:1 + ti]
            rs = self.sm_[:, 4 + ti:5 + ti]
            xsrc, xkeys = self.get_xsrc(t)
            src = xsrc[0:tsz, :]
            s.op('act', lambda e, xs=xs, src=src, ss=ss, tsz=tsz: e.activation(out=xs[0:tsz, :], in_=src, func=AF.Square, accum_out=ss[0:tsz, :]),
                 reads=xkeys, writes=[xk, ('ss', ti), 'so'])
            self.rstd(ss[0:tsz, :], rs[0:tsz, :], 1.0 / D, [('ss', ti)], [('rs', ti)])
            s.op('act', lambda e, xs=xs, src=src, rs=rs, tsz=tsz: e.activation(out=xs[0:tsz, :], in_=src, func=AF.Copy, scale=rs[0:tsz, :]),
                 reads=xkeys + [('rs', ti)], writes=[xk])
            for half in range(2):
                b = (2 * ti + half) % 8
                bank = self.banks[b]
                for c in range(4):
                    kc = half * 4 + c
                    s.op('pe', lambda e, bank=bank, c=c, kc=kc, xs=xs, tsz=tsz: e.transpose(bank[:, c * 128:c * 128 + tsz], xs[0:tsz, kc * 128:(kc + 1) * 128], ident[0:tsz, 0:tsz]),
                         reads=[xk, 'cft'], writes=[self.bk[b]])
                dst = self.hT[:, half * 4:half * 4 + 4, t * 128:t * 128 + tsz]
                srcp = bank[:, :].rearrange("p (c n) -> p c n", n=128)[:, :, 0:tsz]
                g = bcast(gain[:, half * 4:half * 4 + 4].unsqueeze(2), [128, 4, tsz])
                s.op('dve', lambda e, dst=dst, srcp=srcp, g=g: e.tensor_tensor(out=dst, in0=srcp, in1=g, op=ALU.mult),
                     reads=[self.bk[b], 'pbt'], writes=[('hT', t)])

    def build_csg(self, ntile, tsz):
        s = self.s
        for which, nm in enumerate(['qg', 'kg']):
            g = self.pbv(nm)
            for kind, (tab, half) in enumerate([(0, 0), (1, 1), (1, 0), (0, 1)]):
                gb = bcast(g[0:tsz, half * 32:(half + 1) * 32].unsqueeze(1), [tsz, ntile, 32])
                s.op('dve', lambda e, which=which, kind=kind, tab=tab, gb=gb: e.tensor_tensor(out=self.csg[0:tsz, which, kind, 0:ntile, :], in0=self.cs[0:tsz, tab, 0:ntile, :], in1=gb, op=ALU.mult),
                     reads=['cs', 'pbt'], writes=['csg'])

    def get_xsrc(self, t):
        if getattr(self, 'x_in_wk', False):
            return self.wkt[:, 2 * t:2 * t + 2, :].rearrange("p a n -> p (a n)"), [('wk', 2 * t), ('wk', 2 * t + 1)]
        return self.x1[:, t, :], [('x1', t)]

    def rstd(self, ss, rs, inv_n, rk, wk_):
        s = self.s
        s.op('act', lambda e: e.activation(out=rs, in_=ss, func=AF.Ln, scale=inv_n, bias=self.epsc[0:ss.shape[0], :]), reads=list(rk) + ['epsc'], writes=list(wk_))
        s.op('act', lambda e: e.activation(out=rs, in_=rs, func=AF.Exp, scale=-0.5), reads=list(wk_), writes=list(wk_))

    def ffn(self, tiles, ntok, final_out=None):
        s = self.s
        nt = len(tiles)
        hk = [('hT', t) for t, _ in tiles]
        for j in range(NJ):
            slot, sk = self.wslot()
            bg, bu = j % 2, 2 + j % 2
            for kc in range(8):
                s.op('pe', lambda e, slot=slot, kc=kc, bg=bg: e.matmul(self.banks[bg][:, 0:ntok], slot[:, kc * 128:(kc + 1) * 128], self.hT[:, kc, 0:ntok], start=(kc == 0), stop=(kc == 7)),
                     reads=[sk] + hk, writes=[self.bk[bg]])
            for kc in range(8):
                s.op('pe', lambda e, slot=slot, kc=kc, bu=bu: e.matmul(self.banks[bu][:, 0:ntok], slot[:, 1024 + kc * 128:1024 + (kc + 1) * 128], self.hT[:, kc, 0:ntok], start=(kc == 0), stop=(kc == 7)),
                     reads=[sk] + hk, writes=[self.bk[bu]])
            sg = self.sg[j % 2]
            s.op('act', lambda e, sg=sg, bg=bg: e.activation(out=sg[:, 0:ntok], in_=self.banks[bg][:, 0:ntok], func=AF.Silu),
                 reads=[self.bk[bg]], writes=[('sg', j % 2)])
            s.op('dve', lambda e, sg=sg, bu=bu, j=j: e.tensor_tensor(out=self.actT[:, j, 0:ntok], in0=self.banks[bu][:, 0:ntok], in1=sg[:, 0:ntok], op=ALU.mult),
                 reads=[self.bk[bu], ('sg', j % 2)], writes=[('actT', j)] + self.alias_keys(j))
        for jj in range(11):
            slot, sk = self.wslot()
            for a in range(2):
                j = 2 * jj + a
                for ti, (t, tsz) in enumerate(tiles):
                    for hf in range(2):
                        b = ti * 2 + hf
                        s.op('pe', lambda e, slot=slot, a=a, j=j, t=t, tsz=tsz, hf=hf, b=b: e.matmul(
                            self.banks[b][0:tsz, :], self.actT[:, j, t * 128:t * 128 + tsz], slot[:, a * 1024 + hf * 512:a * 1024 + (hf + 1) * 512],
                            start=(j == 0), stop=(j == NJ - 1)),
                            reads=[sk, ('actT', j)], writes=[self.bk[b]])
        for ti, (t, tsz) in enumerate(tiles):
            for hf in range(2):
                b = ti * 2 + hf
                dst = self.x1[0:tsz, t, hf * 512:(hf + 1) * 512]
                xsrc, xkeys = self.get_xsrc(t)
                res = xsrc[0:tsz, hf * 512:(hf + 1) * 512]
                s.op('dve', lambda e, dst=dst, res=res, b=b, tsz=tsz: e.scalar_tensor_tensor(out=dst, in0=self.banks[b][0:tsz, :], scalar=0.5, in1=res, op0=ALU.mult, op1=ALU.add),
                     reads=[self.bk[b]] + xkeys, writes=[('x1', t)])

    def alias_keys(self, j):
        if j < 4:
            return ['QT']
        if j < 8:
            return ['mqT']
        if j < 12:
            return ['mkT']
        if j < 16:
            return ['ktm']
        if j < 21:
            return ['mva']
        return []

    def tm_matmuls(self, tiles, bank0):
        s = self.s
        for half in range(2):
            slot, sk = self.wslot()
            for ti, (t, tsz) in enumerate(tiles):
                b = bank0 + ti
                for c in range(4):
                    kc = half * 4 + c
                    s.op('pe', lambda e, slot=slot, c=c, kc=kc, t=t, tsz=tsz, b=b, half=half: e.matmul(
                        self.banks[b][0:tsz, :], self.hT[:, kc, t * 128:t * 128 + tsz], slot[:, c * 512:(c + 1) * 512],
                        start=(half == 0 and c == 0), stop=(half == 1 and c == 3)),
                        reads=[sk, ('hT', t)], writes=[self.bk[b]])

    def tm_group(self, tiles, bank0, evac):
        self.tm_matmuls(tiles, bank0)
        for ti, (t, tsz) in enumerate(tiles):
            evac(ti, t, tsz, bank0 + ti)

    def qk_evac_a(self, which, ti, t, tsz, b, kout):
        s = self.s
        bank = self.banks[b]
        nw_ = len(self.wk)
        i_sq, i_zn, i_kr = (ti, 4 + ti, 8 + ti) if nw_ >= 12 else (0, 1, 2)
        sq, zn, kr = self.wk[i_sq], self.wk[i_zn], self.wk[i_kr]
        ksq, kzn, krk = ('wk', i_sq), ('wk', i_zn), ('wk', i_kr)
        ssh = self.sm_[:, 16 + 8 * (ti % 2):24 + 8 * (ti % 2)] if False else self.ssh_[ti][:, 0:8]
        rsh = self.ssh_[ti][:, 8:16]
        sk_, rk_ = ('ssh', ti), ('rsh', ti)
        s.op('act', lambda e: e.activation(out=sq[0:tsz, :], in_=bank[0:tsz, :], func=AF.Square), reads=[self.bk[b]], writes=[ksq])
        yield
        s.op('dve', lambda e: e.tensor_reduce(out=ssh[0:tsz, :], in_=sq[0:tsz, :].rearrange("p (h d) -> p h d", d=64), axis=AX.X, op=ALU.add),
             reads=[ksq], writes=[sk_])
        yield
        s.op('act', lambda e: e.activation(out=rsh[0:tsz, :], in_=ssh[0:tsz, :], func=AF.Ln, scale=1.0 / 64, bias=self.epsc[0:tsz, :]), reads=[sk_, 'epsc'], writes=[rk_])
        yield
        s.op('act', lambda e: e.activation(out=rsh[0:tsz, :], in_=rsh[0:tsz, :], func=AF.Exp, scale=-0.5), reads=[rk_], writes=[rk_])
        yield
        b3 = bank[0:tsz, :].rearrange("p (h d) -> p h d", d=64)
        z3 = zn[0:tsz, :].rearrange("p (h d) -> p h d", d=64)
        s.op('dve', lambda e: e.tensor_tensor(out=z3, in0=b3, in1=bcast(rsh[0:tsz, :].unsqueeze(2), [tsz, 8, 64]), op=ALU.mult),
             reads=[self.bk[b], rk_], writes=[kzn])
        yield
        x1v, x2v = z3[:, :, 0:32], z3[:, :, 32:64]
        k3 = kr[0:tsz, :].rearrange("p (h d) -> p h d", d=64)
        o1, o2 = k3[:, :, 0:32], k3[:, :, 32:64]
        tabs = [bcast(self.csg[0:tsz, which, kind, ti, :].unsqueeze(1), [tsz, 8, 32]) for kind in range(4)]
        t3 = sq[0:tsz, :].rearrange("p (h d) -> p h d", d=64)
        ta, tb = t3[:, :, 0:32], t3[:, :, 32:64]
        s.op('dve', lambda e: e.tensor_tensor(out=o1, in0=x1v, in1=tabs[0], op=ALU.mult), reads=[kzn, 'csg'], writes=[krk])
        s.op('pool', lambda e: e.tensor_tensor(out=ta, in0=x2v, in1=tabs[1], op=ALU.mult), reads=[kzn, 'csg'], writes=[ksq])
        yield
        s.op('dve', lambda e: e.tensor_tensor(out=o2, in0=x2v, in1=tabs[3], op=ALU.mult), reads=[kzn, 'csg'], writes=[krk])
        s.op('pool', lambda e: e.tensor_tensor(out=tb, in0=x1v, in1=tabs[2], op=ALU.mult), reads=[kzn, 'csg'], writes=[ksq])
        yield
        s.op('dve', lambda e: e.tensor_tensor(out=o1, in0=o1, in1=ta, op=ALU.subtract), reads=[ksq, krk], writes=[krk])
        yield
        s.op('dve', lambda e: e.tensor_tensor(out=o2, in0=o2, in1=tb, op=ALU.add), reads=[ksq, krk], writes=[krk])
        if kout is not None:
            self.deferred.append(('kout%d' % ti, kout, kr[0:tsz, :], [krk]))
        yield

    def qk_evac_b(self, ti, t, tsz, b, dst_T, dst_key, col0):
        s = self.s
        nw_ = len(self.wk)
        i_kr = 8 + ti if nw_ >= 12 else 2
        kr, krk = self.wk[i_kr], ('wk', i_kr)
        ident = self.cf('ident')
        tbank = self.banks[b]
        for p in range(4):
            s.op('pe', lambda e, p=p: e.transpose(tbank[:, p * 128:p * 128 + tsz], kr[0:tsz, p * 128:(p + 1) * 128], ident[0:tsz, 0:tsz]),
                 reads=[krk, 'cft'], writes=[self.bk[b]])
        dst = dst_T[:, :, col0:col0 + tsz]
        s.op('act', lambda e: e.activation(out=dst, in_=tbank[:, :].rearrange("p (a n) -> p a n", n=128)[:, :, 0:tsz], func=AF.Copy),
             reads=[self.bk[b]], writes=[dst_key] + ([('actT', j) for j in range(4)] if dst_key == 'QT' else []))

    def run_gens(self, gens):
        gens = list(gens)
        while gens:
            nxt = []
            for g in gens:
                try:
                    next(g)
                    nxt.append(g)
                except StopIteration:
                    pass
            gens = nxt

    def w_in_phase(self, tiles, ntok, q, n, kout_fn, vout_fn, kt_col0):
        s = self.s
        self.deferred = []
        s.op('pool', lambda e: e.memset(self.mva[:, 0:len(tiles), :, 128:129], 1.0), writes=['mva'] + [('actT', j) for j in range(16, 21)])
        self.norm_to_hT(tiles, 'gm', None)
        KTd = self.KTs if n is None else self.KT
        self.tm_matmuls(tiles, 0)
        self.run_gens([self.qk_evac_a(0, ti, t, tsz, ti, None) for ti, (t, tsz) in enumerate(tiles)])
        self.tm_matmuls(tiles, 4)
        for ti, (t, tsz) in enumerate(tiles):
            self.qk_evac_b(ti, t, tsz, ti, self.QT, 'QT', t * 128)
        self.run_gens([self.qk_evac_a(1, ti, t, tsz, 4 + ti, kout_fn(t, tsz)) for ti, (t, tsz) in enumerate(tiles)])

        def v_evac(ti, t, tsz, b):
            vi = 4 + ti if len(self.wk) >= 12 else 4 + (ti % 2)
            vf = self.wk[vi]
            vk = ('wk', vi)
            s.op('act', lambda e: e.activation(out=vf[0:tsz, :], in_=self.banks[b][0:tsz, :], func=AF.Copy), reads=[self.bk[b]], writes=[vk])
            self.deferred.append(('vout%d' % ti, vout_fn(t, tsz), vf[0:tsz, :], [vk]))
            self.v_store(ti, t, tsz, self.banks[b], self.bk[b], n)
        self.tm_matmuls(tiles, 0)
        for ti, (t, tsz) in enumerate(tiles):
            self.qk_evac_b(ti, t, tsz, 4 + ti, KTd, 'KT', kt_col0 + t * 128)
        for ti, (t, tsz) in enumerate(tiles):
            v_evac(ti, t, tsz, ti)

        def mk_evac(ti, t, tsz, b):
            s.op('dve', lambda e: e.tensor_scalar(out=self.ktm[0:tsz, ti, :], in0=self.banks[b][0:tsz, :], scalar1=128.0 ** -0.5, scalar2=None, op0=ALU.mult),
                 reads=[self.bk[b]], writes=['ktm', ('actT', 12), ('actT', 13), ('actT', 14), ('actT', 15)])
        self.tm_group(tiles, 4, mk_evac)

        def mv_evac(ti, t, tsz, b):
            s.op('dve', lambda e: e.tensor_copy(self.mva[0:tsz, ti, :, 0:128], self.banks[b][0:tsz, :].rearrange("p (h c) -> p h c", c=128)),
                 reads=[self.bk[b]], writes=['mva'] + [('actT', j) for j in range(16, 21)])
        self.tm_group(tiles, 0, mv_evac)

        def mo_evac(ti, t, tsz, b):
            dst = self.xs_[ti // 2][0:tsz, (ti % 2) * 512:(ti % 2 + 1) * 512]
            s.op('act', lambda e: e.activation(out=dst, in_=self.banks[b][0:tsz, :], func=AF.Sigmoid),
                 reads=[self.bk[b]], writes=['so', ('xs', ti // 2)])
        self.tm_group(tiles, 4, mo_evac)

        for i in range(4):
            slot, sk = self.wslot()
            for u in range(2):
                head = (i % 2) * 2 + u
                b = (2 * i + u) % 4
                for kc in range(8):
                    s.op('pe', lambda e, slot=slot, u=u, kc=kc, b=b: e.matmul(self.banks[b][:, 0:ntok], slot[:, u * 1024 + kc * 128:u * 1024 + (kc + 1) * 128],
                                                                           self.hT[:, kc, 0:ntok], start=(kc == 0), stop=(kc == 7)),
                         reads=[sk] + [('hT', t) for t, _ in tiles], writes=[self.bk[b]])
                if i < 2:
                    s.op('act', lambda e, head=head, b=b: e.activation(out=self.mqT[:, head, 0:ntok], in_=self.banks[b][:, 0:ntok], func=AF.Copy),
                         reads=[self.bk[b]], writes=['mqT'] + [('actT', j) for j in range(4, 8)])
                else:
                    s.op('dve', lambda e, head=head, b=b: e.tensor_scalar(out=self.mkT[:, head, 0:ntok], in0=self.banks[b][:, 0:ntok], scalar1=128.0 ** -0.5, scalar2=None, op0=ALU.mult),
                         reads=[self.bk[b]], writes=['mkT'] + [('actT', j) for j in range(8, 12)])
        for gi in range(2):
            b = 4 + gi
            for kc in range(8):
                s.op('pe', lambda e, gi=gi, kc=kc, b=b: e.matmul(self.banks[b][0:4, 0:ntok], self.wgb[:, kc * 8 + gi * 4:kc * 8 + gi * 4 + 4], self.hT[:, kc, 0:ntok],
                                                               start=(kc == 0), stop=(kc == 7)),
                     reads=['wgb'] + [('hT', t) for t, _ in tiles], writes=[self.bk[b]])
        bi = self.pbv('bi')
        nbf = self.pbv('nbf')
        R = self.rows
        s.op('act', lambda e: e.activation(out=R[0:4, 0, 0:ntok], in_=self.banks[4][0:4, 0:ntok], func=AF.Identity, bias=bi[0:4, :]),
             reads=[self.bk[4], 'pbt'], writes=['ig'])
        s.op('dve', lambda e: e.tensor_scalar(out=R[0:4, 1, 0:ntok], in0=self.banks[5][0:4, 0:ntok], scalar1=nbf[0:4, :], scalar2=-1.0, op0=ALU.add, op1=ALU.mult),
             reads=[self.bk[5], 'pbt'], writes=['nlf'])
        s.op('act', lambda e: e.activation(out=R[0:4, 1, 0:ntok], in_=R[0:4, 1, 0:ntok], func=AF.Exp), reads=['nlf'], writes=['nlf'])
        s.op('dve', lambda e: e.tensor_scalar(out=R[0:4, 1, 0:ntok], in0=R[0:4, 1, 0:ntok], scalar1=1.0, scalar2=None, op0=ALU.add), reads=['nlf'], writes=['nlf'])
        s.op('act', lambda e: e.activation(out=R[0:4, 1, 0:ntok], in_=R[0:4, 1, 0:ntok], func=AF.Ln), reads=['nlf'], writes=['nlf'])

        for (sem, dst, src, rk_) in self.deferred:
            s.dma('sp', sem, lambda e, dst=dst, src=src: e.dma_start(out=dst, in_=src), reads=rk_)
        self.deferred = []

    def v_store(self, ti, t, tsz, vf, vk, n):
        s = self.s
        if n is None:
            s.op('act', lambda e: e.activation(out=self.Vnew[0:tsz, :].rearrange("p (h c) -> p h c", c=65)[:, :, 0:64],
                                               in_=vf[0:tsz, :].rearrange("p (h c) -> p h c", c=64), func=AF.Copy), reads=[vk], writes=['Vnew'])
            return
        tile_i = 4 * (n % 2) + ti
        s.op('act', lambda e: e.activation(out=self.Vnat[0:tsz, tile_i, :].rearrange("p (h c) -> p h c", c=65)[:, :, 0:64],
                                           in_=vf[0:tsz, :].rearrange("p (h c) -> p h c", c=64), func=AF.Copy),
             reads=[vk], writes=['Vnat'])
        if ti == 3:
            blk = 4 * (n % 2)
            s.dma('sp', 'vw', lambda e: e.dma_start(out=self.vscr.rearrange("(t p) c -> p t c", p=128), in_=self.Vnat[:, blk:blk + 4, :]),
                  reads=['Vnat'], writes=['vscr'])
            s.dma('sp', 'vr4', lambda e: e.dma_start(out=self.V4[:, blk:blk + 4, :], in_=self.vscr.rearrange("(i g) c -> i g c", g=4)),
                  reads=['vscr'], writes=['V4'])
            s.dma('sp', 'vr16', lambda e: e.dma_start(out=self.V16[32 * n:32 * n + 32, :, :], in_=self.vscr.rearrange("(i r) c -> i r c", r=16)),
                  reads=['vscr'], writes=['V16'])

    def attention(self, n):
        s = self.s
        mprev = self.cbm('mprev')
        mcur = self.cbm('mcur')
        m16 = self.cbm('m16').rearrange("p (n c) -> p n c", c=32)
        jobs = []
        for h in range(8):
            p, off = h // 2, (h % 2) * 64
            QTh = self.QT[off:off + 64, p, :]
            KTh = self.KT[off:off + 64, p, :]
            acc = self.banks[6 + (h % 2)]
            Q4 = QTh.rearrange("d (i g) -> d g i", g=4)
            K4 = KTh.rearrange("d (i g) -> d g i", g=4)
            A4 = acc[0:65, :].rearrange("p (i g) -> p g i", g=4)
            Q16 = QTh.rearrange("d (i r) -> d r i", r=16)
            K16 = KTh.rearrange("d (i r) -> d r i", r=16)
            A16 = acc[0:65, :].rearrange("p (i r) -> p r i", r=16)
            hj = []
            tl = []
            for b in range(4):
                B = 4 * n + b
                if B == 0:
                    continue
                tl.append((KTh[:, (B - 1) * 128:B * 128], QTh[:, b * 128:(b + 1) * 128], b * 128, 128,
                           self.Vnat[:, (B - 1) % 8, h * 65:(h + 1) * 65], acc[0:65, b * 128:(b + 1) * 128]))
            c0 = 128 if n == 0 else 0
            hj.append((tl, 128, c0, bcast(mprev.unsqueeze(1), [128, (512 - c0) // 128, 128]), 128))
            tl = []
            for b in range(4):
                B = 4 * n + b
                tl.append((KTh[:, B * 128:(B + 1) * 128], QTh[:, b * 128:(b + 1) * 128], b * 128, 128,
                           self.Vnat[:, B % 8, h * 65:(h + 1) * 65], acc[0:65, b * 128:(b + 1) * 128]))
            hj.append((tl, 128, 0, bcast(mcur.unsqueeze(1), [128, 4, 128]), 128))
            if n > 0:
                tl = []
                for g in range(4):
                    tl.append((K4[:, g, (n - 1) * 128:n * 128], Q4[:, g, :], g * 128, 128,
                               self.V4[:, (4 * (n - 1) + g) % 8, h * 65:(h + 1) * 65], A4[:, g, :]))
                hj.append((tl, 128, 0, bcast(mprev.unsqueeze(1), [128, 4, 128]), 128))
            tl = []
            for g in range(4):
                tl.append((K4[:, g, n * 128:(n + 1) * 128], Q4[:, g, :], g * 128, 128,
                           self.V4[:, (4 * n + g) % 8, h * 65:(h + 1) * 65], A4[:, g, :]))
            hj.append((tl, 128, 0, bcast(mcur.unsqueeze(1), [128, 4, 128]), 128))
            nk = 32 * (n + 1)
            tl = []
            for r in range(16):
                tl.append((K16[:, r, 0:nk], Q16[:, r, :], r * 32, 32, self.V16[0:nk, r, h * 65:(h + 1) * 65], A16[:, r, :]))
            hj.append((tl, nk, 0, bcast(m16[0:nk, n, :].unsqueeze(1), [nk, 16, 32]), 32))
            for bi_, it in enumerate(hj):
                jobs.append((h, it, bi_ == 0, bi_ == len(hj) - 1))
        N = len(jobs)
        NPT = len(self.PTs)

        def qk(i):
            h, (tl, nk, c0, mk_, tw), first, last = jobs[i]
            sb = i % 5
            bank = self.banks[sb]
            for (kap, qap, col, nq, vap, oap) in tl:
                s.op('pe', lambda e, kap=kap, qap=qap, col=col, nq=nq, bank=bank, nk=nk: e.matmul(bank[0:nk, col:col + nq], kap, qap, start=True, stop=True),
                     reads=['KT', 'QT'], writes=[self.bk[sb]])

        def em(i):
            h, (tl, nk, c0, mk_, tw), first, last = jobs[i]
            sb = i % 5
            bank = self.banks[sb]
            PT = self.PTs[i % NPT]
            pk = ('PT', i % NPT)
            s.op('act', lambda e: e.activation(out=PT[0:nk, c0:512], in_=bank[0:nk, c0:512], func=AF.Exp, scale=0.125), reads=[self.bk[sb]], writes=[pk])
            pv = PT[0:nk, c0:512].rearrange("p (a n) -> p a n", n=tw)
            s.op('dve', lambda e: e.tensor_tensor(out=pv, in0=pv, in1=mk_, op=ALU.mult), reads=[pk, 'cbt'], writes=[pk])

        def pvm(i):
            h, (tl, nk, c0, mk_, tw), first, last = jobs[i]
            PT = self.PTs[i % NPT]
            pk = ('PT', i % NPT)
            accb = 6 + (h % 2)
            nt_ = len(tl)
            for j, (kap, qap, col, nq, vap, oap) in enumerate(tl):
                s.op('pe', lambda e, j=j, col=col, nq=nq, vap=vap, oap=oap: e.matmul(oap, vap, PT[0:nk, col:col + nq], start=(first and j == 0), stop=(last and j == nt_ - 1), skip_group_check=True),
                     reads=[pk, 'Vnat', 'V4', 'V16'], writes=[self.bk[accb]])

        pending = []
        LA = 4
        for i in range(min(LA, N)):
            qk(i)
        for i in range(N):
            em(i)
            if i + LA < N:
                qk(i + LA)
            pvm(i)
            h, _, first, last = jobs[i]
            if last:
                self.attn_norm_a(h)
                pending.append((h, i + 2))
            for (hh, when) in list(pending):
                if when <= i:
                    self.attn_norm_b(hh)
                    pending.remove((hh, when))
        for (hh, when) in pending:
            self.attn_norm_b(hh)

    def attn_norm_a(self, h):
        s = self.s
        accb = 6 + (h % 2)
        aw = self.accw[h % 2]
        ak = ('accw', h % 2)
        s.op('act', lambda e: e.activation(out=aw[0:65, :], in_=self.banks[accb][0:65, :], func=AF.Copy), reads=[self.bk[accb]], writes=[ak])
        s.op('act', lambda e: e.activation(out=aw[64:65, :], in_=aw[64:65, :], func=AF.Ln), reads=[ak], writes=[ak])
        s.op('act', lambda e: e.activation(out=aw[64:65, :], in_=aw[64:65, :], func=AF.Exp, scale=-1.0), reads=[ak], writes=[ak])

    def attn_norm_b(self, h):
        s = self.s
        aw = self.accw[h % 2]
        ak = ('accw', h % 2)
        ones = self.cf('onescol')
        s.op('pe', lambda e: e.matmul(self.banks[5][0:64, :], ones[64:65, 0:64], aw[64:65, :], start=True, stop=True), reads=[ak, 'cft'], writes=[self.bk[5]])
        s.op('dve', lambda e: e.tensor_tensor(out=self.attT[0:64, h, :], in0=aw[0:64, :], in1=self.banks[5][0:64, :], op=ALU.mult),
             reads=[ak, self.bk[5]], writes=['attT'])

    def attn_norm(self, h, accb, bcb, ncol, dst):
        s = self.s
        aw = self.accw[h % 2]
        ak = ('accw', h % 2)
        s.op('act', lambda e: e.activation(out=aw[0:65, 0:ncol], in_=self.banks[accb][0:65, 0:ncol], func=AF.Copy), reads=[self.bk[accb]], writes=[ak])
        s.op('dve', lambda e: e.reciprocal(aw[64:65, 0:ncol], aw[64:65, 0:ncol]), reads=[ak], writes=[ak])
        ones = self.cf('onescol')
        s.op('pe', lambda e: e.matmul(self.banks[bcb][0:64, 0:ncol], ones[64:65, 0:64], aw[64:65, 0:ncol], start=True, stop=True),
             reads=[ak, 'cft'], writes=[self.bk[bcb]])
        s.op('dve', lambda e: e.tensor_tensor(out=dst, in0=aw[0:64, 0:ncol], in1=self.banks[bcb][0:64, 0:ncol], op=ALU.mult),
             reads=[ak, self.bk[bcb]], writes=['attT'])

    def ml_cols_cbc(self, L, rk, posm):
        s = self.s
        CR = self.crow
        nb, u, c, wint, emt, wsr = [CR[0:4, i, 0:L] for i in range(6)]
        ident = self.cf('ident')
        sel = self.cf('sel').rearrange("p (h m) -> p h m", m=128)
        for i, src in enumerate([u, wint, emt, wsr]):
            s.op('pe', lambda e, i=i, src=src: e.transpose(self.banks[1][0:L, 4 * i:4 * i + 4], src, ident[0:4, 0:4]), reads=[rk, 'cft'], writes=[self.bk[1]])
        s.op('act', lambda e: e.activation(out=self.cols[0:L, :], in_=self.banks[1][0:L, 0:16], func=AF.Copy), reads=[self.bk[1]], writes=['cols'])
        for h in range(4):
            s.op('pe', lambda e, h=h: e.matmul(self.banks[0][0:L, h * 128:h * 128 + L], sel[0:4, h, 0:L], c, start=True, stop=False), reads=[rk, 'cft'], writes=[self.bk[0]])
            s.op('pe', lambda e, h=h: e.matmul(self.banks[0][0:L, h * 128:h * 128 + L], ident[0:L, 0:L], posm[0:L, 0:L], start=False, stop=True), reads=['cft'], writes=[self.bk[0]])

    def ml_head_av(self, h, L, ci, cs_):
        s = self.s
        stb = 2 + (h % 2)
        s.op('pe', lambda e: e.matmul(self.banks[stb][0:L, 0:L], self.mkT[:, h, cs_], self.mqT[:, h, cs_], start=True, stop=True),
             reads=['mkT', 'mqT'], writes=[self.bk[stb]])
        s.op('act', lambda e: e.activation(out=self.Ew[0:L, 0:L], in_=self.banks[0][0:L, h * 128:h * 128 + L], func=AF.Exp, scale=-1.0, bias=self.cols[0:L, h:h + 1]),
             reads=[self.bk[0], 'cols'], writes=['Ew'])
        s.op('dve', lambda e: e.tensor_tensor(out=self.aT[0:L, 0:L], in0=self.banks[stb][0:L, 0:L], in1=self.Ew[0:L, 0:L], op=ALU.mult),
             reads=[self.bk[stb], 'Ew'], writes=['aT'])
        s.op('pe', lambda e: e.matmul(self.banks[4][0:L, 0:129], self.aT[0:L, 0:L], self.mva[0:L, ci, h, :], start=True, stop=True),
             reads=['aT', 'mva'], writes=[self.bk[4]])

    def ml_head_out(self, h, L, ci, cs_):
        s = self.s
        ident = self.cf('ident')
        mlg = self.pbv('mlg')
        nw = self.numw
        s.op('act', lambda e: e.activation(out=nw[0:L, 0:129], in_=self.banks[5][0:L, 0:129], func=AF.Copy, scale=self.cols[0:L, 4 + h:5 + h]),
             reads=[self.bk[5], 'cols'], writes=['numw'])
        s.op('dve', lambda e: e.tensor_tensor(out=nw[0:L, 0:129], in0=nw[0:L, 0:129], in1=self.banks[4][0:L, 0:129], op=ALU.add),
             reads=[self.bk[4], 'numw'], writes=['numw'])
        den, ssq, rr = nw[0:L, 129:130], nw[0:L, 130:131], nw[0:L, 131:132]
        s.op('dve', lambda e: e.scalar_tensor_tensor(out=den, in0=nw[0:L, 128:129], scalar=-1.0, in1=nw[0:L, 128:129], op0=ALU.mult, op1=ALU.max),
             reads=['numw'], writes=['numw'])
        s.op('dve', lambda e: e.tensor_tensor(out=den, in0=den, in1=self.cols[0:L, 8 + h:9 + h], op=ALU.max),
             reads=['numw', 'cols'], writes=['numw'])
        s.op('dve', lambda e: e.reciprocal(den, den), reads=['numw'], writes=['numw'])
        hgj = self.hg[0:L, 0, :]
        s.op('act', lambda e: e.activation(out=hgj, in_=nw[0:L, 0:128], func=AF.Square, scale=den, accum_out=ssq), reads=['numw'], writes=['hg0', 'ssq'])
        self.rstd(ssq, ssq, 1.0 / 128, ['ssq'], ['ssq'])
        s.op('dve', lambda e: e.tensor_tensor(out=rr, in0=den, in1=ssq, op=ALU.mult), reads=['numw', 'ssq'], writes=['rr'])
        s.op('dve', lambda e: e.scalar_tensor_tensor(out=hgj, in0=nw[0:L, 0:128], scalar=rr, in1=mlg[0:L, h * 128:(h + 1) * 128], op0=ALU.mult, op1=ALU.mult),
             reads=['numw', 'rr', 'pbt'], writes=['hg0'])
        so_ap = self.xs_[ci // 2][0:L, (ci % 2) * 512 + h * 128:(ci % 2) * 512 + (h + 1) * 128]
        hg2 = self.hg[0:L, 1, :]
        s.op('pool', lambda e: e.tensor_tensor(out=hg2, in0=hgj, in1=so_ap, op=ALU.mult), reads=['hg0', 'so'], writes=['hg1'])
        s.op('pe', lambda e: e.transpose(self.banks[7][:, 0:L], hg2, ident[0:L, 0:L]), reads=['hg1', 'cft'], writes=[self.bk[7]])
        s.op('act', lambda e: e.activation(out=self.hmT[:, h, cs_], in_=self.banks[7][:, 0:L], func=AF.Copy), reads=[self.bk[7]], writes=['hmT'])

    def mlstm_chunk(self, ci, tsz, col0, first_of_seq):
        s = self.s
        R = self.rows
        L = tsz
        cs_ = slice(col0, col0 + L)
        CR = self.crow
        ig, nlf = R[0:4, 0, cs_], R[0:4, 1, cs_]
        nb, u, c, wint, emt, wsr = [CR[0:4, i, 0:L] for i in range(6)]
        mprev, ncl = self.mrow[0:4, 0:1], self.mrow[0:4, 1:2]
        ones4 = self.cf('onesrow')[0:4, 0:L]
        rk = ('rows', ci)
        s.op('dve', lambda e: e.tensor_tensor_scan(nb, ones4, nlf, 0.0, ALU.mult, ALU.add), reads=['nlf', 'cft'], writes=[rk])
        s.op('dve', lambda e: e.tensor_tensor(out=u, in0=ig, in1=nb, op=ALU.add), reads=['ig', rk], writes=[rk])
        s.op('dve', lambda e: e.tensor_tensor_scan(c, u, u, mprev, ALU.max, ALU.max), reads=[rk, 'mprev'], writes=[rk])
        s.op('act', lambda e: e.activation(out=wint, in_=c, func=AF.Exp, scale=-1.0, bias=mprev), reads=[rk, 'mprev'], writes=[rk])
        s.op('dve', lambda e: e.tensor_tensor(out=emt, in0=nb, in1=c, op=ALU.subtract), reads=[rk], writes=[rk])
        s.op('act', lambda e: e.activation(out=emt, in_=emt, func=AF.Exp), reads=[rk], writes=[rk])
        s.op('dve', lambda e: e.tensor_scalar(out=ncl, in0=CR[0:4, 2, L - 1:L], scalar1=-1.0, scalar2=None, op0=ALU.mult), reads=[rk], writes=['ncl'])
        s.op('act', lambda e: e.activation(out=wsr, in_=u, func=AF.Exp, bias=ncl), reads=[rk, 'ncl'], writes=[rk])
        s.op('dve', lambda e: e.tensor_tensor(out=mprev, in0=CR[0:4, 2, L - 1:L], in1=CR[0:4, 0, L - 1:L], op=ALU.subtract),
             reads=[rk], writes=['mprev'])
        self.ml_cols_cbc(L, rk, self.cf('posmask'))
        sel = self.cf('sel').rearrange("p (h m) -> p h m", m=128)
        for h in range(4):
            s.op('pe', lambda e, h=h: e.matmul(self.banks[1][:, 32 + h:33 + h], sel[0:4, h, :], CR[0:4, 3, L - 1:L], start=True, stop=True),
                 reads=[rk, 'cft'], writes=[self.bk[1]])
        s.op('act', lambda e: e.activation(out=self.wcs[:, :], in_=self.banks[1][:, 32:36], func=AF.Copy), reads=[self.bk[1]], writes=['wcs'])
        for h in range(4):
            self.ml_head_av(h, L, ci, cs_)
            s.op('pe', lambda e, h=h: e.matmul(self.banks[5][0:L, 0:129], self.mqT[:, h, cs_], self.Cbf[:, h, :], start=True, stop=True),
                 reads=['mqT', 'Cbf'], writes=[self.bk[5]])
            self.ml_head_out(h, L, ci, cs_)
            s.op('dve', lambda e, h=h: e.tensor_scalar(out=self.ks[0:L, :], in0=self.ktm[0:L, ci, h * 128:(h + 1) * 128], scalar1=self.cols[0:L, 12 + h:13 + h], scalar2=None, op0=ALU.mult),
                 reads=['ktm', 'cols'], writes=['ks'])
            s.op('pe', lambda e, h=h: e.matmul(self.banks[6][:, 0:129], self.ks[0:L, :], self.mva[0:L, ci, h, :], start=True, stop=True),
                 reads=['ks', 'mva'], writes=[self.bk[6]])
            s.op('dve', lambda e, h=h: e.scalar_tensor_tensor(out=self.Caug[:, h, :], in0=self.Caug[:, h, :], scalar=self.wcs[:, h:h + 1], in1=self.banks[6][:, 0:129], op0=ALU.mult, op1=ALU.add),
                 reads=[self.bk[6], 'wcs', 'Caug'], writes=['Caug'])
            s.op('act', lambda e, h=h: e.activation(out=self.Cbf[:, h, :], in_=self.Caug[:, h, :], func=AF.Copy), reads=['Caug'], writes=['Cbf'])

    def sample_phase(self, T):
        s = self.s
        tiles = [(0, 64)]
        ntok = 64
        self.crow = T("crow", [4, 6, 128], F32)
        self.cols = T("cols", [128, 16], F32)
        self.wcs = T("wcs", [128, 4], F32)
        self.Ew = T("Ew", [128, 128], F32)
        self.aT = T("aT", [128, 128], BF16)
        self.ks = T("ks", [128, 128], BF16)
        self.numw = T("numw", [128, 132], F32)
        self.hg = T("hg", [128, 2, 128], F32)
        self.KTs = T("KTs", [128, 4, 64], BF16)
        self.Vnew = T("Vnew", [64, 520], BF16)
        x1keys = [('x1', 0)]
        s.dma('sp', 'xin', lambda e: e.dma_start(out=self.x1[0:64, 0, :], in_=self.xs), writes=x1keys)
        s.dma('sp', 'csin', lambda e: e.dma_start(out=self.cs[0:64, 0, 0, :], in_=self.coss), writes=['cs'])
        s.dma('sp', 'csin', lambda e: e.dma_start(out=self.cs[0:64, 1, 0, :], in_=self.sins), writes=['cs'])
        self.build_csg(1, 64)
        self.mark('s_start')
        if 'pre' in self.sample_parts:
            self.sample_prefetch(T)
        self.norm_to_hT(tiles, 'g1', None)
        self.ffn(tiles, ntok)
        self.mark('s_ffn1')
        self.w_in_phase(tiles, ntok, None, None, lambda t, tsz: self.ksn[0:64, :], lambda t, tsz: self.vsn[0:64, :], 0)
        self.mark('s_w_in')
        if 'dummy' in self.sample_parts:
            self._dummy = T('dummy_pad', [128, 6144], BF16)
        if 'attn' in self.sample_parts:
            self.sample_attention(T)
        self.mark('s_attn')
        if 'ml' in self.sample_parts:
            self.sample_mlstm(T)
        self.mark('s_ml')
        self.w_out_phase(tiles)
        self.norm_to_hT(tiles, 'g2', None)
        self.ffn(tiles, ntok)
        self.mark('s_end')
        s.dma('sp', 'yout', lambda e: e.dma_start(out=self.ys, in_=self.x1[0:64, 0, :]), reads=x1keys)

    def cache_load(self, b):
        s = self.s
        base = b * WIN_BUF
        for nm, src, bufs in (('kc', self.ck, self.kcb), ('vc', self.cv, self.vcb)):
            buf = bufs[b % 2]
            key = (nm, b % 2)
            sem = '%s%d' % (nm, b % 2)
            s.dma('pool', sem, lambda e, buf=buf, src=src: e.dma_start(out=buf[:, 0, :], in_=src[base + 1920:base + 2048, :], max_dma_last_dim=4096), writes=[key])
            s.dma('pool', sem, lambda e, buf=buf, src=src: e.dma_start(out=buf[:, 1:5, :], in_=src[base + 1536:base + 2048, :].rearrange("(i g) c -> i g c", g=4), max_dma_last_dim=4096), writes=[key])
            s.dma('pool', sem, lambda e, buf=buf, src=src: e.dma_start(out=buf[:, 5:9, :], in_=src[base:base + 2048, :].rearrange("(i r) c -> i r c", r=16)[:, 0:4, :], max_dma_last_dim=4096), writes=[key])

    def sample_prefetch(self, T):
        self.kcb = [T("kcb%d" % i, [128, 9, 512], BF16) for i in range(2)]
        self.vcb = [T("vcb%d" % i, [128, 9, 512], BF16) for i in range(2)]
        self.cache_load(0)
        self.cache_load(1)

    def sample_attention(self, T):
        s = self.s
        KcT = T("KcT", [128, 36, 128], BF16)
        PTs = [T("PTs%d" % i, [128, 288], BF16) for i in range(2)]
        accb = 6
        acc = self.banks[accb]
        onesb = self.cbm('onesb')
        identb = self.cbm('identb')
        mA = self.cbm('mA')
        mnew = self.cbm('mnew64')
        Qpad = T("Qpad", [128, 4, 2, 64], BF16)
        self.Qpad = Qpad
        s.op('pool', lambda e: e.memset(Qpad[:], 0.0), writes=['Qpad'])
        s.op('pool', lambda e: e.tensor_copy(Qpad[0:64, :, 0, :], self.QT[0:64, :, 0:64]), reads=['QT'], writes=['Qpad'])
        s.op('pool', lambda e: e.tensor_copy(Qpad[64:128, :, 1, :], self.QT[64:128, :, 0:64]), reads=['QT'], writes=['Qpad'])
        for p in range(4):
            s.op('pe', lambda e, p=p: e.matmul(self.banks[0][0:64, p * 128:(p + 1) * 128], self.KTs[:, p, 0:64], Qpad[:, p, :, :], start=True, stop=True),
                 reads=['KT', 'Qpad'], writes=[self.bk[0]])
        if getattr(self, 'sa_level', 9) == -3:
            return
        s.op('act', lambda e: e.activation(out=self.PT[0:64, :], in_=self.banks[0][0:64, :], func=AF.Exp, scale=0.125), reads=[self.bk[0]], writes=['PT'])
        if getattr(self, 'sa_level', 9) == -4:
            return
        pv = self.PT[0:64, :].rearrange("p (h n) -> p h n", n=64)
        s.op('dve', lambda e: e.tensor_tensor(out=pv, in0=pv, in1=bcast(mnew[0:64, :].unsqueeze(1), [64, 8, 64]), op=ALU.mult), reads=['PT', 'cbt'], writes=['PT'])
        if getattr(self, 'sa_level', 9) == -1:
            return
        for h in range(8):
            s.op('pe', lambda e, h=h: e.matmul(acc[0:64, h * 64:(h + 1) * 64], self.Vnew[0:64, h * 65:h * 65 + 64], self.PT[0:64, h * 64:(h + 1) * 64],
                                               start=(h == 0), stop=False, skip_group_check=True), reads=['PT', 'Vnew'], writes=[self.bk[accb]])
        s.op('pe', lambda e: e.matmul(acc[64:65, :], onesb[0:64, 0:1], self.PT[0:64, :], start=True, stop=False, skip_group_check=True),
             reads=['PT', 'cbt'], writes=[self.bk[accb]])
        if getattr(self, 'sa_level', 9) == -2:
            return
        acc3 = acc[0:64, :].rearrange("p (h n) -> p h n", n=64)
        den3 = acc[64:65, :].rearrange("p (h n) -> p h n", n=64)
        args = (KcT, PTs, acc, accb, onesb, identb, mA)
        self._sample_attn_seq(0, *args, 'T')
        self._sample_attn_seq(0, *args, 'E')
        for b in range(NSB):
            self._sample_attn_seq(b, *args, 'QK')
            if b + 1 < NSB:
                self._sample_attn_seq(b + 1, *args, 'T')
            self._sample_attn_seq(b, *args, 'EM')
            if b + 1 < NSB:
                self._sample_attn_seq(b + 1, *args, 'E')
            self._sample_attn_seq(b, *args, 'PV')
        aw = self.accw[0]
        s.op('act', lambda e: e.activation(out=aw[0:65, :], in_=acc[0:65, :], func=AF.Copy), reads=[self.bk[accb]], writes=[('accw', 0)])
        s.op('act', lambda e: e.activation(out=aw[64:65, :], in_=aw[64:65, :], func=AF.Ln), reads=[('accw', 0)], writes=[('accw', 0)])
        s.op('act', lambda e: e.activation(out=aw[64:65, :], in_=aw[64:65, :], func=AF.Exp, scale=-1.0), reads=[('accw', 0)], writes=[('accw', 0)])
        ones = self.cf('onescol')
        s.op('pe', lambda e: e.matmul(self.banks[7][0:64, :], ones[64:65, 0:64], aw[64:65, :], start=True, stop=True), reads=[('accw', 0), 'cft'], writes=[self.bk[7]])
        s.op('dve', lambda e: e.tensor_tensor(out=self.attT[0:64, :, 0:64], in0=aw[0:64, :].rearrange("p (h n) -> p h n", n=64),
                                              in1=self.banks[7][0:64, :].rearrange("p (h n) -> p h n", n=64), op=ALU.mult),
             reads=[('accw', 0), self.bk[7]], writes=['attT'])

    def _sample_attn_seq(self, b, KcT, PTs, acc, accb, onesb, identb, mA, part):
        s = self.s
        den3 = acc[64:65, :].rearrange("p (h n) -> p h n", n=64)
        kb, vb = self.kcb[b % 2], self.vcb[b % 2]
        kk, vk = ('kc', b % 2), ('vc', b % 2)
        PTb = PTs[b % 2]
        pk = ('PTs', b % 2)
        if part in ('T', 'E'):
            for bb in range(5):
                bank = self.banks[1 + bb]
                bbf = bank[:, :].bitcast(BF16)
                n_in = 8 if bb < 4 else 4
                if part == 'T':
                    for j in range(n_in):
                        idx = bb * 8 + j
                        tile_, p = idx // 4, idx % 4
                        s.op('pe', lambda e, bbf=bbf, j=j, tile_=tile_, p=p, kb=kb: e.transpose(bbf[:, j * 128:(j + 1) * 128], kb[:, tile_, p * 128:(p + 1) * 128], identb),
                             reads=[kk, 'cbt'], writes=[self.bk[1 + bb]])
                else:
                    dst = KcT[:, bb * 8:bb * 8 + n_in, :]
                    srcv = bbf[:, 0:n_in * 128].rearrange("p (a n) -> p a n", n=128)
                    if bb % 2 == 0:
                        s.op('act', lambda e, dst=dst, srcv=srcv: e.activation(out=dst, in_=srcv, func=AF.Copy), reads=[self.bk[1 + bb]], writes=[('KcT', bb)])
                    else:
                        s.op('dve', lambda e, dst=dst, srcv=srcv: e.tensor_copy(dst, srcv), reads=[self.bk[1 + bb]], writes=[('KcT', bb)])
            return
        if part == 'QK':
            for tile_ in range(9):
                for p in range(4):
                    idx = tile_ * 4 + p
                    col = (tile_ * 8 + 2 * p) * 4
                    s.op('pe', lambda e, idx=idx, col=col, p=p: e.matmul(self.banks[0][:, col:col + 8].rearrange("r (a t) -> r a t", t=4), KcT[:, idx, :], self.Qpad[:, p, :, 4 * b:4 * b + 4], start=True, stop=True),
                         reads=[('KcT', idx // 8), 'Qpad'], writes=[self.bk[0]])
            return
        if part == 'EM':
            s.op('act', lambda e: e.activation(out=PTb[:, :], in_=self.banks[0][:, 0:288], func=AF.Exp, scale=0.125), reads=[self.bk[0]], writes=[pk])
            pa = PTb[:, 0:32].rearrange("p (h t) -> p h t", t=4)
            s.op('dve', lambda e: e.tensor_tensor(out=pa, in0=pa, in1=bcast(mA.unsqueeze(1), [128, 8, 4]), op=ALU.mult), reads=[pk, 'cbt'], writes=[pk])
            return
        last_b = (b == NSB - 1)
        for h in range(8):
            s.op('pe', lambda e, h=h: e.matmul(acc[0:64, h * 64 + 4 * b:h * 64 + 4 * b + 4], vb[:, 0, h * 64:(h + 1) * 64], PTb[:, h * 4:h * 4 + 4],
                                               start=False, stop=False, skip_group_check=True), reads=[pk, vk], writes=[self.bk[accb]])
            for t in range(4):
                for grp in (1, 5):
                    tile_ = grp + t
                    col = (tile_ * 8 + h) * 4 + t
                    s.op('pe', lambda e, h=h, t=t, tile_=tile_, col=col: e.matmul(acc[0:64, h * 64 + 4 * b + t:h * 64 + 4 * b + t + 1], vb[:, tile_, h * 64:(h + 1) * 64],
                                                                             PTb[:, col:col + 1], start=False, stop=False, skip_group_check=True),
                         reads=[pk, vk], writes=[self.bk[accb]])
        for h in range(8):
            s.op('pe', lambda e, h=h: e.matmul(acc[64:65, h * 64 + 4 * b:h * 64 + 4 * b + 4], onesb[:, 0:1], PTb[:, h * 4:h * 4 + 4], start=False, stop=False, skip_group_check=True),
                 reads=[pk, 'cbt'], writes=[self.bk[accb]])
        for t in range(4):
            for grp in (1, 5):
                tile_ = grp + t
                rhs = PTb[:, tile_ * 32:(tile_ + 1) * 32].rearrange("p (h t) -> p h t", t=4)[:, :, t:t + 1]
                lastmm = last_b and t == 3 and grp == 5
                s.op('pe', lambda e, rhs=rhs, t=t, lastmm=lastmm: e.matmul(den3[:, :, 4 * b + t:4 * b + t + 1], onesb[:, 0:1], rhs, start=False, stop=lastmm, skip_group_check=True),
                     reads=[pk, 'cbt'], writes=[self.bk[accb]])
        if b + 2 < NSB:
            self.cache_load(b + 2)

    def sample_mlstm(self, T):
        s = self.s
        L = 64
        ci = 0
        cs_ = slice(0, 64)
        R = self.rows
        CR = self.crow
        ig, nlf = R[0:4, 0, cs_], R[0:4, 1, cs_]
        nb, u, c, wint, emt, wsr = [CR[0:4, i, 0:L] for i in range(6)]
        v3 = lambda ap: ap.rearrange("p (b t) -> p b t", t=4)
        nb3, nlf3, u3, c3, wint3, ws3 = v3(nb), v3(nlf), v3(u), v3(c), v3(wint), v3(wsr)
        m0 = T("m0r", [4, 16], F32)
        mnw = T("mnw", [4, 16], F32)
        s.dma('sp', 'm0in', lambda e: e.dma_start(out=m0[:, :], in_=self.sm.rearrange("b h -> h b"), allow_slow_non_contiguous=True), writes=['m0'])
        rk = ('rows', 0)
        s.op('dve', lambda e: e.tensor_copy(nb3[:, :, 0:1], nlf3[:, :, 0:1]), reads=['nlf'], writes=[rk])
        for i in range(1, 4):
            s.op('dve', lambda e, i=i: e.tensor_tensor(out=nb3[:, :, i:i + 1], in0=nb3[:, :, i - 1:i], in1=nlf3[:, :, i:i + 1], op=ALU.add), reads=['nlf', rk], writes=[rk])
        s.op('dve', lambda e: e.tensor_tensor(out=u, in0=ig, in1=nb, op=ALU.add), reads=['ig', rk], writes=[rk])
        s.op('dve', lambda e: e.tensor_tensor(out=c3[:, :, 0:1], in0=u3[:, :, 0:1], in1=m0[:, :].unsqueeze(2), op=ALU.max), reads=[rk, 'm0'], writes=[rk])
        for i in range(1, 4):
            s.op('dve', lambda e, i=i: e.tensor_tensor(out=c3[:, :, i:i + 1], in0=c3[:, :, i - 1:i], in1=u3[:, :, i:i + 1], op=ALU.max), reads=[rk], writes=[rk])
        s.op('dve', lambda e: e.tensor_tensor(out=wint3, in0=bcast(m0[:, :].unsqueeze(2), [4, 16, 4]), in1=c3, op=ALU.subtract), reads=[rk, 'm0'], writes=[rk])
        s.op('act', lambda e: e.activation(out=wint, in_=wint, func=AF.Exp), reads=[rk], writes=[rk])
        s.op('dve', lambda e: e.tensor_tensor(out=emt, in0=nb, in1=c, op=ALU.subtract), reads=[rk], writes=[rk])
        s.op('act', lambda e: e.activation(out=emt, in_=emt, func=AF.Exp), reads=[rk], writes=[rk])
        s.op('dve', lambda e: e.tensor_tensor(out=ws3, in0=u3, in1=bcast(c3[:, :, 3:4], [4, 16, 4]), op=ALU.subtract), reads=[rk], writes=[rk])
        s.op('act', lambda e: e.activation(out=wsr, in_=wsr, func=AF.Exp), reads=[rk], writes=[rk])
        s.op('dve', lambda e: e.tensor_tensor(out=mnw[:, :].unsqueeze(2), in0=c3[:, :, 3:4], in1=nb3[:, :, 3:4], op=ALU.subtract), reads=[rk], writes=['mnw'])
        s.dma('sp', 'mout', lambda e: e.dma_start(out=self.msn.rearrange("b h -> h b"), in_=mnw[:, :], allow_slow_non_contiguous=True), reads=['mnw'])
        self.ml_cols_cbc(L, rk, self.cf('posmask_s'))
        sel = self.cf('sel').rearrange("p (h m) -> p h m", m=128)
        wcs_s = T("wcs_s", [128, 4, 16], F32)
        for h in range(4):
            s.op('pe', lambda e, h=h: e.matmul(self.banks[1][:, 32 + 16 * h:48 + 16 * h], sel[0:4, h, :], wint3[:, :, 3], start=True, stop=True), reads=[rk, 'cft'], writes=[self.bk[1]])
        s.op('act', lambda e: e.activation(out=wcs_s[:, :, :], in_=self.banks[1][:, 32:96].rearrange("p (h b) -> p h b", b=16), func=AF.Copy), reads=[self.bk[1]], writes=['wcs_s'])
        ident = self.cf('ident')
        n0 = T("n0", [64, 128], F32)
        n0T = T("n0T", [128, 64], F32)
        nnT = T("nnT", [128, 64], F32)
        nno = T("nno", [64, 128], F32)
        s.dma('sp', 'n0in', lambda e: e.dma_start(out=n0[:, :], in_=self.sn), writes=['n0'])
        s.op('pe', lambda e: e.transpose(self.banks[7][:, 0:64], n0[:, :], ident[0:64, 0:64]), reads=['n0', 'cft'], writes=[self.bk[7]])
        s.op('act', lambda e: e.activation(out=n0T[:, :], in_=self.banks[7][:, 0:64], func=AF.Copy), reads=[self.bk[7]], writes=['n0T'])
        C0 = [T("C0_%d" % i, [128, 16, 129], F32) for i in range(2)]
        C0b = T("C0b", [128, 16, 129], BF16)
        qpad = T("qpad", [128, 16, 64], BF16)
        kspad = T("kspad", [64, 16, 128], BF16)
        bmask = self.cbm('bmask').rearrange("p (b n) -> p b n", n=64)
        rmask = self.cbm('rmask')
        sCv = self.sC.rearrange("(b h k) v -> h k b v", h=4, k=128)
        Csv = self.Cs.rearrange("(b h k) v -> h k b v", h=4, k=128)
        for h in range(4):
            Ch = C0[h % 2]
            ck_ = ('C0', h % 2)
            s.dma('sp', 'c0in%d' % (h % 2), lambda e, h=h, Ch=Ch: e.dma_start(out=Ch[:, :, 0:128], in_=sCv[h]), writes=[ck_])
            s.op('pool', lambda e, h=h, Ch=Ch: e.tensor_copy(Ch[:, :, 128], n0T[:, :].rearrange("k (b h) -> k h b", h=4)[:, h, :]), reads=['n0T'], writes=[ck_])
            s.op('act', lambda e, Ch=Ch: e.activation(out=C0b[:, :, :], in_=Ch[:, :, :], func=AF.Copy), reads=[ck_], writes=['C0b'])
            self.ml_head_av(h, L, ci, cs_)
            s.op('dve', lambda e, h=h: e.tensor_tensor(out=qpad[:, :, :], in0=bcast(self.mqT[:, h, 0:64].unsqueeze(1), [128, 16, 64]), in1=bmask, op=ALU.mult),
                 reads=['mqT', 'cbt'], writes=['qpad'])
            for b in range(NSB):
                s.op('pe', lambda e, b=b: e.matmul(self.banks[5][0:L, 0:129], qpad[:, b, :], C0b[:, b, :], start=(b == 0), stop=(b == NSB - 1)),
                     reads=['qpad', 'C0b'], writes=[self.bk[5]])
            self.ml_head_out(h, L, ci, cs_)
            s.op('dve', lambda e, h=h: e.tensor_scalar(out=self.ks[0:L, :], in0=self.ktm[0:L, ci, h * 128:(h + 1) * 128], scalar1=self.cols[0:L, 12 + h:13 + h], scalar2=None, op0=ALU.mult),
                 reads=['ktm', 'cols'], writes=['ks'])
            s.op('dve', lambda e: e.tensor_tensor(out=kspad[:, :, :], in0=bcast(self.ks[0:64, :].unsqueeze(1), [64, 16, 128]), in1=bcast(rmask[0:64, :].unsqueeze(2), [64, 16, 128]), op=ALU.mult),
                 reads=['ks', 'cbt'], writes=['kspad'])
            for b in range(NSB):
                ub = 6 + (b % 2)
                s.op('pe', lambda e, b=b, ub=ub, h=h: e.matmul(self.banks[ub][:, 0:129], kspad[0:64, b, :], self.mva[0:64, ci, h, :], start=True, stop=True),
                     reads=['kspad', 'mva'], writes=[self.bk[ub]])
                s.op('dve', lambda e, b=b, ub=ub, h=h, Ch=Ch: e.scalar_tensor_tensor(out=Ch[:, b, :], in0=Ch[:, b, :], scalar=wcs_s[:, h, b:b + 1], in1=self.banks[ub][:, 0:129], op0=ALU.mult, op1=ALU.add),
                     reads=[self.bk[ub], 'wcs_s', 'C0b'], writes=[ck_])
            s.dma('sp', 'c0out%d' % (h % 2), lambda e, h=h, Ch=Ch: e.dma_start(out=Csv[h], in_=Ch[:, :, 0:128]), reads=[ck_])
            s.op('pool', lambda e, h=h, Ch=Ch: e.tensor_copy(nnT[:, :].rearrange("k (b h) -> k h b", h=4)[:, h, :], Ch[:, :, 128]), reads=[ck_], writes=['nnT'])
        s.op('pe', lambda e: e.transpose(self.banks[7][0:64, 0:128], nnT[:, :], ident[:, :]), reads=['nnT', 'cft'], writes=[self.bk[7]])
        s.op('act', lambda e: e.activation(out=nno[:, :], in_=self.banks[7][0:64, 0:128], func=AF.Copy), reads=[self.bk[7]], writes=['nno'])
        s.dma('sp', 'nout', lambda e: e.dma_start(out=self.nsn, in_=nno[:, :]), reads=['nno'])

    def ml_rows(self, ci):
        s = self.s
        L = 128
        R = self.rows
        cs_ = slice(ci * 128, ci * 128 + L)
        CR = self.crow2
        pb_ = ci % 2
        ig, nlf = R[0:4, 0, cs_], R[0:4, 1, cs_]
        nb, u, c, wint, emt, wsr = [CR[0:4, pb_, i, :] for i in range(6)]
        mprev, ncl = self.mrow[0:4, 0:1], self.mrow[0:4, 1:2]
        ones4 = self.cf('onesrow')[0:4, 0:L]
        rk = ('rows2', pb_)
        s.op('dve', lambda e: e.tensor_tensor_scan(nb, ones4, nlf, 0.0, ALU.mult, ALU.add), reads=['nlf', 'cft'], writes=[rk])
        s.op('dve', lambda e: e.tensor_tensor(out=u, in0=ig, in1=nb, op=ALU.add), reads=['ig', rk], writes=[rk])
        s.op('dve', lambda e: e.tensor_tensor_scan(c, u, u, mprev, ALU.max, ALU.max), reads=[rk, 'mprev'], writes=[rk])
        s.op('act', lambda e: e.activation(out=wint, in_=c, func=AF.Exp, scale=-1.0, bias=mprev), reads=[rk, 'mprev'], writes=[rk])
        s.op('dve', lambda e: e.tensor_tensor(out=emt, in0=nb, in1=c, op=ALU.subtract), reads=[rk], writes=[rk])
        s.op('act', lambda e: e.activation(out=emt, in_=emt, func=AF.Exp), reads=[rk], writes=[rk])
        s.op('dve', lambda e: e.tensor_scalar(out=ncl, in0=CR[0:4, pb_, 2, L - 1:L], scalar1=-1.0, scalar2=None, op0=ALU.mult), reads=[rk], writes=['ncl'])
        s.op('act', lambda e: e.activation(out=wsr, in_=u, func=AF.Exp, bias=ncl), reads=[rk, 'ncl'], writes=[rk])
        s.op('dve', lambda e: e.tensor_tensor(out=mprev, in0=CR[0:4, pb_, 2, L - 1:L], in1=CR[0:4, pb_, 0, L - 1:L], op=ALU.subtract), reads=[rk], writes=['mprev'])
        sel = self.cf('sel').rearrange("p (h m) -> p h m", m=128)
        s.op('dve', lambda e: e.tensor_tensor(out=self.U4[:, pb_, :, :], in0=bcast(u.unsqueeze(1), [4, 4, 128]), in1=sel[0:4, :, :], op=ALU.mult), reads=[rk, 'cft'], writes=[('U4', pb_)])

    def ml_front(self, ci):
        s = self.s
        L = 128
        pb_ = ci % 2
        CR = self.crow2
        nb, u, c, wint, emt, wsr = [CR[0:4, pb_, i, :] for i in range(6)]
        rk = ('rows2', pb_)
        cs_ = slice(ci * 128, ci * 128 + L)
        ident = self.cf('ident')
        identb = self.cbm('identb')
        posmb = self.cbm('posmb')
        negones = self.cf('negones')
        sel = self.cf('sel').rearrange("p (h m) -> p h m", m=128)
        A, B, C_, W = self.banks[7], self.banks[0], self.banks[1], self.banks[6]
        for i, src in enumerate([wint, emt, wsr]):
            s.op('pe', lambda e, i=i, src=src: e.transpose(A[:, 4 * i:4 * i + 4], src, ident[0:4, 0:4]), reads=[rk, 'cft'], writes=[self.bk[7]])
        for h in range(4):
            s.op('pe', lambda e, h=h: e.matmul(A[:, 12 + h:13 + h], sel[0:4, h, :], CR[0:4, pb_, 3, L - 1:L], start=True, stop=True), reads=[rk, 'cft'], writes=[self.bk[7]])
        s.op('act', lambda e: e.activation(out=self.cols2[:, pb_, 0:16], in_=A[:, 0:16], func=AF.Copy), reads=[self.bk[7]], writes=[('cols2', pb_)])
        for h in range(4):
            o = B[:, h * 128:(h + 1) * 128]
            s.op('pe', lambda e, h=h, o=o: e.matmul(o, sel[0:4, h, :], c, start=True, stop=False), reads=[rk, 'cft'], writes=[self.bk[0]])
            s.op('pe', lambda e, h=h, o=o: e.matmul(o, self.U4[:, pb_, h, :], negones[0:4, :], start=False, stop=False), reads=[('U4', pb_), 'cft'], writes=[self.bk[0]])
            s.op('pe', lambda e, h=h, o=o: e.matmul(o, identb, posmb, start=False, stop=True), reads=['cbt'], writes=[self.bk[0]])
        s.op('act', lambda e: e.activation(out=self.Ew4[:, :], in_=B[:, :], func=AF.Exp, scale=-1.0), reads=[self.bk[0]], writes=[('wk', 0)])
        for h in range(4):
            s.op('pe', lambda e, h=h: e.matmul(C_[:, h * 128:(h + 1) * 128], self.mkT[:, h, cs_], self.mqT[:, h, cs_], start=True, stop=True), reads=['mkT', 'mqT'], writes=[self.bk[1]])
        s.op('dve', lambda e: e.tensor_tensor(out=self.aT4[:, :, :], in0=C_[:, :].rearrange("p (h n) -> p h n", n=128), in1=self.Ew4[:, :].rearrange("p (h n) -> p h n", n=128), op=ALU.mult),
             reads=[self.bk[1], ('wk', 0)], writes=['aT4'])
        for h in range(4):
            s.op('pe', lambda e, h=h: e.matmul(W[:, h * 128:(h + 1) * 128], sel[0:4, h, :], wint, start=True, stop=True), reads=[rk, 'cft'], writes=[self.bk[6]])
        s.op('dve', lambda e: e.tensor_tensor(out=self.qs4[:, :, :], in0=self.mqT[:, :, cs_], in1=W[:, :].rearrange("p (h n) -> p h n", n=128), op=ALU.mult),
             reads=[self.bk[6], 'mqT'], writes=['qs4'])

    def ml_back_a(self, ci):
        s = self.s
        L = 128
        cs_ = slice(ci * 128, ci * 128 + L)
        ident = self.cf('ident')
        A, D_, E_, F_ = self.banks[7], self.banks[2], self.banks[3], self.banks[4]
        mlg = self.pbv('mlg')
        c2 = self.cols2[:, ci % 2, :]
        ck2 = ('cols2', ci % 2)
        ck2b = ('cols2b', ci % 2)
        for h in range(4):
            s.op('pe', lambda e, h=h: e.matmul(D_[:, h * 128:(h + 1) * 128], self.aT4[:, h, :], self.mva[:, ci, h, 0:128], start=True, stop=False), reads=['aT4', 'mva'], writes=[self.bk[2]])
            s.op('pe', lambda e, h=h: e.matmul(D_[:, h * 128:(h + 1) * 128], self.qs4[:, h, :], self.Cbf[:, h, 0:128], start=False, stop=True), reads=['qs4', 'Cbf'], writes=[self.bk[2]])
            s.op('pe', lambda e, h=h: e.matmul(A[:, 16 + h:17 + h], self.aT4[:, h, :], self.mva[:, ci, h, 128:129], start=True, stop=False), reads=['aT4', 'mva'], writes=[self.bk[7]])
            s.op('pe', lambda e, h=h: e.matmul(A[:, 16 + h:17 + h], self.qs4[:, h, :], self.Cbf[:, h, 128:129], start=False, stop=True), reads=['qs4', 'Cbf'], writes=[self.bk[7]])
        s.op('dve', lambda e: e.tensor_tensor(out=self.ks4[:, :, :], in0=self.ktm[:, ci, :].rearrange("p (h n) -> p h n", n=128), in1=bcast(c2[:, 8:12].unsqueeze(2), [128, 4, 128]), op=ALU.mult),
             reads=['ktm', ck2], writes=['ks4'])
        for h in range(4):
            s.op('pe', lambda e, h=h: e.matmul(E_[:, h * 128:(h + 1) * 128], self.ks4[:, h, :], self.mva[:, ci, h, 0:128], start=True, stop=True), reads=['ks4', 'mva'], writes=[self.bk[3]])
            s.op('pe', lambda e, h=h: e.matmul(A[:, 24 + h:25 + h], self.ks4[:, h, :], self.mva[:, ci, h, 128:129], start=True, stop=True), reads=['ks4', 'mva'], writes=[self.bk[7]])
        s.op('act', lambda e: e.activation(out=c2[:, 16:28], in_=A[:, 16:28], func=AF.Copy), reads=[self.bk[7]], writes=[ck2b])

    def ml_back_a2(self, ci):
        s = self.s
        L = 128
        cs_ = slice(ci * 128, ci * 128 + L)
        A, D_, E_, F_ = self.banks[7], self.banks[2], self.banks[3], self.banks[4]
        mlg = self.pbv('mlg')
        c2 = self.cols2[:, ci % 2, :]
        ck2 = ('cols2', ci % 2)
        ck2b = ('cols2b', ci % 2)
        nw = self.nw4
        sc = self.sc4
        s.op('act', lambda e: e.activation(out=nw[:, :], in_=D_[:, :], func=AF.Copy), reads=[self.bk[2]], writes=[('wk', 1)])
        s.op('dve', lambda e: e.scalar_tensor_tensor(out=sc[:, 0:4], in0=c2[:, 16:20], scalar=-1.0, in1=c2[:, 16:20], op0=ALU.mult, op1=ALU.max), reads=[ck2b], writes=['sc4'])
        s.op('dve', lambda e: e.tensor_tensor(out=sc[:, 0:4], in0=sc[:, 0:4], in1=c2[:, 4:8], op=ALU.max), reads=['sc4', ck2], writes=['sc4'])
        s.op('dve', lambda e: e.reciprocal(sc[:, 4:8], sc[:, 0:4]), reads=['sc4'], writes=['sc4'])
        v3 = lambda ap: ap.rearrange("p (h n) -> p h n", n=128)
        s.op('dve', lambda e: e.tensor_tensor(out=v3(self.t14[:, :]), in0=v3(nw[:, :]), in1=bcast(sc[:, 4:8].unsqueeze(2), [128, 4, 128]), op=ALU.mult), reads=[('wk', 1), 'sc4'], writes=[('wk', 2)])
        s.op('act', lambda e: e.activation(out=self.sq4[:, :], in_=self.t14[:, :], func=AF.Square), reads=[('wk', 2)], writes=[('wk', 4)])
        s.op('pool', lambda e: e.tensor_tensor(out=self.t24[:, :], in0=self.t14[:, :], in1=mlg, op=ALU.mult), reads=[('wk', 2), 'pbt'], writes=[('wk', 3)])
        so_ap = self.xs_[ci // 2][:, (ci % 2) * 512:(ci % 2 + 1) * 512]
        s.op('pool', lambda e: e.tensor_tensor(out=self.t24[:, :], in0=self.t24[:, :], in1=so_ap, op=ALU.mult), reads=[('wk', 3), 'so'], writes=[('wk', 3)])
        s.op('dve', lambda e: e.tensor_reduce(out=sc[:, 8:12], in_=v3(self.sq4[:, :]), axis=AX.X, op=ALU.add), reads=[('wk', 4)], writes=['sc4'])
        self.rstd(sc[:, 8:12], sc[:, 12:16], 1.0 / 128, ['sc4'], ['sc4b'])
        s.op('dve', lambda e: e.tensor_tensor(out=v3(self.t14[:, :]), in0=v3(self.t24[:, :]), in1=bcast(sc[:, 12:16].unsqueeze(2), [128, 4, 128]), op=ALU.mult), reads=[('wk', 3), 'sc4b'], writes=[('wk', 2)])

    def ml_back_b(self, ci):
        s = self.s
        L = 128
        cs_ = slice(ci * 128, ci * 128 + L)
        ident = self.cf('ident')
        A, D_, E_, F_ = self.banks[7], self.banks[2], self.banks[3], self.banks[4]
        c2 = self.cols2[:, ci % 2, :]
        ck2 = ('cols2', ci % 2)
        ck2b = ('cols2b', ci % 2)
        v3 = lambda ap: ap.rearrange("p (h n) -> p h n", n=128)
        for h in range(4):
            s.op('pe', lambda e, h=h: e.transpose(F_[:, h * 128:(h + 1) * 128], self.t14[:, h * 128:(h + 1) * 128], ident), reads=[('wk', 2), 'cft'], writes=[self.bk[4]])
        s.op('act', lambda e: e.activation(out=self.hmT[:, :, cs_], in_=v3(F_[:, :]), func=AF.Copy), reads=[self.bk[4]], writes=['hmT'])

    def ml_back_c(self, ci):
        s = self.s
        L = 128
        cs_ = slice(ci * 128, ci * 128 + L)
        ident = self.cf('ident')
        A, D_, E_, F_ = self.banks[7], self.banks[2], self.banks[3], self.banks[4]
        c2 = self.cols2[:, ci % 2, :]
        ck2 = ('cols2', ci % 2)
        ck2b = ('cols2b', ci % 2)
        v3 = lambda ap: ap.rearrange("p (h n) -> p h n", n=128)
        C3 = self.Caug[:, :, 0:128]
        wcb = bcast(c2[:, 12:16].unsqueeze(2), [128, 4, 128])
        s.op('dve', lambda e: e.tensor_tensor(out=C3, in0=C3, in1=wcb, op=ALU.mult), reads=['Caug', ck2], writes=['Caug'])
        s.op('dve', lambda e: e.tensor_tensor(out=C3, in0=C3, in1=v3(E_[:, :]), op=ALU.add), reads=['Caug', self.bk[3]], writes=['Caug'])
        n3 = self.Caug[:, :, 128]
        s.op('dve', lambda e: e.tensor_tensor(out=n3, in0=n3, in1=c2[:, 12:16], op=ALU.mult), reads=['Caug', ck2], writes=['Caug'])
        s.op('dve', lambda e: e.tensor_tensor(out=n3, in0=n3, in1=c2[:, 24:28], op=ALU.add), reads=['Caug', ck2b], writes=['Caug'])
        s.op('act', lambda e: e.activation(out=self.Cbf[:, :, :], in_=self.Caug[:, :, :], func=AF.Copy), reads=['Caug'], writes=['Cbf'])

    def mlstm_span_fast(self):
        self.ml_rows(0)
        self.ml_rows(1)
        self.ml_front(0)
        for ci in range(4):
            self.ml_back_a(ci)
            self.ml_back_c(ci)
            self.ml_back_a2(ci)
            if ci + 1 < 4:
                self.ml_front(ci + 1)
            self.ml_back_b(ci)
            if ci + 2 < 4:
                self.ml_rows(ci + 2)

    def w_out_phase(self, tiles):
        s = self.s
        for i in range(4):
            slot, sk = self.wslot()
            for a in range(2):
                h = 2 * i + a
                for ti, (t, tsz) in enumerate(tiles):
                    for hf in range(2):
                        b = ti * 2 + hf
                        s.op('pe', lambda e, slot=slot, a=a, h=h, t=t, tsz=tsz, hf=hf, b=b: e.matmul(
                            self.banks[b][0:tsz, :], self.attT[0:64, h, t * 128:t * 128 + tsz], slot[0:64, a * 1024 + hf * 512:a * 1024 + (hf + 1) * 512],
                            start=(h == 0), stop=False), reads=[sk, 'attT'], writes=[self.bk[b]])
        for i in range(2):
            slot, sk = self.wslot()
            for a in range(2):
                h = 2 * i + a
                for ti, (t, tsz) in enumerate(tiles):
                    for hf in range(2):
                        b = ti * 2 + hf
                        s.op('pe', lambda e, slot=slot, a=a, h=h, t=t, tsz=tsz, hf=hf, b=b: e.matmul(
                            self.banks[b][0:tsz, :], self.hmT[:, h, t * 128:t * 128 + tsz], slot[:, a * 1024 + hf * 512:a * 1024 + (hf + 1) * 512],
                            start=False, stop=(h == 3)), reads=[sk, 'hmT'], writes=[self.bk[b]])
        for ti, (t, tsz) in enumerate(tiles):
            for hf in range(2):
                b = ti * 2 + hf
                dst = self.x1[0:tsz, t, hf * 512:(hf + 1) * 512]
                s.op('dve', lambda e, dst=dst, b=b, tsz=tsz: e.tensor_tensor(out=dst, in0=self.banks[b][0:tsz, :], in1=dst, op=ALU.add),
                     reads=[self.bk[b]], writes=[('x1', t)])

    def prompt_span(self, q, n):
        s = self.s
        tiles = [(t, 128) for t in range(4)]
        ntok = 512
        row0 = q * S + n * 512
        x1keys = [('x1', t) for t in range(4)]
        if self.xpre == (q, n):
            self.x_in_wk = True
        else:
            self.x_in_wk = False
            s.dma('sp', 'xin', lambda e: e.dma_start(out=self.x1[:, :, :], in_=self.xp[row0:row0 + 512, :].rearrange("(t p) d -> p t d", p=128)), writes=x1keys)
        s.dma('sp', 'csin', lambda e: e.dma_start(out=self.cs[:, 0, :, :], in_=self.cosp[n * 512:(n + 1) * 512, :].rearrange("(t p) d -> p t d", p=128)), writes=['cs'])
        s.dma('sp', 'csin', lambda e: e.dma_start(out=self.cs[:, 1, :, :], in_=self.sinp[n * 512:(n + 1) * 512, :].rearrange("(t p) d -> p t d", p=128)), writes=['cs'])
        self.build_csg(4, 128)
        self.mark('start')
        self.norm_to_hT(tiles, 'g1', None)
        self.ffn(tiles, ntok)
        self.x_in_wk = False
        self.mark('ffn1')
        if 'x1a' in self.dbg and q == 0 and n == 0:
            s.dma('sp', 'dbg', lambda e: e.dma_start(out=self.dbg_out['x1a'].rearrange("(t p) d -> p t d", p=128), in_=self.x1[:, :, :]), reads=x1keys)
        if n == 0:
            s.op('pool', lambda e: e.memset(self.Caug[:], 0.0), writes=['Caug'])
            s.op('pool', lambda e: e.memset(self.Cbf[:], 0.0), writes=['Cbf'])
            s.op('pool', lambda e: e.memset(self.mrow[:], 0.0), writes=['mprev', 'ncl'])
        kout_fn = lambda t, tsz: self.kp[row0 + t * 128:row0 + t * 128 + tsz, :]
        vout_fn = lambda t, tsz: self.vp[row0 + t * 128:row0 + t * 128 + tsz, :]
        self.w_in_phase(tiles, ntok, q, n, kout_fn, vout_fn, n * 512)
        self.mark('w_in')
        self.attention(n)
        self.mark('attn')
        self.mlstm_span_fast()
        self.mark('mlstm')
        if n == 3:
            for h in range(4):
                r0 = (q * 4 + h) * 128
                s.dma('sp', 'cout', lambda e, h=h, r0=r0: e.dma_start(out=self.Cp[r0:r0 + 128, :], in_=self.Caug[:, h, 0:128]), reads=['Caug'])
            for h in range(4):
                s.dma('sp', 'cout', lambda e, h=h: e.dma_start(out=self.np_[q * 4 + h:q * 4 + h + 1, :].rearrange("a k -> k a"), in_=self.Caug[:, h, 128:129]), reads=['Caug'])
            s.dma('sp', 'cout', lambda e: e.dma_start(out=self.mp[q:q + 1, :].rearrange("a h -> h a"), in_=self.mrow[0:4, 0:1]), reads=['mprev'])
        self.w_out_phase(tiles)
        self.mark('w_out')
        nxt = (q, n + 1) if n + 1 < self.n_spans else ((q + 1, 0) if q + 1 < self.n_seq else None)
        if nxt is not None:
            nrow = nxt[0] * S + nxt[1] * 512
            s.dma('sp', 'xpre', lambda e: e.dma_start(out=self.wkt[:, 0:8, :].rearrange("p (t a) n -> p t (a n)", a=2), in_=self.xp[nrow:nrow + 512, :].rearrange("(t p) d -> p t d", p=128)),
                  writes=[('wk', i) for i in range(8)])
            self.xpre = nxt
        self.norm_to_hT(tiles, 'g2', None)
        self.ffn(tiles, ntok)
        self.mark('ffn2')
        s.dma('sp', 'yout', lambda e: e.dma_start(out=self.yp[row0:row0 + 512, :].rearrange("(t p) d -> p t d", p=128), in_=self.x1[:, :, :]), reads=x1keys)


def mask_view(PT, nk, c0):
    return PT[0:nk, c0:512]


_NC_CACHE = {}


def make_in_maps(inp, n_seq=NSEQ, do_sample=True, cores=NCORES):
    cf, cb = make_consts()
    cosp, sinp = rope_tables(np.arange(S))
    coss, sins = rope_tables(PAST + (np.arange(64) % 4))
    wall = pack_weights(inp).reshape(NSLOT * 128, 2048)
    pb = pack_params(inp)
    maps = []
    for c in range(cores):
        m = {"xp": np.ascontiguousarray(inp['x_prompt'][NSEQ * c:NSEQ * c + n_seq].reshape(n_seq * S, D)),
             "wall": wall, "pb": pb, "cf": cf, "cb": cb, "cosp": cosp, "sinp": sinp}
        if do_sample:
            b0, b1 = NSB * c, NSB * (c + 1)
            m.update({
                "xs": np.ascontiguousarray(inp['x_sample'][b0:b1].reshape(64, D)),
                "ck": np.ascontiguousarray(inp['cache_k_win'][0, b0:b1].reshape(NSB * WIN_BUF, 512)),
                "cv": np.ascontiguousarray(inp['cache_v_win'][0, b0:b1].reshape(NSB * WIN_BUF, 512)),
                "sC": np.ascontiguousarray(inp['state_C'][0, b0:b1].reshape(NSB * 4 * 128, 128)),
                "sn": np.ascontiguousarray(inp['state_n'][0, b0:b1].reshape(NSB * 4, 128)),
                "sm": np.ascontiguousarray(inp['state_m'][0, b0:b1].reshape(NSB, 4)),
                "coss": coss, "sins": sins,
            })
        maps.append(m)
    return maps


def kernel(**inputs):
    inp = {k: np.asarray(v) for k, v in inputs.items()}
    if 'nc' not in _NC_CACHE:
        _NC_CACHE['nc'] = MK().build()
    nc = _NC_CACHE['nc']
    maps = make_in_maps(inp)
    res = run_bass_kernel_spmd(nc, maps, core_ids=list(range(NCORES)))
    R = res.results
    f = lambda name: [np.asarray(r[name], dtype=np.float32) for r in R]
    yp = np.concatenate([a.reshape(NSEQ, S, D) for a in f('yp')], 0)
    ys = np.concatenate([a.reshape(NSB, 4, D) for a in f('ys')], 0)
    kp = np.concatenate([a.reshape(NSEQ, S, 8, 64) for a in f('kp')], 0)[None]
    vp = np.concatenate([a.reshape(NSEQ, S, 8, 64) for a in f('vp')], 0)[None]
    ks = np.concatenate([a.reshape(NSB, 4, 8, 64) for a in f('ksn')], 0)[None]
    vs = np.concatenate([a.reshape(NSB, 4, 8, 64) for a in f('vsn')], 0)[None]
    Cp = np.concatenate([a.reshape(NSEQ, 4, 128, 128) for a in f('Cp')], 0)[None]
    npp = np.concatenate([a.reshape(NSEQ, 4, 128) for a in f('np')], 0)[None]
    mp = np.concatenate([a.reshape(NSEQ, 4) for a in f('mp')], 0)[None]
    Cs = np.concatenate([a.reshape(NSB, 4, 128, 128) for a in f('Cs')], 0)[None]
    nss = np.concatenate([a.reshape(NSB, 4, 128) for a in f('nsn')], 0)[None]
    ms = np.concatenate([a.reshape(NSB, 4) for a in f('msn')], 0)[None]
    return (yp, ys, kp, vp, ks, vs, Cp, npp, mp, Cs, nss, ms)
```
